# Optimizing a Trainium2 kernel written in Bass

```python
import math
import jax
import jax.numpy as jnp
from jax import lax
import numpy as np

D_MODEL = 4096
BATCH = 2
SEQ = 8192
DEPTH = 2

PLE_DIM = 256
HEAD_DIM = 128
D_ATTN = D_MODEL // 2
N_ATTN_HEADS = D_ATTN // HEAD_DIM
D_CONV = D_MODEL // 2
CONV_WIDTH = 31
MOBA_BLOCK = 256
MOBA_TOP_K = 3
QUERY_CHUNK = 32
LN_EPS = 1e-5
DEEPNORM_ALPHA = (2.0 * DEPTH) ** 0.25
DEEPNORM_BETA = (8.0 * DEPTH) ** -0.25

IN_WIDTHS = (D_ATTN, D_ATTN, D_ATTN, D_ATTN, D_CONV, D_CONV, D_CONV, D_MODEL, D_MODEL)
IN_BETA_SCALED = (False, False, True, False, True, False, False, False, False)
D_IN = sum(IN_WIDTHS)
SPLIT_POINTS = tuple(int(s) for s in np.cumsum(IN_WIDTHS)[:-1])

kernel_name = "hybrid_moba_conformer_gated_deepnorm"


def layer_norm(x, g, b):
    xf = x.astype(jnp.float32)
    mu = jnp.mean(xf, axis=-1, keepdims=True)
    var = jnp.mean(jnp.square(xf - mu), axis=-1, keepdims=True)
    return ((xf - mu) * lax.rsqrt(var + LN_EPS) * g.astype(jnp.float32)
            + b.astype(jnp.float32)).astype(x.dtype)


def moba_attention(q, k, v):
    B, T, H, Dh = q.shape
    n_blk = -(-T // MOBA_BLOCK)
    t_pad = n_blk * MOBA_BLOCK
    top_k = min(MOBA_TOP_K, n_blk)
    pad = ((0, 0), (0, t_pad - T), (0, 0), (0, 0))
    q, k, v = (jnp.pad(t, pad).transpose(0, 2, 1, 3) for t in (q, k, v))
    kb = k.reshape(B, H, n_blk, MOBA_BLOCK, Dh)
    vb = v.reshape(B, H, n_blk, MOBA_BLOCK, Dh)
    k_mean = jnp.mean(kb.astype(jnp.float32), axis=3).astype(k.dtype)
    n_chunks = t_pad // QUERY_CHUNK
    chunks_per_block = MOBA_BLOCK // QUERY_CHUNK
    q_chunks = q.reshape(B, H, n_chunks, QUERY_CHUNK, Dh).transpose(2, 0, 1, 3, 4)
    scale = Dh ** -0.5
    bi = jnp.arange(B)[:, None, None, None]
    hi = jnp.arange(H)[None, :, None, None]
    blk_ids = jnp.arange(n_blk)
    in_blk = jnp.arange(MOBA_BLOCK)

    def chunk_attend(args):
        c, q_c = args
        own = c // chunks_per_block
        q_pos = c * QUERY_CHUNK + jnp.arange(QUERY_CHUNK)
        gate = jnp.einsum('bhqd,bhnd->bhqn', q_c, k_mean).astype(jnp.float32)
        gate = jnp.where((blk_ids < own)[None, None, None, :], gate, -jnp.inf)
        _, sel = lax.top_k(gate, top_k)
        sel_valid = sel < own
        k_sel = kb[bi, hi, sel]
        v_sel = vb[bi, hi, sel]
        s_sel = jnp.einsum('bhqd,bhqnkd->bhqnk', q_c, k_sel).astype(jnp.float32) * scale
        s_sel = jnp.where(sel_valid[..., None], s_sel, -jnp.inf)
        s_sel = s_sel.reshape(B, H, QUERY_CHUNK, top_k * MOBA_BLOCK)
        k_own = lax.dynamic_index_in_dim(kb, own, axis=2, keepdims=False)
        v_own = lax.dynamic_index_in_dim(vb, own, axis=2, keepdims=False)
        s_own = jnp.einsum('bhqd,bhkd->bhqk', q_c, k_own).astype(jnp.float32) * scale
        k_pos = own * MOBA_BLOCK + in_blk
        s_own = jnp.where((k_pos[None, :] <= q_pos[:, None])[None, None], s_own, -jnp.inf)
        probs = jax.nn.softmax(jnp.concatenate([s_sel, s_own], axis=-1), axis=-1).astype(v.dtype)
        p_sel = probs[..., :top_k * MOBA_BLOCK].reshape(B, H, QUERY_CHUNK, top_k, MOBA_BLOCK)
        p_own = probs[..., top_k * MOBA_BLOCK:]
        return (jnp.einsum('bhqnk,bhqnkd->bhqd', p_sel, v_sel)
                + jnp.einsum('bhqk,bhkd->bhqd', p_own, v_own))

    out = lax.map(chunk_attend, (jnp.arange(n_chunks), q_chunks))
    out = out.transpose(1, 0, 3, 2, 4).reshape(B, t_pad, H * Dh)
    return out[:, :T]


def causal_depthwise_conv(u, w, b):
    out = lax.conv_general_dilated(
        u, w[:, None, :], window_strides=(1,), padding=[(CONV_WIDTH - 1, 0)],
        dimension_numbers=('NWC', 'WIO', 'NWC'), feature_group_count=u.shape[-1])
    return out + b


def hybrid_layer(x, p_i, w_in, b_in, w_conv, b_conv, conv_ln_g, conv_ln_b,
                 w_o_attn, w_o_conv, w_out, w_ple_up, w_ple_gate, b_ple_gate, ln_g, ln_b):
    B, T, _ = x.shape
    h = jnp.einsum('btd,de->bte', x, w_in) + b_in
    q, k, v, z_a, c_val, c_gate, z_c, g_a, g_c = jnp.split(h, SPLIT_POINTS, axis=-1)
    shp = (B, T, N_ATTN_HEADS, HEAD_DIM)
    attn = moba_attention(q.reshape(shp), k.reshape(shp), v.reshape(shp))
    o_a = jnp.einsum('btc,cd->btd', attn * jax.nn.silu(z_a), w_o_attn)
    u = c_val * jax.nn.sigmoid(c_gate)
    u = causal_depthwise_conv(u, w_conv, b_conv)
    u = jax.nn.silu(layer_norm(u, conv_ln_g, conv_ln_b))
    o_c = jnp.einsum('btc,cd->btd', u * jax.nn.silu(z_c), w_o_conv)
    merged = jax.nn.sigmoid(g_a) * o_a + jax.nn.sigmoid(g_c) * o_c
    y = DEEPNORM_ALPHA * x + jnp.einsum('btd,de->bte', merged, w_out)
    ple = (jax.nn.sigmoid(jnp.einsum('btd,de->bte', y, w_ple_gate) + b_ple_gate)
           * jnp.einsum('btp,pd->btd', p_i, w_ple_up))
    return layer_norm(y + ple, ln_g, ln_b)


def setup_inputs(seed: int = 0) -> dict:
    key = jax.random.key(seed)
    ks = jax.random.split(key, 16)
    f32 = jnp.float32
    nrm = lambda k, shape, s: jax.random.normal(k, shape, f32) * s
    x = nrm(ks[0], (BATCH, SEQ, D_MODEL), 1.0)
    p = nrm(ks[1], (DEPTH, BATCH, SEQ, PLE_DIM), 1.0)
    col_scale = jnp.concatenate([jnp.full((w,), DEEPNORM_BETA if s else 1.0, f32)
                                 for w, s in zip(IN_WIDTHS, IN_BETA_SCALED)])
    w_in = nrm(ks[2], (DEPTH, D_MODEL, D_IN), D_MODEL ** -0.5) * col_scale
    b_in = nrm(ks[3], (DEPTH, D_IN), 0.01)
    w_conv = nrm(ks[4], (DEPTH, CONV_WIDTH, D_CONV), CONV_WIDTH ** -0.5)
    b_conv = nrm(ks[5], (DEPTH, D_CONV), 0.01)
    conv_ln_g = 1.0 + nrm(ks[6], (DEPTH, D_CONV), 0.01)
    conv_ln_b = nrm(ks[7], (DEPTH, D_CONV), 0.01)
    w_o_attn = nrm(ks[8], (DEPTH, D_ATTN, D_MODEL), DEEPNORM_BETA * D_ATTN ** -0.5)
    w_o_conv = nrm(ks[9], (DEPTH, D_CONV, D_MODEL), DEEPNORM_BETA * D_CONV ** -0.5)
    w_out = nrm(ks[10], (DEPTH, D_MODEL, D_MODEL), DEEPNORM_BETA * D_MODEL ** -0.5)
    w_ple_up = nrm(ks[11], (DEPTH, PLE_DIM, D_MODEL), PLE_DIM ** -0.5)
    w_ple_gate = nrm(ks[12], (DEPTH, D_MODEL, D_MODEL), D_MODEL ** -0.5)
    b_ple_gate = nrm(ks[13], (DEPTH, D_MODEL), 0.01)
    ln_g = 1.0 + nrm(ks[14], (DEPTH, D_MODEL), 0.01)
    ln_b = nrm(ks[15], (DEPTH, D_MODEL), 0.01)
    return {"x": x, "p": p, "w_in": w_in, "b_in": b_in, "w_conv": w_conv, "b_conv": b_conv,
            "conv_ln_g": conv_ln_g, "conv_ln_b": conv_ln_b, "w_o_attn": w_o_attn,
            "w_o_conv": w_o_conv, "w_out": w_out, "w_ple_up": w_ple_up,
            "w_ple_gate": w_ple_gate, "b_ple_gate": b_ple_gate, "ln_g": ln_g, "ln_b": ln_b}


def reference(x, p, w_in, b_in, w_conv, b_conv, conv_ln_g, conv_ln_b, w_o_attn, w_o_conv,
              w_out, w_ple_up, w_ple_gate, b_ple_gate, ln_g, ln_b):
    for i in range(DEPTH):
        x = hybrid_layer(x, p[i], w_in[i], b_in[i], w_conv[i], b_conv[i], conv_ln_g[i],
                         conv_ln_b[i], w_o_attn[i], w_o_conv[i], w_out[i], w_ple_up[i],
                         w_ple_gate[i], b_ple_gate[i], ln_g[i], ln_b[i])
    return x
```

```python
import contextlib
import numpy as np
import ml_dtypes
import concourse.bass as bass
import concourse.mybir as mybir
from concourse.bass_utils import run_bass_kernel_spmd

F32 = mybir.dt.float32
BF16 = mybir.dt.bfloat16
AF = mybir.ActivationFunctionType
ALU = mybir.AluOpType
AX = mybir.AxisListType

NCORES = 8
GROUP = 4
D = 4096
DA = 2048
NH = 16
DH = 128
PLE = 256
CW = 31
BLK = 256
DIN = 22528
DEPTH = 2
LN_EPS = 1e-5
ALPHA = (2.0 * DEPTH) ** 0.25
SCALE = DH ** -0.5
BIG = 30000.0
NDS = 40


class Res:
    __slots__ = ("name", "lw", "rd")

    def __init__(self, name):
        self.name = name
        self.lw = None
        self.rd = []


class Op:
    __slots__ = ("eng", "fn", "deps", "dma", "sem", "ticket", "inc", "grp")

    def __init__(self, eng, fn, deps, dma, grp=None):
        self.grp = grp
        self.eng = eng
        self.fn = fn
        self.deps = deps
        self.dma = dma
        self.sem = None
        self.ticket = 0
        self.inc = 0


class Sched:
    CE = ("pe", "act", "dve", "pool")

    def __init__(self, nc, esems, dsems, ccsem):
        self.nc = nc
        self.e = {"pe": nc.tensor, "act": nc.scalar, "dve": nc.vector, "pool": nc.gpsimd,
                  "sp": nc.sync}
        self.esem = esems
        self.dsem = dsems
        self.ccsem = ccsem
        self.ops = []
        self.emitted = 0
        self.barrier_idx = 0
        self.ecnt = {k: 0 for k in self.CE}
        self.dcnt = [0] * len(dsems)
        self.cccnt = 0
        self.dmap = {}
        self.waited = {}
        self.n_inst = 0

    def dkey(self, name):
        if name not in self.dmap:
            idx = len(self.dmap)
            assert idx < len(self.dsem), "out of dma semaphores"
            self.dmap[name] = idx
        return self.dmap[name]

    def add(self, eng, fn, reads=(), writes=(), dma=None, extra_deps=(), grp=None):
        i = len(self.ops)
        deps = set(extra_deps)
        for r in reads:
            if r.lw is not None:
                deps.add(r.lw)
        for w in writes:
            if w.lw is not None:
                deps.add(w.lw)
            deps.update(w.rd)
        for r in reads:
            r.rd.append(i)
        for w in writes:
            w.lw = i
            w.rd = []
        bi = self.barrier_idx
        deps = {d for d in deps if d >= bi}
        if dma is not None and dma != "cc":
            dma = self.dkey(dma)
        self.ops.append(Op(eng, fn, deps, dma, grp))
        return i

    def barrier(self):
        last = {}
        for i in range(self.barrier_idx, len(self.ops)):
            op = self.ops[i]
            if op.fn is None:
                continue
            key = op.eng if op.dma is None else ("d", op.dma)
            last[key] = i
        deps = set(last.values())
        for eng in ("pe", "act", "dve", "pool", "sp"):
            self.ops.append(Op(eng, None, set(deps), None))
        self.emit()
        self.barrier_idx = len(self.ops)
        self.dmap = {}

    def emit(self):
        ops = self.ops
        start = self.emitted
        needed = set()
        for i in range(start, len(ops)):
            needed |= ops[i].deps
        for i in range(start, len(ops)):
            op = ops[i]
            if op.fn is None:
                continue
            if op.dma == "cc":
                self.cccnt += 1
                op.sem, op.ticket, op.inc = self.ccsem, self.cccnt, 1
            elif op.dma is not None:
                self.dcnt[op.dma] += 16
                op.sem, op.ticket, op.inc = self.dsem[op.dma], self.dcnt[op.dma], 16
            elif i in needed:
                self.ecnt[op.eng] += 1
                op.sem, op.ticket, op.inc = self.esem[op.eng], self.ecnt[op.eng], 1
        gmax = {}
        for i in range(start, len(ops)):
            op = ops[i]
            if op.grp is not None and op.sem is not None:
                gmax[op.grp] = max(gmax.get(op.grp, 0), op.ticket)
        for i in range(start, len(ops)):
            op = ops[i]
            if op.grp is not None and op.sem is not None:
                op.ticket = gmax[op.grp]
        for i in range(start, len(ops)):
            op = ops[i]
            E = self.e[op.eng]
            w = {}
            for d in op.deps:
                dop = ops[d]
                if dop.sem is None:
                    continue
                if op.eng == "pe" and dop.eng == "pe" and dop.dma is None and op.dma is None:
                    continue
                k = id(dop.sem)
                if k not in w or w[k][1] < dop.ticket:
                    w[k] = (dop.sem, dop.ticket)
            for k, (sem, val) in w.items():
                wk = (op.eng, k)
                if self.waited.get(wk, 0) >= val:
                    continue
                E.wait_ge(sem, val)
                self.waited[wk] = val
                self.n_inst += 1
            if op.fn is not None:
                ins = op.fn(E)
                self.n_inst += 1
                if op.sem is not None:
                    ins.then_inc(op.sem, op.inc)
            op.fn = None if op.fn is None else 0
        self.emitted = len(ops)


class Ring:
    def __init__(self, aps, name):
        self.aps = aps
        self.res = [Res(f"{name}{i}") for i in range(len(aps))]
        self.i = 0
        self.name = name

    def next(self):
        k = self.i % len(self.aps)
        self.i += 1
        return self.aps[k], self.res[k], k


class Cfg:
    def __init__(self, tok):
        self.TOK = tok
        self.T = tok * GROUP
        self.NB = self.T // BLK
        self.NBL = tok // BLK
        self.NT = min(1024, tok)
        self.NQ = tok // 128
        self.NQT = tok // 512
        self.NPC = max(1, (2 * DA * tok * 2) // (1 << 20))
        self.RP = 2 * DA // self.NPC


def gemm_phase(S, nc, es, cfg, name, streams, ncols, evac, tok0, stage, NT, banks, wbf, side=None, cast_eng=("dve", "act")):
    ntt = NT // 512
    ncg = ncols // 512
    ns = len(streams)
    pieces = []
    for cg in range(ncg):
        for si, (_, _, W, KC) in enumerate(streams):
            for k0 in range(0, KC, 4):
                pieces.append((cg, si, k0, min(4, KC - k0)))
    per_cg = len(pieces) // ncg
    state = {"dma": 0, "cast": 0}

    def emit_dma(p):
        cg, si, k0, nk = pieces[p]
        W = streams[si][2]
        st_ap, st_res, slot = stage.aps[p % len(stage.aps)], stage.res[p % len(stage.aps)], p % len(stage.aps)
        src = W[k0 * 128:(k0 + nk) * 128, cg * 512:(cg + 1) * 512].rearrange("(kc p) c -> p kc c", p=128)
        S.add("sp", lambda E, o=st_ap[:, 0:nk, :], i=src: E.dma_start(out=o, in_=i),
              writes=[st_res], dma=f"stage{slot}")

    def emit_cast(p):
        cg, si, k0, nk = pieces[p]
        st_ap, st_res = stage.aps[p % len(stage.aps)], stage.res[p % len(stage.aps)]
        wt, wres = wbf[si]
        eng = cast_eng[p % 2]
        if eng == "act":
            S.add(eng, lambda E, o=wt[:, cg % 2, k0:k0 + nk, :], i=st_ap[:, 0:nk, :]: E.copy(out=o, in_=i),
                  reads=[st_res], writes=[wres[cg % 2]])
        else:
            S.add(eng, lambda E, o=wt[:, cg % 2, k0:k0 + nk, :], i=st_ap[:, 0:nk, :]: E.tensor_copy(out=o, in_=i),
                  reads=[st_res], writes=[wres[cg % 2]])

    def advance(cast_to):
        cast_to = min(cast_to, len(pieces))
        while state["cast"] < cast_to:
            while state["dma"] < min(state["cast"] + len(stage.aps), len(pieces)):
                emit_dma(state["dma"])
                state["dma"] += 1
            emit_cast(state["cast"])
            state["cast"] += 1
        while state["dma"] < min(state["cast"] + len(stage.aps) - 1, len(pieces)):
            emit_dma(state["dma"])
            state["dma"] += 1

    advance(per_cg)
    nsteps = ntt * 4
    for cg in range(ncg):
        step = 0
        for tt in range(ntt):
            for sub in range(4):
                bl = []
                for si, (act, ares, W, KC) in enumerate(streams):
                    bank, bres, _ = banks.next()
                    wt, wres = wbf[si]

                    def mm(E, bank=bank, wt=wt, act=act, KC=KC, cg=cg, sub=sub, tt=tt):
                        ins = None
                        for kc in range(KC):
                            ins = E.matmul(bank[:, :], lhsT=wt[:, cg % 2, kc, sub * 128:(sub + 1) * 128],
                                           rhs=act[:, kc, tt * 512:(tt + 1) * 512],
                                           start=(kc == 0), stop=(kc == KC - 1))
                        return ins
                    S.add("pe", mm, reads=[wres[cg % 2]] + list(ares), writes=[bres])
                    bl.append((bank, bres))
                evac(bl, cg * 512 + sub * 128, tok0 + tt * 512)
                if side:
                    side.pop(0)()
                step += 1
                advance(min((cg + 2) * per_cg, (cg + 1) * per_cg + ((step + 2) * per_cg + nsteps - 1) // nsteps))


S_UNIQ = []


def alloc_wbf(es, nc, name, kcs):
    out = []
    for si, KC in enumerate(kcs):
        t = es.enter_context(nc.sbuf_tensor(f"{name}_wbf{si}_L{len(S_UNIQ)}", [128, 2, KC, 512], BF16))
        S_UNIQ.append(0)
        out.append((t, [Res(f"{name}_wbf{si}_0"), Res(f"{name}_wbf{si}_1")]))
    return out


def load_act(S, cfg, act, ares, src, KC, tokoff, stage, is_f32, NT):
    if not is_f32:
        g = ("act", id(act), tokoff, S.barrier_idx, len(S.ops))
        n = 0
        for k0 in range(0, KC, 8):
            nk = min(8, KC - k0)
            s = src[k0 * 128:(k0 + nk) * 128, tokoff:tokoff + NT].rearrange("(kc p) t -> p kc t", p=128)
            S.add("sp", lambda E, o=act[:, k0:k0 + nk, :], i=s: E.dma_start(out=o, in_=i),
                  writes=(list(ares) if n == 0 else []), dma=f"act{id(act) % 97}", grp=g)
            n += 1
        return
    per = max(1, 2048 // NT)
    n = 0
    for k0 in range(0, KC, per):
        nk = min(per, KC - k0)
        st_ap, st_res, slot = stage.next()
        sv = st_ap[:, :, :].rearrange("p a b -> p (a b)")[:, 0:nk * NT].rearrange("p (k t) -> p k t", t=NT)
        s = src[k0 * 128:(k0 + nk) * 128, tokoff:tokoff + NT].rearrange("(kc p) t -> p kc t", p=128)
        S.add("sp", lambda E, o=sv, i=s: E.dma_start(out=o, in_=i), writes=[st_res], dma=f"stage{slot}")
        S.add(["dve", "pool"][n % 2], lambda E, o=act[:, k0:k0 + nk, :], i=sv: E.tensor_copy(out=o, in_=i),
              reads=[st_res], writes=[ares[n % len(ares)]])
        n += 1


P1_SEGS = [
    (0, 2048, "qT", 0, "Identity", BF16),
    (2048, 4096, "kvT", 0, "Identity", BF16),
    (4096, 6144, "kvT", 2048, "Identity", BF16),
    (6144, 8192, "zaT", 0, "Silu", BF16),
    (8192, 10240, "valT", 0, "Identity", F32),
    (10240, 12288, "sgT", 0, "Sigmoid", F32),
    (12288, 14336, "zcT", 0, "Silu", BF16),
    (14336, 18432, "gaT", 0, "Sigmoid", BF16),
    (18432, 22528, "gcT", 0, "Sigmoid", BF16),
]
V_BIN = 0
V_WCONV = DIN // 128
V_BCONV = V_WCONV + 16 * CW
V_CLNG = V_BCONV + 16
V_CLNB = V_CLNG + 16
V_BPG = V_CLNB + 16
V_LNG = V_BPG + 32
V_LNB = V_LNG + 32
NV = V_LNB + 32


_UNIQ = [0]


def uniq(name):
    _UNIQ[0] += 1
    return f"{name}_u{_UNIQ[0]}"


def sb(es, nc, name, shape, dtype):
    return es.enter_context(nc.sbuf_tensor(uniq(name), list(shape), dtype))


def ps(es, nc, name, shape, dtype):
    return es.enter_context(nc.psum_tensor(uniq(name), list(shape), dtype))


def dma(S, out, in_, key, reads=(), writes=(), grp=None, q="sp"):
    return S.add(q, lambda E, o=out, i=in_: E.dma_start(out=o, in_=i), reads=reads, writes=writes,
                 dma=key, grp=grp)


def phase_p1(S, nc, cfg, dt, L, x_src):
    TOK, NT = cfg.TOK, cfg.NT
    with contextlib.ExitStack() as es:
        vec = sb(es, nc, "p1_vec", [128, DIN // 128], F32)
        vres = Res("vec")
        dma(S, vec[:, :], dt["vecs"][L, :, 0:DIN // 128], "vec", writes=[vres])
        stage = Ring([sb(es, nc, f"p1_st{i}", [128, 4, 512], F32) for i in range(4)], "st")
        act = sb(es, nc, "p1_act", [128, 32, NT], BF16)
        ares = [Res(f"p1_act{i}") for i in range(8)]
        wbf = alloc_wbf(es, nc, "p1", [32])
        obf = Ring([sb(es, nc, f"p1_obf{i}", [128, 512], BF16) for i in range(4)], "obf")
        o32 = Ring([sb(es, nc, f"p1_o32{i}", [128, 512], F32) for i in range(4)], "o32")
        banks = Ring([ps(es, nc, f"p1_ps{i}", [128, 512], F32) for i in range(8)], "ps")

        def evac_p1(bl, col0, tokoff):
            bank, bres = bl[0]
            seg = [s for s in P1_SEGS if s[0] <= col0 < s[1]][0]
            row0 = seg[3] + col0 - seg[0]
            if seg[2] == "kvT":
                dst = dt["kvT"][row0 // cfg.RP]
                row0 = row0 % cfg.RP
            else:
                dst = dt[seg[2]]
            ring = obf if seg[5] == BF16 else o32
            o_ap, o_res, slot = ring.next()
            g = col0 // 128
            func = getattr(AF, seg[4])
            S.add("act", lambda E, o=o_ap[:, :], i=bank[:, :], f=func, b=vec[:, g:g + 1]:
                  E.activation(out=o, in_=i, func=f, bias=b, scale=1.0),
                  reads=[bres, vres], writes=[o_res])
            dma(S, dst[row0:row0 + 128, tokoff:tokoff + 512], o_ap[:, :], f"{ring.name}{slot}", reads=[o_res], q="act")

        for nt in range(TOK // NT):
            load_act(S, cfg, act, ares, x_src, 32, nt * NT, stage, True, NT)
            gemm_phase(S, nc, es, cfg, f"p1_{nt}", [(act, ares, dt["w_in"][L], 32)], DIN,
                       evac_p1, nt * NT, stage, NT, banks, wbf, None, ("dve", "pool"))
        S.barrier()


def phase_tail(S, nc, cfg, dt):
    TOK = cfg.TOK
    with contextlib.ExitStack() as es:
        a = sb(es, nc, "tl_a", [128, 16, 32], F32)
        b = sb(es, nc, "tl_b", [128, 16, 32], F32)
        ra, rb = Res("tl_a"), Res("tl_b")
        if True:
            dma(S, a[:, :, :], dt["valT"][:, TOK - 32:TOK].rearrange("(c p) t -> p c t", p=128), "tla", writes=[ra])
            dma(S, b[:, :, :], dt["sgT"][:, TOK - 32:TOK].rearrange("(c p) t -> p c t", p=128), "tlb", writes=[rb])
            S.add("dve", lambda E: E.tensor_tensor(out=a[:, :, :], in0=a[:, :, :], in1=b[:, :, :], op=ALU.mult),
                  reads=[ra, rb], writes=[ra])
            dma(S, dt["tail"].rearrange("(c p) t -> p c t", p=128), a[:, :, :], "tlo", reads=[ra])
            S.barrier()


def phase_exchange(S, nc, cfg, dt):
    groups = [list(range(g * GROUP, (g + 1) * GROUP)) for g in range(NCORES // GROUP)]
    r_tail = Res("tail_all")
    S.add("pool", lambda E: E.collective_compute("AllGather", ALU.bypass, replica_groups=groups,
                                                 ins=[dt["tail"].opt()], outs=[dt["tail_all"].opt()]),
          dma="cc", writes=[r_tail])
    for i in range(cfg.NPC):
        S.add("pool", lambda E, i=i: E.collective_compute("AllGather", ALU.bypass, replica_groups=groups,
                                                          ins=[dt["kvT"][i].opt()], outs=[dt["kvT_all"][i].opt()]),
              dma="cc")
    return r_tail


def phase_attn(S, nc, cfg, dt):
    TOK, T, NB, NQ, NQT = cfg.TOK, cfg.T, cfg.NB, cfg.NQ, cfg.NQT
    NKC = T // 128
    NKO = TOK // 128
    with contextlib.ExitStack() as es:
        NCB = 256 + NB * 128 + 512
        cst = sb(es, nc, "at_cst", [128, NCB], BF16)
        pm = sb(es, nc, "at_pm", [128, NQ * NB], F32)
        rc = Res("cst")
        dma(S, cst[:, :], dt["cst_bf"][:, :], "cst", writes=[rc])
        dma(S, pm[:, :], dt["pm"][:, :], "pm", writes=[rc])
        ident = cst[:, 0:128]
        ones = cst[:, 128:256]
        sel = cst[:, 256:256 + NB * 128].rearrange("p (n m) -> p n m", m=128)
        tri = cst[:, 256 + NB * 128:256 + NB * 128 + 512].rearrange("p (c q) -> p c q", q=256)

        KT = [sb(es, nc, f"at_KT{i}", [128, T], BF16) for i in range(2)]
        VT = [sb(es, nc, f"at_VT{i}", [128, T], BF16) for i in range(2)]
        QT = [sb(es, nc, f"at_QT{i}", [128, TOK], BF16) for i in range(2)]
        KTo = [sb(es, nc, f"at_KTo{i}", [128, TOK], BF16) for i in range(2)]
        VTo = [sb(es, nc, f"at_VTo{i}", [128, TOK], BF16) for i in range(2)]
        Vsb = [sb(es, nc, f"at_V{i}", [128, NKC, 129], BF16) for i in range(2)]
        Vo = [sb(es, nc, f"at_Vo{i}", [128, NKO, 129], BF16) for i in range(2)]
        mbT = [sb(es, nc, f"at_mbT{i}", [128, TOK], BF16) for i in range(2)]
        kmb = [sb(es, nc, f"at_kmb{i}", [128, NB], BF16) for i in range(2)]
        r_ld = [Res(f"ld{i}") for i in range(2)]
        r_V = [Res(f"V{i}") for i in range(2)]
        r_Vo = [Res(f"Vo{i}") for i in range(2)]
        r_mbT = [Res(f"mbT{i}") for i in range(2)]
        r_kmb = [Res(f"kmb{i}") for i in range(2)]
        km32 = sb(es, nc, "at_km32", [128, NB], F32)
        r_km32 = Res("km32")
        gm = sb(es, nc, "at_gm", [128, NQ * NB], F32)
        r_gm = Res("gm")
        top8 = sb(es, nc, "at_top8", [128, NQ, 8], F32)
        r_top8 = Res("top8")
        thr = sb(es, nc, "at_thr", [128, NQ], F32)
        r_thr = Res("thr")
        mbb = sb(es, nc, "at_mbb", [128, NQ, 128], BF16)
        r_mbb = Res("mbb")
        S.add("pool", lambda E: E.memset(mbb[:, :, :], 0.0), writes=[r_mbb])
        for i in range(2):
            S.add("pool", lambda E, i=i: E.memset(Vsb[i][:, :, 128:129], 1.0), writes=[r_V[i]])
            S.add("pool", lambda E, i=i: E.memset(Vo[i][:, :, 128:129], 1.0), writes=[r_Vo[i]])
        identf = sb(es, nc, "at_identf", [128, 128], F32)
        r_idf = Res("identf")
        S.add("dve", lambda E: E.tensor_copy(out=identf[:, :], in_=ident), reads=[rc], writes=[r_idf])
        rdn = sb(es, nc, "at_rdn", [128, 4], F32)
        r_rdn = Res("rdn")
        attm = sb(es, nc, "at_attm", [128, 4, 128], F32)
        r_attm = Res("attm")
        Pt = Ring([sb(es, nc, f"at_Pt{i}", [128, 512], BF16) for i in range(5)], "Pt")
        zat = Ring([sb(es, nc, f"at_za{i}", [128, 512], BF16) for i in range(2)], "za")
        rd = sb(es, nc, "at_rd", [128, 512], F32)
        r_rd = Res("rd")
        at32 = sb(es, nc, "at_at32", [128, 512], F32)
        r_at32 = Res("at32")
        agb = Ring([sb(es, nc, f"at_agb{i}", [128, 512], BF16) for i in range(2)], "agb")
        Sb = Ring([ps(es, nc, f"at_S{i}", [128, 512], F32) for i in range(3)], "S")
        O2 = [ps(es, nc, "at_O2a", [128, 512], F32), ps(es, nc, "at_O2b", [128, 512], F32)]
        r_O2 = [Res("O2a"), Res("O2b")]
        gps = ps(es, nc, "at_gps", [128, 512], F32)
        r_gps = Res("gps")
        mps = ps(es, nc, "at_mps", [128, 1024], BF16)
        r_mps = Res("mps")
        vtp = Ring([ps(es, nc, f"at_vtp{i}", [128, 1024], BF16) for i in range(1)], "vtp")

        def stageA(h):
            s = h % 2
            RP = cfg.RP

            def kv_all(row):
                return dt["kvT_all"][row // RP].rearrange("(r f) t -> f r t", r=GROUP)[row % RP:row % RP + 128, :, :]

            def kv_own(row):
                return dt["kvT"][row // RP][row % RP:row % RP + 128, :]
            g = ("ld", h)
            dma(S, KT[s][:, :].rearrange("p (r t) -> p r t", r=GROUP), kv_all(h * 128),
                f"ld{s}", writes=[r_ld[s]], grp=g)
            dma(S, VT[s][:, :].rearrange("p (r t) -> p r t", r=GROUP), kv_all(DA + h * 128), f"ld{s}", grp=g)
            dma(S, QT[s][:, :], dt["qT"][h * 128:(h + 1) * 128, :], f"ld{s}", grp=g)
            dma(S, KTo[s][:, :], kv_own(h * 128), f"ld{s}", grp=g)
            dma(S, VTo[s][:, :], kv_own(DA + h * 128), f"ld{s}", grp=g)

        def transposes(src, n, dst, rdst, rsrc):
            k = 0
            for c0 in range(0, n, 8):
                nn = min(8, n - c0)
                vp, rvp, _ = vtp.next()

                def tr(E, vp=vp, c0=c0, nn=nn):
                    ins = None
                    for c in range(nn):
                        ins = E.transpose(out=vp[:, c * 128:(c + 1) * 128],
                                          in_=src[:, (c0 + c) * 128:(c0 + c + 1) * 128], identity=ident)
                    return ins
                S.add("pe", tr, reads=[rsrc, rc], writes=[rvp])
                eng = "act" if k % 2 == 0 else "dve"
                if eng == "act":
                    S.add("act", lambda E, o=dst[:, c0:c0 + nn, 0:128], i=vp[:, 0:nn * 128].rearrange("p (c d) -> p c d", d=128):
                          E.copy(out=o, in_=i), reads=[rvp], writes=[rdst])
                else:
                    S.add("dve", lambda E, o=dst[:, c0:c0 + nn, 0:128], i=vp[:, 0:nn * 128].rearrange("p (c d) -> p c d", d=128):
                          E.tensor_copy(out=o, in_=i), reads=[rvp], writes=[rdst])
                k += 1

        def stageB(h):
            s = h % 2
            transposes(VT[s], NKC, Vsb[s], r_V[s], r_ld[s])
            transposes(VTo[s], NKO, Vo[s], r_Vo[s], r_ld[s])
            S.add("dve", lambda E: E.tensor_reduce(out=km32[:, :], in_=KT[s][:, :].rearrange("p (n k) -> p n k", k=BLK),
                                                   axis=AX.X, op=ALU.add), reads=[r_ld[s]], writes=[r_km32])
            S.add("dve", lambda E: E.tensor_scalar(out=kmb[s][:, :], in0=km32[:, :], scalar1=1.0 / BLK, scalar2=None,
                                                   op0=ALU.mult), reads=[r_km32], writes=[r_kmb[s]])

        def stageC(h):
            s = h % 2

            def gmm(E):
                ins = None
                for j in range(NQ):
                    ins = E.matmul(gps[:, j * NB:(j + 1) * NB], lhsT=QT[s][:, j * 128:(j + 1) * 128],
                                   rhs=kmb[s][:, :], start=True, stop=True)
                return ins
            S.add("pe", gmm, reads=[r_ld[s], r_kmb[s]], writes=[r_gps])
            S.add("dve", lambda E: E.tensor_tensor(out=gm[:, :], in0=gps[:, 0:NQ * NB], in1=pm[:, :], op=ALU.add),
                  reads=[r_gps, rc], writes=[r_gm])
            for j in range(NQ):
                S.add("dve", lambda E, j=j: E.max(out=top8[:, j, :], in_=gm[:, j * NB:(j + 1) * NB]),
                      reads=[r_gm], writes=[r_top8])
            S.add("dve", lambda E: E.tensor_scalar(out=thr[:, :], in0=top8[:, :, 2], scalar1=-1e29, scalar2=None,
                                                   op0=ALU.max), reads=[r_top8], writes=[r_thr])
            for j in range(NQ):
                S.add("dve", lambda E, j=j: E.tensor_scalar(out=mbb[:, j, 0:NB], in0=gm[:, j * NB:(j + 1) * NB],
                                                            scalar1=thr[:, j:j + 1], scalar2=1.0,
                                                            op0=ALU.is_ge, op1=ALU.subtract),
                      reads=[r_gm, r_thr], writes=[r_mbb])

        def stageD(h):
            s = h % 2
            for j0 in range(0, NQ, 8):
                nn = min(8, NQ - j0)

                def tr(E, j0=j0, nn=nn):
                    ins = None
                    for j in range(nn):
                        ins = E.transpose(out=mps[:, j * 128:(j + 1) * 128], in_=mbb[:, j0 + j, :], identity=ident)
                    return ins
                S.add("pe", tr, reads=[r_mbb, rc], writes=[r_mps])
                S.add("act", lambda E, j0=j0, nn=nn: E.copy(out=mbT[s][:, j0 * 128:(j0 + nn) * 128], in_=mps[:, 0:nn * 128]),
                      reads=[r_mps], writes=[r_mbT[s]])

        def main(h, hooks):
            s = h % 2
            for t in range(NQT):
                za_ap, za_res, zslot = zat.next()
                dma(S, za_ap[:, :], dt["zaT"][h * 128:(h + 1) * 128, t * 512:(t + 1) * 512], f"za{zslot}", writes=[za_res])
                q512 = QT[s][:, t * 512:(t + 1) * 512]
                pendq = []
                chunks = []
                for n in range(NB):
                    for c in range(2):
                        chunks.append(("past", n, c))
                for half in range(2):
                    for c in range(2):
                        chunks.append(("own", half, c))
                nch = len(chunks)
                first = [True]

                def pv(kind, a, c, p_ap, p_res, last):
                    st = first[0]
                    first[0] = False
                    if kind == "past":
                        v1 = Vsb[s][:, a * 2 + c, :]
                        rv = r_V[s]

                        def f(E):
                            ins = None
                            for qs in range(4):
                                ins = E.matmul(O2[qs // 2][:, (qs % 2) * 129:(qs % 2) * 129 + 129],
                                               lhsT=p_ap[:, qs * 128:(qs + 1) * 128], rhs=v1,
                                               start=(st and qs % 2 == 0), stop=last)
                            return ins
                        S.add("pe", f, reads=[rv, p_res], writes=[r_O2[0], r_O2[1]])
                    else:
                        lb = 2 * t + a
                        v1 = Vo[s][:, lb * 2 + c, :]
                        rv = r_Vo[s]

                        def f(E):
                            ins = None
                            for qq in range(2):
                                ins = E.matmul(O2[a][:, qq * 129:qq * 129 + 129],
                                               lhsT=p_ap[:, qq * 128:(qq + 1) * 128], rhs=v1, start=False, stop=last)
                            return ins
                        S.add("pe", f, reads=[rv, p_res], writes=[r_O2[a]])

                for ci, (kind, a, c) in enumerate(chunks):
                    s_ap, s_res, _ = Sb.next()
                    p_ap, p_res, _ = Pt.next()
                    if kind == "past":
                        def f(E, s_ap=s_ap, a=a, c=c, q512=q512, t=t):
                            E.matmul(s_ap[:, :], lhsT=KT[s][:, a * 256 + c * 128:a * 256 + (c + 1) * 128], rhs=q512,
                                     start=True, stop=False)
                            return E.matmul(s_ap[:, :], lhsT=sel[:, a, :], rhs=mbT[s][:, t * 512:(t + 1) * 512],
                                            start=False, stop=True)
                        S.add("pe", f, reads=[r_ld[s], r_mbT[s], rc], writes=[s_res])
                        S.add("act", lambda E, o=p_ap[:, :], i=s_ap[:, :]: E.activation(out=o, in_=i, func=AF.Exp, scale=SCALE),
                              reads=[s_res], writes=[p_res])
                    else:
                        lb = 2 * t + a

                        def f(E, s_ap=s_ap, a=a, c=c, lb=lb, t=t):
                            E.matmul(s_ap[:, 0:256], lhsT=KTo[s][:, lb * 256 + c * 128:lb * 256 + (c + 1) * 128],
                                     rhs=QT[s][:, t * 512 + a * 256:t * 512 + (a + 1) * 256], start=True, stop=False)
                            return E.matmul(s_ap[:, 0:256], lhsT=ident, rhs=tri[:, c, :], start=False, stop=True)
                        S.add("pe", f, reads=[r_ld[s], rc], writes=[s_res])
                        S.add("act", lambda E, o=p_ap[:, 0:256], i=s_ap[:, 0:256]: E.activation(out=o, in_=i, func=AF.Exp, scale=SCALE),
                              reads=[s_res], writes=[p_res])
                    pendq.append((kind, a, c, p_ap, p_res))
                    if len(pendq) > 2:
                        pv(*pendq.pop(0), False)
                while pendq:
                    item = pendq.pop(0)
                    pv(*item, len(pendq) == 0)
                for qs in range(4):
                    bk, off = O2[qs // 2], (qs % 2) * 129
                    S.add("dve", lambda E, bk=bk, off=off, qs=qs: E.reciprocal(out=rdn[:, qs:qs + 1], in_=bk[:, off + 128:off + 129]),
                          reads=[r_O2[qs // 2]], writes=[r_rdn])
                    S.add("dve", lambda E, bk=bk, off=off, qs=qs: E.tensor_scalar(out=attm[:, qs, :], in0=bk[:, off:off + 128], scalar1=rdn[:, qs:qs + 1], scalar2=None, op0=ALU.mult),
                          reads=[r_O2[qs // 2], r_rdn], writes=[r_attm])
                tp_ap, tp_res, _ = Sb.next()

                def trb(E, tp_ap=tp_ap):
                    ins = None
                    for qs in range(4):
                        ins = E.transpose(out=tp_ap[:, qs * 128:(qs + 1) * 128], in_=attm[:, qs, :], identity=identf[:, :])
                    return ins
                S.add("pe", trb, reads=[r_attm, r_idf], writes=[tp_res])
                g_ap, g_res, gslot = agb.next()
                S.add("dve", lambda E, o=g_ap[:, :], z=za_ap[:, :], tp=tp_ap: E.tensor_tensor(out=o, in0=tp[:, :], in1=z, op=ALU.mult),
                      reads=[tp_res, za_res], writes=[g_res])
                dma(S, dt["agT"][h * 128:(h + 1) * 128, t * 512:(t + 1) * 512], g_ap[:, :], f"agb{gslot}", reads=[g_res], q="act")
                if hooks:
                    hooks.pop(0)()
            while hooks:
                hooks.pop(0)()

        for hh in range(min(1, NH)):
            stageA(0), stageB(0), stageC(0), stageD(0)
        for h in range(NH):
            hooks = []
            if h + 1 < NH:
                hooks = [lambda h=h: stageA(h + 1), lambda h=h: stageB(h + 1), lambda h=h: stageC(h + 1),
                         lambda h=h: stageD(h + 1)]
                if NQT >= 4:
                    hooks[0]()
                    hooks = hooks[1:]
            main(h, hooks)
        S.barrier()


def phase_conv(S, nc, cfg, dt, L, r_tail=None):
    TOK = cfg.TOK
    HT = min(1024, TOK)
    NH2 = TOK // HT
    NTT = HT // 512
    with contextlib.ExitStack() as es:
        vec = sb(es, nc, "cv_vec", [128, NV - V_WCONV], F32)
        rvec = Res("cv_vec")
        dma(S, vec[:, :], dt["vecs"][L, :, V_WCONV:NV], "vec", writes=[rvec])
        bc = lambda ch: vec[:, V_BCONV - V_WCONV + ch:V_BCONV - V_WCONV + ch + 1]
        lg = lambda ch: vec[:, V_CLNG - V_WCONV + ch:V_CLNG - V_WCONV + ch + 1]
        lb = lambda ch: vec[:, V_CLNB - V_WCONV + ch:V_CLNB - V_WCONV + ch + 1]
        hw = sb(es, nc, "cv_hw", [128, GROUP], F32)
        rhw = Res("hw")
        dma(S, hw[:, :], dt["halo_w"][:, :], "hw", writes=[rhw])
        cst = sb(es, nc, "cv_cst", [128, 256], BF16)
        rcst = Res("cst")
        dma(S, cst[:, :], dt["cst_bf"][:, 0:256], "cst", writes=[rcst])
        ident, ones = cst[:, 0:128], cst[:, 128:256]
        vl = Ring([sb(es, nc, f"cv_vl{i}", [128, 32 + HT], F32) for i in range(2)], "vl")
        sg = Ring([sb(es, nc, f"cv_sg{i}", [128, 32 + HT], F32) for i in range(2)], "sg")
        tl = Ring([sb(es, nc, f"cv_tl{i}", [128, GROUP, 32], F32) for i in range(3)], "tl")
        hl = Ring([sb(es, nc, f"cv_hl{i}", [128, 32], F32) for i in range(2)], "hl")
        ub = Ring([sb(es, nc, f"cv_ub{i}", [128, 32 + HT], BF16) for i in range(2)], "ub")
        dg = Ring([sb(es, nc, f"cv_dg{i}", [128, CW, 128], BF16) for i in range(2)], "dg")
        dgB = [Res("dgB0"), Res("dgB1")]
        c32 = Ring([sb(es, nc, f"cv_c{i}", [128, 512], F32) for i in range(4)], "c")
        cb = Ring([sb(es, nc, f"cv_cb{i}", [128, 512], BF16) for i in range(6)], "cb")
        cq = Ring([sb(es, nc, f"cv_cq{i}", [128, 512], BF16) for i in range(6)], "cq")
        cps = Ring([ps(es, nc, f"cv_ps{i}", [128, 512], F32) for i in range(4)], "cps")
        st_sum = [ps(es, nc, f"cv_ss{i}", [128, 512], F32) for i in range(NTT)]
        st_sq = [ps(es, nc, f"cv_sq{i}", [128, 512], F32) for i in range(NTT)]
        r_st = Res("stats")
        mu = sb(es, nc, "cv_mu", [128, HT], F32)
        rs = sb(es, nc, "cv_rs", [128, HT], F32)
        r_mu, r_rs = Res("mu"), Res("rs")
        ct = Ring([sb(es, nc, f"cv_ct{i}", [128, 512], F32) for i in range(3)], "ct")
        zc = Ring([sb(es, nc, f"cv_zc{i}", [128, 512], BF16) for i in range(3)], "zc")
        og = Ring([sb(es, nc, f"cv_og{i}", [128, 512], BF16) for i in range(3)], "og")
        tail_all = dt["tail_all"].rearrange("(r c p) t -> c p r t", r=GROUP, p=128)
        for half in range(NH2):
            t0 = half * HT
            loaded = {}

            def loads(ch, half=half, t0=t0):
                v_ap, v_res, vs = vl.next()
                s_ap, s_res, ss = sg.next()
                rows = slice(ch * 128, (ch + 1) * 128)
                tinfo = None
                if half == 0:
                    dma(S, v_ap[:, 32:32 + HT], dt["valT"][rows, 0:HT], f"vl{vs}", writes=[v_res])
                    dma(S, s_ap[:, 32:32 + HT], dt["sgT"][rows, 0:HT], f"sg{ss}", writes=[s_res])
                    t_ap, t_res, ts = tl.next()
                    dma(S, t_ap[:, :, :], tail_all[ch], f"tl{ts}", reads=([r_tail] if r_tail is not None else []),
                        writes=[t_res])
                    tinfo = (t_ap, t_res)
                else:
                    dma(S, v_ap[:, :], dt["valT"][rows, t0 - 32:t0 + HT], f"vl{vs}", writes=[v_res])
                    dma(S, s_ap[:, :], dt["sgT"][rows, t0 - 32:t0 + HT], f"sg{ss}", writes=[s_res])
                loaded[ch] = (v_ap, v_res, s_ap, s_res, tinfo)

            pend = []
            loads(0)
            for ch in range(16):
                if ch + 1 < 16:
                    loads(ch + 1)
                v_ap, v_res, s_ap, s_res, tinfo = loaded.pop(ch)
                u_ap, u_res, _ = ub.next()
                d_ap, d_res, dslot = dg.next()
                rows = slice(ch * 128, (ch + 1) * 128)
                if half == 0:
                    t_ap, t_res = tinfo
                    h_ap, h_res, _ = hl.next()
                    S.add("dve", lambda E, h=h_ap, t=t_ap: E.tensor_scalar(out=h[:, :], in0=t[:, 0, :], scalar1=hw[:, 0:1], scalar2=None, op0=ALU.mult),
                          reads=[t_res, rhw], writes=[h_res])
                    for r in range(1, GROUP):
                        S.add("dve", lambda E, h=h_ap, t=t_ap, r=r: E.scalar_tensor_tensor(out=h[:, :], in0=t[:, r, :], scalar=hw[:, r:r + 1], in1=h[:, :], op0=ALU.mult, op1=ALU.add),
                              reads=[t_res, rhw, h_res], writes=[h_res])
                    S.add("dve", lambda E, h=h_ap, u=u_ap: E.tensor_copy(out=u[:, 0:32], in_=h[:, :]), reads=[h_res], writes=[u_res])
                    S.add("dve", lambda E, u=u_ap, v=v_ap, s_=s_ap: E.tensor_tensor(out=u[:, 32:32 + HT], in0=v[:, 32:32 + HT], in1=s_[:, 32:32 + HT], op=ALU.mult),
                          reads=[v_res, s_res], writes=[u_res])
                else:
                    S.add("dve", lambda E, u=u_ap, v=v_ap, s_=s_ap: E.tensor_tensor(out=u[:, :], in0=v[:, :], in1=s_[:, :], op=ALU.mult),
                          reads=[v_res, s_res], writes=[u_res])
                for j in range(CW):
                    S.add("dve", lambda E, d=d_ap, ch=ch, j=j: E.tensor_scalar(out=d[:, j, :], in0=ident, scalar1=vec[:, ch * CW + j:ch * CW + j + 1], scalar2=None, op0=ALU.mult),
                          reads=[rvec, rcst], writes=[d_res])
                newp = []
                for tt in range(NTT):
                    p_ap, p_res, _ = cps.next()

                    def taps(E, p=p_ap, d=d_ap, u=u_ap, tt=tt):
                        ins = None
                        for j in range(CW):
                            ins = E.matmul(p[:, :], lhsT=d[:, j, :], rhs=u[:, 2 + j + tt * 512:2 + j + tt * 512 + 512],
                                           start=(j == 0), stop=(j == CW - 1))
                        return ins
                    S.add("pe", taps, reads=[d_res, u_res], writes=[p_res])
                    c_ap, c_res, cs = c32.next()
                    b_ap, b_res, _ = cb.next()
                    q_ap, q_res, _ = cq.next()
                    S.add("act", lambda E, c=c_ap, p=p_ap, ch=ch: E.activation(out=c[:, :], in_=p[:, :], func=AF.Identity, bias=bc(ch), scale=1.0),
                          reads=[p_res, rvec], writes=[c_res])
                    dma(S, dt["cT"][rows, t0 + tt * 512:t0 + (tt + 1) * 512], c_ap[:, :], f"c{cs}", reads=[c_res], q="act")
                    S.add("pool", lambda E, o=b_ap, c=c_ap: E.tensor_copy(out=o[:, :], in_=c[:, :]), reads=[c_res], writes=[b_res])
                    S.add("act", lambda E, o=q_ap, c=c_ap: E.activation(out=o[:, :], in_=c[:, :], func=AF.Square), reads=[c_res], writes=[q_res])

                    def stat(E, b=b_ap, q=q_ap, ch=ch, tt=tt):
                        E.matmul(st_sum[tt][:, :], lhsT=ones, rhs=b[:, :], start=(ch == 0), stop=(ch == 15))
                        return E.matmul(st_sq[tt][:, :], lhsT=ones, rhs=q[:, :], start=(ch == 0), stop=(ch == 15))
                    newp.append(lambda stat=stat, b_res=b_res, q_res=q_res: S.add("pe", stat, reads=[b_res, q_res, rcst], writes=[r_st]))
                while pend:
                    pend.pop(0)()
                pend = newp
            while pend:
                pend.pop(0)()
            ln_stats(S, es, nc, f"cv{half}", NTT, st_sum, st_sq, r_st, mu, rs, r_mu, r_rs, 1.0 / DA)
            for ch in range(16):
                for tt in range(NTT):
                    c_ap, c_res, cs = ct.next()
                    z_ap, z_res, zs = zc.next()
                    o_ap, o_res, os_ = og.next()
                    tsl = slice(t0 + tt * 512, t0 + (tt + 1) * 512)
                    msl = slice(tt * 512, (tt + 1) * 512)
                    dma(S, c_ap[:, :], dt["cT"][ch * 128:(ch + 1) * 128, tsl], f"ct{cs}", writes=[c_res])
                    dma(S, z_ap[:, :], dt["zcT"][ch * 128:(ch + 1) * 128, tsl], f"zc{zs}", writes=[z_res])
                    S.add("dve", lambda E, c=c_ap, msl=msl: E.tensor_tensor(out=c[:, :], in0=c[:, :], in1=mu[:, msl], op=ALU.subtract),
                          reads=[c_res, r_mu], writes=[c_res])
                    S.add("dve", lambda E, c=c_ap, msl=msl: E.tensor_tensor(out=c[:, :], in0=c[:, :], in1=rs[:, msl], op=ALU.mult),
                          reads=[c_res, r_rs], writes=[c_res])
                    S.add("act", lambda E, c=c_ap, ch=ch: E.activation(out=c[:, :], in_=c[:, :], func=AF.Silu, scale=lg(ch), bias=lb(ch)),
                          reads=[c_res, rvec], writes=[c_res])
                    S.add("pool", lambda E, c=c_ap, z=z_ap, o=o_ap: E.tensor_tensor(out=o[:, :], in0=c[:, :], in1=z[:, :], op=ALU.mult),
                          reads=[c_res, z_res], writes=[o_res])
                    dma(S, dt["cgT"][ch * 128:(ch + 1) * 128, tsl], o_ap[:, :], f"og{os_}", reads=[o_res], q="act")
        S.barrier()


def ln_stats(S, es, nc, name, NTT, st_sum, st_sq, r_st, mu, rs, r_mu, r_rs, inv_n, tmp=None, r_tmp=None):
    for tt in range(NTT):
        tsl = slice(tt * 512, (tt + 1) * 512)
        S.add("act", lambda E, tt=tt, tsl=tsl: E.activation(out=mu[:, tsl], in_=st_sum[tt][:, :], func=AF.Identity, scale=inv_n),
              reads=[r_st], writes=[r_mu])
        S.add("act", lambda E, tt=tt, tsl=tsl: E.activation(out=rs[:, tsl], in_=st_sq[tt][:, :], func=AF.Identity, scale=inv_n),
              reads=[r_st], writes=[r_rs])
    if tmp is None:
        tmp = sb(es, nc, f"{name}_lntmp", [128, NTT * 512], F32)
        r_tmp = Res("lntmp")
    S.add("dve", lambda E: E.tensor_tensor(out=tmp[:, :], in0=mu[:, :], in1=mu[:, :], op=ALU.mult), reads=[r_mu], writes=[r_tmp])
    S.add("dve", lambda E: E.tensor_tensor(out=rs[:, :], in0=rs[:, :], in1=tmp[:, :], op=ALU.subtract), reads=[r_rs, r_tmp], writes=[r_rs])
    S.add("dve", lambda E: E.tensor_scalar(out=rs[:, :], in0=rs[:, :], scalar1=0.0, scalar2=LN_EPS, op0=ALU.max, op1=ALU.add),
          reads=[r_rs], writes=[r_rs])
    S.add("act", lambda E: E.activation(out=rs[:, :], in_=rs[:, :], func=AF.Sqrt), reads=[r_rs], writes=[r_rs])
    S.add("dve", lambda E: E.reciprocal(out=rs[:, :], in_=rs[:, :]), reads=[r_rs], writes=[r_rs])


def phase_p6(S, nc, cfg, dt, L):
    TOK, NT = cfg.TOK, cfg.NT
    with contextlib.ExitStack() as es:
        stage = Ring([sb(es, nc, f"p6_st{i}", [128, 4, 512], F32) for i in range(4)], "st")
        a1 = sb(es, nc, "p6_a1", [128, 16, NT], BF16)
        a2 = sb(es, nc, "p6_a2", [128, 16, NT], BF16)
        r1, r2 = [Res("p6_a1")], [Res("p6_a2")]
        wbf = alloc_wbf(es, nc, "p6", [16, 16])
        gat = Ring([sb(es, nc, f"p6_ga{i}", [128, 512], BF16) for i in range(3)], "ga")
        gct = Ring([sb(es, nc, f"p6_gc{i}", [128, 512], BF16) for i in range(3)], "gc")
        t1 = Ring([sb(es, nc, f"p6_t1{i}", [128, 512], F32) for i in range(2)], "t1")
        t2 = Ring([sb(es, nc, f"p6_t2{i}", [128, 512], F32) for i in range(2)], "t2")
        om = Ring([sb(es, nc, f"p6_om{i}", [128, 512], BF16) for i in range(3)], "om")
        banks = Ring([ps(es, nc, f"p6_ps{i}", [128, 512], F32) for i in range(8)], "ps")

        def evac(bl, col0, tokoff):
            (ba, ra), (bc_, rc_) = bl
            ga_ap, ga_res, gs = gat.next()
            gc_ap, gc_res, cs = gct.next()
            dma(S, ga_ap[:, :], dt["gaT"][col0:col0 + 128, tokoff:tokoff + 512], f"ga{gs}", writes=[ga_res])
            dma(S, gc_ap[:, :], dt["gcT"][col0:col0 + 128, tokoff:tokoff + 512], f"gc{cs}", writes=[gc_res])
            t1a, t1r, _ = t1.next()
            t2a, t2r, _ = t2.next()
            o_ap, o_res, os_ = om.next()
            S.add("dve", lambda E: E.tensor_tensor(out=t1a[:, :], in0=ba[:, :], in1=ga_ap[:, :], op=ALU.mult),
                  reads=[ra, ga_res], writes=[t1r])
            S.add("dve", lambda E: E.tensor_tensor(out=t2a[:, :], in0=bc_[:, :], in1=gc_ap[:, :], op=ALU.mult),
                  reads=[rc_, gc_res], writes=[t2r])
            S.add("pool", lambda E: E.tensor_tensor(out=o_ap[:, :], in0=t1a[:, :], in1=t2a[:, :], op=ALU.add),
                  reads=[t1r, t2r], writes=[o_res])
            dma(S, dt["mT"][col0:col0 + 128, tokoff:tokoff + 512], o_ap[:, :], f"om{os_}", reads=[o_res], q="act")

        for nt in range(TOK // NT):
            load_act(S, cfg, a1, r1, dt["agT"], 16, nt * NT, stage, False, NT)
            load_act(S, cfg, a2, r2, dt["cgT"], 16, nt * NT, stage, False, NT)
            gemm_phase(S, nc, es, cfg, f"p6_{nt}", [(a1, r1, dt["w_o_attn"][L], 16), (a2, r2, dt["w_o_conv"][L], 16)],
                       D, evac, nt * NT, stage, NT, banks, wbf)
        S.barrier()


def phase_p7(S, nc, cfg, dt, L, x_src):
    TOK, NT = cfg.TOK, cfg.NT
    with contextlib.ExitStack() as es:
        stage = Ring([sb(es, nc, f"p7_st{i}", [128, 4, 512], F32) for i in range(4)], "st")
        a1 = sb(es, nc, "p7_a1", [128, 32, NT], BF16)
        r1 = [Res("p7_a1")]
        wbf = alloc_wbf(es, nc, "p7", [32])
        xt = Ring([sb(es, nc, f"p7_xt{i}", [128, 512], F32) for i in range(3)], "xt")
        y32 = Ring([sb(es, nc, f"p7_y{i}", [128, 512], F32) for i in range(3)], "y")
        yb = Ring([sb(es, nc, f"p7_yb{i}", [128, 512], BF16) for i in range(3)], "yb")
        banks = Ring([ps(es, nc, f"p7_ps{i}", [128, 512], F32) for i in range(8)], "ps")

        def evac(bl, col0, tokoff):
            (ba, ra), = bl
            x_ap, x_res, xs = xt.next()
            y_ap, y_res, ys = y32.next()
            b_ap, b_res, bs = yb.next()
            dma(S, x_ap[:, :], x_src[col0:col0 + 128, tokoff:tokoff + 512], f"xt{xs}", writes=[x_res])
            S.add("dve", lambda E: E.scalar_tensor_tensor(out=y_ap[:, :], in0=x_ap[:, :], scalar=float(ALPHA), in1=ba[:, :],
                                                          op0=ALU.mult, op1=ALU.add),
                  reads=[x_res, ra], writes=[y_res])
            dma(S, dt["yT"][col0:col0 + 128, tokoff:tokoff + 512], y_ap[:, :], f"y{ys}", reads=[y_res], q="act")
            S.add("pool", lambda E: E.tensor_copy(out=b_ap[:, :], in_=y_ap[:, :]), reads=[y_res], writes=[b_res])
            dma(S, dt["ybT"][col0:col0 + 128, tokoff:tokoff + 512], b_ap[:, :], f"yb{bs}", reads=[b_res], q="act")

        for nt in range(TOK // NT):
            load_act(S, cfg, a1, r1, dt["mT"], 32, nt * NT, stage, False, NT)
            gemm_phase(S, nc, es, cfg, f"p7_{nt}", [(a1, r1, dt["w_out"][L], 32)], D, evac, nt * NT, stage, NT, banks, wbf)
        S.barrier()


def phase_p8(S, nc, cfg, dt, L, dst):
    TOK = cfg.TOK
    NT = min(1024, TOK)
    NTT = NT // 512
    with contextlib.ExitStack() as es:
        vec = sb(es, nc, "p8_vec", [128, 96], F32)
        rvec = Res("p8_vec")
        dma(S, vec[:, :], dt["vecs"][L, :, V_BPG:V_BPG + 96], "vec", writes=[rvec])
        ones = sb(es, nc, "p8_ones", [128, 128], BF16)
        rones = Res("ones")
        dma(S, ones[:, :], dt["cst_bf"][:, 128:256], "ones", writes=[rones])
        stage = Ring([sb(es, nc, f"p8_st{i}", [128, 4, 512], F32) for i in range(3)], "st")
        a1 = sb(es, nc, "p8_a1", [128, 32, NT], BF16)
        a2 = sb(es, nc, "p8_a2", [128, 2, NT], BF16)
        r1, r2 = [Res("p8_a1")], [Res("p8_a2")]
        wbf = alloc_wbf(es, nc, "p8", [32, 2])
        sgt = Ring([sb(es, nc, f"p8_sg{i}", [128, 512], F32) for i in range(2)], "sg")
        yt = Ring([sb(es, nc, f"p8_y{i}", [128, 512], F32) for i in range(2)], "y")
        zt = Ring([sb(es, nc, f"p8_z{i}", [128, 512], F32) for i in range(2)], "z")
        zb = Ring([sb(es, nc, f"p8_zb{i}", [128, 512], BF16) for i in range(4)], "zb")
        zq = Ring([sb(es, nc, f"p8_zq{i}", [128, 512], BF16) for i in range(4)], "zq")
        banks = Ring([ps(es, nc, f"p8_ps{i}", [128, 512], F32) for i in range(8 - 2 * NTT)], "ps")
        st_sum = [ps(es, nc, f"p8_ss{i}", [128, 512], F32) for i in range(NTT)]
        st_sq = [ps(es, nc, f"p8_sq{i}", [128, 512], F32) for i in range(NTT)]
        r_st = Res("stats")
        pend = []

        def evac(bl, col0, tokoff):
            (bg, rg), (bu, ru) = bl
            g = col0 // 128
            tt = (tokoff % NT) // 512
            s_ap, s_res, _ = sgt.next()
            y_ap, y_res, ys = yt.next()
            z_ap, z_res, zs = zt.next()
            b_ap, b_res, _ = zb.next()
            q_ap, q_res, _ = zq.next()
            dma(S, y_ap[:, :], dt["yT"][col0:col0 + 128, tokoff:tokoff + 512], f"y{ys}", writes=[y_res])
            S.add("act", lambda E: E.activation(out=s_ap[:, :], in_=bg[:, :], func=AF.Sigmoid, bias=vec[:, g:g + 1], scale=1.0),
                  reads=[rg, rvec], writes=[s_res])
            S.add("dve", lambda E: E.tensor_tensor(out=s_ap[:, :], in0=bu[:, :], in1=s_ap[:, :], op=ALU.mult),
                  reads=[ru, s_res], writes=[s_res])
            S.add("pool", lambda E: E.tensor_tensor(out=z_ap[:, :], in0=s_ap[:, :], in1=y_ap[:, :], op=ALU.add),
                  reads=[s_res, y_res], writes=[z_res])
            dma(S, dt["zT"][col0:col0 + 128, tokoff:tokoff + 512], z_ap[:, :], f"z{zs}", reads=[z_res], q="act")
            S.add("act", lambda E: E.copy(out=b_ap[:, :], in_=z_ap[:, :]), reads=[z_res], writes=[b_res])
            S.add("act", lambda E: E.activation(out=q_ap[:, :], in_=z_ap[:, :], func=AF.Square), reads=[z_res], writes=[q_res])

            def stat(E):
                E.matmul(st_sum[tt][:, :], lhsT=ones[:, :], rhs=b_ap[:, :], start=(g == 0), stop=(g == 31))
                return E.matmul(st_sq[tt][:, :], lhsT=ones[:, :], rhs=q_ap[:, :], start=(g == 0), stop=(g == 31))
            pend.append(lambda: S.add("pe", stat, reads=[b_res, q_res, rones], writes=[r_st]))
            while len(pend) > 2:
                pend.pop(0)()

        ot = Ring([sb(es, nc, f"p8_o{i}", [128, 512], F32) for i in range(2)], "o")
        zn = Ring([sb(es, nc, f"p8_zn{i}", [128, 512], F32) for i in range(2)], "zn")
        mu = sb(es, nc, "p8_mu", [128, NT], F32)
        rs = sb(es, nc, "p8_rs", [128, NT], F32)
        r_mu, r_rs = Res("mu"), Res("rs")
        lnt = sb(es, nc, "p8_lnt", [128, NT], F32)
        r_lnt = Res("lnt")

        def norm_unit(nt, f, tt):
            z_ap, z_res, zs = zn.next()
            o_ap, o_res, os_ = ot.next()
            tsl = slice(nt * NT + tt * 512, nt * NT + (tt + 1) * 512)
            msl = slice(tt * 512, (tt + 1) * 512)
            dma(S, z_ap[:, :], dt["zT"][f * 128:(f + 1) * 128, tsl], f"zn{zs}", writes=[z_res])
            S.add("dve", lambda E, z=z_ap: E.tensor_tensor(out=z[:, :], in0=z[:, :], in1=mu[:, msl], op=ALU.subtract),
                  reads=[z_res, r_mu], writes=[z_res])
            S.add("dve", lambda E, z=z_ap: E.tensor_tensor(out=z[:, :], in0=z[:, :], in1=rs[:, msl], op=ALU.mult),
                  reads=[z_res, r_rs], writes=[z_res])
            S.add("act", lambda E, z=z_ap, o=o_ap, f=f: E.activation(out=o[:, :], in_=z[:, :], func=AF.Identity,
                                                                   scale=vec[:, 32 + f:33 + f], bias=vec[:, 64 + f:65 + f]),
                  reads=[z_res, rvec], writes=[o_res])
            dma(S, dst[f * 128:(f + 1) * 128, tsl], o_ap[:, :], f"o{os_}", reads=[o_res], q="act")

        side = []
        for nt in range(TOK // NT):
            load_act(S, cfg, a1, r1, dt["ybT"], 32, nt * NT, stage, False, NT)
            load_act(S, cfg, a2, r2, dt["pT"][L], 2, nt * NT, stage, True, NT)
            gemm_phase(S, nc, es, cfg, f"p8_{nt}", [(a1, r1, dt["w_ple_gate"][L], 32), (a2, r2, dt["w_ple_up"][L], 2)],
                       D, evac, nt * NT, stage, NT, banks, wbf, side)
            while side:
                side.pop(0)()
            while pend:
                pend.pop(0)()
            ln_stats(S, es, nc, f"p8_{nt}", NTT, st_sum, st_sq, r_st, mu, rs, r_mu, r_rs, 1.0 / D, lnt, r_lnt)
            side = [lambda nt=nt, f=f, tt=tt: norm_unit(nt, f, tt) for f in range(32) for tt in range(NTT)]
        while side:
            side.pop(0)()
        S.barrier()


def build_program(tok, depth=DEPTH, phases=None, debug_out=()):
    cfg = Cfg(tok)
    TOK, T, NB, NQ = cfg.TOK, cfg.T, cfg.NB, cfg.NQ
    nc = bass.Bass("TRN2", target_bir_lowering=False)
    dt = {}

    def din(name, shape, dtype):
        dt[name] = nc.dram_tensor(name, list(shape), dtype, kind="ExternalInput").ap()

    def dscr(name, shape, dtype, internal=False):
        if name in debug_out and not internal:
            dt[name] = nc.dram_tensor(name, list(shape), dtype, kind="ExternalOutput").ap()
        else:
            dt[name] = nc.dram_tensor(name, list(shape), dtype).ap()

    din("xT", [D, TOK], F32)
    din("pT", [depth, PLE, TOK], F32)
    din("w_in", [depth, D, DIN], F32)
    din("w_o_attn", [depth, DA, D], F32)
    din("w_o_conv", [depth, DA, D], F32)
    din("w_out", [depth, D, D], F32)
    din("w_ple_up", [depth, PLE, D], F32)
    din("w_ple_gate", [depth, D, D], F32)
    din("vecs", [depth, 128, NV], F32)
    din("cst_bf", [128, 256 + NB * 128 + 512], BF16)
    din("pm", [128, NQ * NB], F32)
    din("halo_w", [128, GROUP], F32)
    dt["outT"] = nc.dram_tensor("outT", [D, TOK], F32, kind="ExternalOutput").ap()
    dscr("qT", [DA, TOK], BF16)
    dt["kvT"] = [nc.dram_tensor(f"kvT{i}", [cfg.RP, TOK], BF16).ap() for i in range(cfg.NPC)]
    dt["kvT_all"] = [nc.dram_tensor(f"kvT_all{i}", [GROUP * cfg.RP, TOK], BF16).ap() for i in range(cfg.NPC)]
    dscr("zaT", [DA, TOK], BF16)
    dscr("valT", [DA, TOK], F32)
    dscr("sgT", [DA, TOK], F32)
    dscr("zcT", [DA, TOK], BF16)
    dscr("gaT", [D, TOK], BF16)
    dscr("gcT", [D, TOK], BF16)
    dscr("agT", [DA, TOK], BF16)
    dscr("cT", [DA, TOK], F32)
    dscr("cgT", [DA, TOK], BF16)
    dscr("mT", [D, TOK], BF16)
    dscr("yT", [D, TOK], F32)
    dscr("ybT", [D, TOK], BF16)
    dscr("zT", [D, TOK], F32)
    dscr("x1T", [D, TOK], F32)
    dscr("tail", [DA, 32], F32, internal=True)
    dscr("tail_all", [GROUP * DA, 32], F32, internal=True)

    def on(p):
        return phases is None or p in phases

    with contextlib.ExitStack() as top:
        esems = {k: top.enter_context(nc.semaphore(f"s_{k}")) for k in Sched.CE}
        dsems = [top.enter_context(nc.semaphore(f"s_d{i}")) for i in range(NDS)]
        ccsem = top.enter_context(nc.semaphore("s_cc"))
        S = Sched(nc, esems, dsems, ccsem)
        for L in range(depth):
            x_src = dt["xT"] if L == 0 else dt["x1T"]
            dst = dt["outT"] if L == depth - 1 else dt["x1T"]
            if on("p1"):
                phase_p1(S, nc, cfg, dt, L, x_src)
            r_tail = None
            if on("xch"):
                phase_tail(S, nc, cfg, dt)
                r_tail = phase_exchange(S, nc, cfg, dt)
            if on("conv"):
                phase_conv(S, nc, cfg, dt, L, r_tail)
            elif on("xch"):
                S.barrier()
            if on("attn"):
                phase_attn(S, nc, cfg, dt)
            if on("p6"):
                phase_p6(S, nc, cfg, dt, L)
            if on("p7"):
                phase_p7(S, nc, cfg, dt, L, x_src)
            if on("p8"):
                phase_p8(S, nc, cfg, dt, L, dst)
        S.barrier()
        print("ops:", len(S.ops), "instructions:", S.n_inst)
    return nc, cfg


def host_constants(cfg, rank):
    NB, NQ = cfg.NB, cfg.NQ
    bf = ml_dtypes.bfloat16
    ident = np.eye(128, dtype=np.float32)
    ones = np.ones((128, 128), np.float32)
    sel = np.zeros((128, NB, 128), np.float32)
    for n in range(NB):
        sel[n, n, :] = BIG
    tri = np.zeros((128, 2, 256), np.float32)
    for c in range(2):
        k = c * 128 + np.arange(128)[:, None]
        q = np.arange(256)[None, :]
        tri[:, c, :] = np.where(k <= q, 0.0, -BIG)
    cst = np.concatenate([ident, ones, sel.reshape(128, -1), tri.reshape(128, -1)], axis=1).astype(bf)
    pm = np.zeros((128, NQ, NB), np.float32)
    for j in range(NQ):
        own = rank * cfg.NBL + j // 2
        pm[:, j, own:] = -1e30
    hw = np.zeros((128, GROUP), np.float32)
    if rank > 0:
        hw[:, rank - 1] = 1.0
    return cst, pm.reshape(128, -1), hw


def pack_vecs(inp, depth):
    out = []
    for l in range(depth):
        cols = [inp["b_in"][l].reshape(-1, 128).T]
        cols.append(inp["w_conv"][l].T.reshape(16, 128, CW).transpose(1, 0, 2).reshape(128, 16 * CW))
        cols.append(inp["b_conv"][l].reshape(16, 128).T)
        cols.append(inp["conv_ln_g"][l].reshape(16, 128).T)
        cols.append(inp["conv_ln_b"][l].reshape(16, 128).T)
        cols.append(inp["b_ple_gate"][l].reshape(32, 128).T)
        cols.append(inp["ln_g"][l].reshape(32, 128).T)
        cols.append(inp["ln_b"][l].reshape(32, 128).T)
        out.append(np.concatenate(cols, axis=1))
    return np.ascontiguousarray(np.stack(out).astype(np.float32))


def make_in_maps(inp, cfg, depth, seq_len=None):
    TOK = cfg.TOK
    vecs = pack_vecs(inp, depth)
    maps = []
    for c in range(NCORES):
        b, r = c // GROUP, c % GROUP
        sl = slice(r * TOK, (r + 1) * TOK)
        cst, pm, hs = host_constants(cfg, r)
        m = {
            "xT": np.ascontiguousarray(inp["x"][b, sl, :].T),
            "pT": np.ascontiguousarray(inp["p"][:depth, b, sl, :].transpose(0, 2, 1)),
            "w_in": inp["w_in"][:depth], "w_o_attn": inp["w_o_attn"][:depth],
            "w_o_conv": inp["w_o_conv"][:depth], "w_out": inp["w_out"][:depth],
            "w_ple_up": inp["w_ple_up"][:depth], "w_ple_gate": inp["w_ple_gate"][:depth],
            "vecs": vecs, "cst_bf": cst, "pm": pm, "halo_w": hs,
        }
        maps.append(m)
    return maps


_CACHE = {}


def kernel(**inputs):
    inp = {k: np.asarray(v) for k, v in inputs.items()}
    B, Tfull, _ = inp["x"].shape
    tok = Tfull // GROUP
    if tok not in _CACHE:
        _CACHE[tok] = build_program(tok)
    nc, cfg = _CACHE[tok]
    maps = make_in_maps(inp, cfg, DEPTH)
    res = run_bass_kernel_spmd(nc, maps, core_ids=list(range(NCORES)))
    out = np.empty((B, Tfull, D), np.float32)
    for c in range(NCORES):
        b, r = c // GROUP, c % GROUP
        out[b, r * tok:(r + 1) * tok, :] = res.results[c]["outT"].T
    return out
```

```python
import contextlib
import numpy as np
import ml_dtypes
import concourse.bass as bass
import concourse.mybir as mybir
from concourse.bass_utils import run_bass_kernel_spmd

F32 = mybir.dt.float32
BF16 = mybir.dt.bfloat16
AF = mybir.ActivationFunctionType
ALU = mybir.AluOpType
AX = mybir.AxisListType

NCORES = 8
GROUP = 4
D = 4096
DA = 2048
NH = 16
DH = 128
PLE = 256
CW = 31
BLK = 256
DIN = 22528
DEPTH = 2
LN_EPS = 1e-5
ALPHA = (2.0 * DEPTH) ** 0.25
SCALE = DH ** -0.5
BIG = 30000.0
NDS = 40


class Res:
    __slots__ = ("name", "lw", "rd")

    def __init__(self, name):
        self.name = name
        self.lw = None
        self.rd = []


class Op:
    __slots__ = ("eng", "fn", "deps", "dma", "sem", "ticket", "inc", "grp")

    def __init__(self, eng, fn, deps, dma, grp=None):
        self.grp = grp
        self.eng = eng
        self.fn = fn
        self.deps = deps
        self.dma = dma
        self.sem = None
        self.ticket = 0
        self.inc = 0


class Sched:
    CE = ("pe", "act", "dve", "pool")

    def __init__(self, nc, esems, dsems, ccsem):
        self.nc = nc
        self.e = {"pe": nc.tensor, "act": nc.scalar, "dve": nc.vector, "pool": nc.gpsimd,
                  "sp": nc.sync}
        self.esem = esems
        self.dsem = dsems
        self.ccsem = ccsem
        self.ops = []
        self.emitted = 0
        self.barrier_idx = 0
        self.ecnt = {k: 0 for k in self.CE}
        self.dcnt = [0] * len(dsems)
        self.cccnt = 0
        self.dmap = {}
        self.waited = {}
        self.n_inst = 0

    def dkey(self, name):
        if name not in self.dmap:
            idx = len(self.dmap)
            assert idx < len(self.dsem), "out of dma semaphores"
            self.dmap[name] = idx
        return self.dmap[name]

    def add(self, eng, fn, reads=(), writes=(), dma=None, extra_deps=(), grp=None):
        i = len(self.ops)
        deps = set(extra_deps)
        for r in reads:
            if r.lw is not None:
                deps.add(r.lw)
        for w in writes:
            if w.lw is not None:
                deps.add(w.lw)
            deps.update(w.rd)
        for r in reads:
            r.rd.append(i)
        for w in writes:
            w.lw = i
            w.rd = []
        bi = self.barrier_idx
        deps = {d for d in deps if d >= bi}
        if dma is not None and dma != "cc":
            dma = self.dkey(dma)
        self.ops.append(Op(eng, fn, deps, dma, grp))
        return i

    def barrier(self):
        last = {}
        for i in range(self.barrier_idx, len(self.ops)):
            op = self.ops[i]
            if op.fn is None:
                continue
            key = op.eng if op.dma is None else ("d", op.dma)
            last[key] = i
        deps = set(last.values())
        for eng in ("pe", "act", "dve", "pool", "sp"):
            self.ops.append(Op(eng, None, set(deps), None))
        self.emit()
        self.barrier_idx = len(self.ops)
        self.dmap = {}

    def emit(self):
        ops = self.ops
        start = self.emitted
        needed = set()
        for i in range(start, len(ops)):
            needed |= ops[i].deps
        for i in range(start, len(ops)):
            op = ops[i]
            if op.fn is None:
                continue
            if op.dma == "cc":
                self.cccnt += 1
                op.sem, op.ticket, op.inc = self.ccsem, self.cccnt, 1
            elif op.dma is not None:
                self.dcnt[op.dma] += 16
                op.sem, op.ticket, op.inc = self.dsem[op.dma], self.dcnt[op.dma], 16
            elif i in needed:
                self.ecnt[op.eng] += 1
                op.sem, op.ticket, op.inc = self.esem[op.eng], self.ecnt[op.eng], 1
        gmax = {}
        for i in range(start, len(ops)):
            op = ops[i]
            if op.grp is not None and op.sem is not None:
                gmax[op.grp] = max(gmax.get(op.grp, 0), op.ticket)
        for i in range(start, len(ops)):
            op = ops[i]
            if op.grp is not None and op.sem is not None:
                op.ticket = gmax[op.grp]
        for i in range(start, len(ops)):
            op = ops[i]
            E = self.e[op.eng]
            w = {}
            for d in op.deps:
                dop = ops[d]
                if dop.sem is None:
                    continue
                if op.eng == "pe" and dop.eng == "pe" and dop.dma is None and op.dma is None:
                    continue
                k = id(dop.sem)
                if k not in w or w[k][1] < dop.ticket:
                    w[k] = (dop.sem, dop.ticket)
            for k, (sem, val) in w.items():
                wk = (op.eng, k)
                if self.waited.get(wk, 0) >= val:
                    continue
                E.wait_ge(sem, val)
                self.waited[wk] = val
                self.n_inst += 1
            if op.fn is not None:
                ins = op.fn(E)
                self.n_inst += 1
                if op.sem is not None:
                    ins.then_inc(op.sem, op.inc)
            op.fn = None if op.fn is None else 0
        self.emitted = len(ops)


class Ring:
    def __init__(self, aps, name):
        self.aps = aps
        self.res = [Res(f"{name}{i}") for i in range(len(aps))]
        self.i = 0
        self.name = name

    def next(self):
        k = self.i % len(self.aps)
        self.i += 1
        return self.aps[k], self.res[k], k


class Cfg:
    def __init__(self, tok):
        self.TOK = tok
        self.T = tok * GROUP
        self.NB = self.T // BLK
        self.NBL = tok // BLK
        self.NT = min(1024, tok)
        self.NQ = tok // 128
        self.NQT = tok // 512
        self.NPC = max(1, (2 * DA * tok * 2) // (1 << 20))
        self.RP = 2 * DA // self.NPC


def gemm_phase(S, nc, es, cfg, name, streams, ncols, evac, tok0, stage, NT, banks, wbf, side=None, cast_eng=("dve", "act")):
    ntt = NT // 512
    ncg = ncols // 512
    ns = len(streams)
    pieces = []
    for cg in range(ncg):
        for si, (_, _, W, KC) in enumerate(streams):
            for k0 in range(0, KC, 4):
                pieces.append((cg, si, k0, min(4, KC - k0)))
    per_cg = len(pieces) // ncg
    state = {"dma": 0, "cast": 0}

    def emit_dma(p):
        cg, si, k0, nk = pieces[p]
        W = streams[si][2]
        st_ap, st_res, slot = stage.aps[p % len(stage.aps)], stage.res[p % len(stage.aps)], p % len(stage.aps)
        src = W[k0 * 128:(k0 + nk) * 128, cg * 512:(cg + 1) * 512].rearrange("(kc p) c -> p kc c", p=128)
        S.add("sp", lambda E, o=st_ap[:, 0:nk, :], i=src: E.dma_start(out=o, in_=i),
              writes=[st_res], dma=f"stage{slot}")

    def emit_cast(p):
        cg, si, k0, nk = pieces[p]
        st_ap, st_res = stage.aps[p % len(stage.aps)], stage.res[p % len(stage.aps)]
        wt, wres = wbf[si]
        eng = cast_eng[p % 2]
        if eng == "act":
            S.add(eng, lambda E, o=wt[:, cg % 2, k0:k0 + nk, :], i=st_ap[:, 0:nk, :]: E.copy(out=o, in_=i),
                  reads=[st_res], writes=[wres[cg % 2]])
        else:
            S.add(eng, lambda E, o=wt[:, cg % 2, k0:k0 + nk, :], i=st_ap[:, 0:nk, :]: E.tensor_copy(out=o, in_=i),
                  reads=[st_res], writes=[wres[cg % 2]])

    def advance(cast_to):
        cast_to = min(cast_to, len(pieces))
        while state["cast"] < cast_to:
            while state["dma"] < min(state["cast"] + len(stage.aps), len(pieces)):
                emit_dma(state["dma"])
                state["dma"] += 1
            emit_cast(state["cast"])
            state["cast"] += 1
        while state["dma"] < min(state["cast"] + len(stage.aps) - 1, len(pieces)):
            emit_dma(state["dma"])
            state["dma"] += 1

    advance(per_cg)
    nsteps = ntt * 4
    for cg in range(ncg):
        step = 0
        for tt in range(ntt):
            for sub in range(4):
                bl = []
                for si, (act, ares, W, KC) in enumerate(streams):
                    bank, bres, _ = banks.next()
                    wt, wres = wbf[si]

                    def mm(E, bank=bank, wt=wt, act=act, KC=KC, cg=cg, sub=sub, tt=tt):
                        ins = None
                        for kc in range(KC):
                            ins = E.matmul(bank[:, :], lhsT=wt[:, cg % 2, kc, sub * 128:(sub + 1) * 128],
                                           rhs=act[:, kc, tt * 512:(tt + 1) * 512],
                                           start=(kc == 0), stop=(kc == KC - 1))
                        return ins
                    S.add("pe", mm, reads=[wres[cg % 2]] + list(ares), writes=[bres])
                    bl.append((bank, bres))
                evac(bl, cg * 512 + sub * 128, tok0 + tt * 512)
                if side:
                    side.pop(0)()
                step += 1
                advance(min((cg + 2) * per_cg, (cg + 1) * per_cg + ((step + 2) * per_cg + nsteps - 1) // nsteps))


S_UNIQ = []


def alloc_wbf(es, nc, name, kcs):
    out = []
    for si, KC in enumerate(kcs):
        t = es.enter_context(nc.sbuf_tensor(f"{name}_wbf{si}_L{len(S_UNIQ)}", [128, 2, KC, 512], BF16))
        S_UNIQ.append(0)
        out.append((t, [Res(f"{name}_wbf{si}_0"), Res(f"{name}_wbf{si}_1")]))
    return out


def load_act(S, cfg, act, ares, src, KC, tokoff, stage, is_f32, NT):
    if not is_f32:
        g = ("act", id(act), tokoff, S.barrier_idx, len(S.ops))
        n = 0
        for k0 in range(0, KC, 8):
            nk = min(8, KC - k0)
            s = src[k0 * 128:(k0 + nk) * 128, tokoff:tokoff + NT].rearrange("(kc p) t -> p kc t", p=128)
            S.add("sp", lambda E, o=act[:, k0:k0 + nk, :], i=s: E.dma_start(out=o, in_=i),
                  writes=(list(ares) if n == 0 else []), dma=f"act{id(act) % 97}", grp=g)
            n += 1
        return
    per = max(1, 2048 // NT)
    n = 0
    for k0 in range(0, KC, per):
        nk = min(per, KC - k0)
        st_ap, st_res, slot = stage.next()
        sv = st_ap[:, :, :].rearrange("p a b -> p (a b)")[:, 0:nk * NT].rearrange("p (k t) -> p k t", t=NT)
        s = src[k0 * 128:(k0 + nk) * 128, tokoff:tokoff + NT].rearrange("(kc p) t -> p kc t", p=128)
        S.add("sp", lambda E, o=sv, i=s: E.dma_start(out=o, in_=i), writes=[st_res], dma=f"stage{slot}")
        S.add(["dve", "pool"][n % 2], lambda E, o=act[:, k0:k0 + nk, :], i=sv: E.tensor_copy(out=o, in_=i),
              reads=[st_res], writes=[ares[n % len(ares)]])
        n += 1


P1_SEGS = [
    (0, 2048, "qT", 0, "Identity", BF16),
    (2048, 4096, "kvT", 0, "Identity", BF16),
    (4096, 6144, "kvT", 2048, "Identity", BF16),
    (6144, 8192, "zaT", 0, "Silu", BF16),
    (8192, 10240, "valT", 0, "Identity", F32),
    (10240, 12288, "sgT", 0, "Sigmoid", F32),
    (12288, 14336, "zcT", 0, "Silu", BF16),
    (14336, 18432, "gaT", 0, "Sigmoid", BF16),
    (18432, 22528, "gcT", 0, "Sigmoid", BF16),
]
V_BIN = 0
V_WCONV = DIN // 128
V_BCONV = V_WCONV + 16 * CW
V_CLNG = V_BCONV + 16
V_CLNB = V_CLNG + 16
V_BPG = V_CLNB + 16
V_LNG = V_BPG + 32
V_LNB = V_LNG + 32
NV = V_LNB + 32


_UNIQ = [0]


def uniq(name):
    _UNIQ[0] += 1
    return f"{name}_u{_UNIQ[0]}"


def sb(es, nc, name, shape, dtype):
    return es.enter_context(nc.sbuf_tensor(uniq(name), list(shape), dtype))


def ps(es, nc, name, shape, dtype):
    return es.enter_context(nc.psum_tensor(uniq(name), list(shape), dtype))


def dma(S, out, in_, key, reads=(), writes=(), grp=None, q="sp"):
    return S.add(q, lambda E, o=out, i=in_: E.dma_start(out=o, in_=i), reads=reads, writes=writes,
                 dma=key, grp=grp)


def phase_p1(S, nc, cfg, dt, L, x_src):
    TOK, NT = cfg.TOK, cfg.NT
    with contextlib.ExitStack() as es:
        vec = sb(es, nc, "p1_vec", [128, DIN // 128], F32)
        vres = Res("vec")
        dma(S, vec[:, :], dt["vecs"][L, :, 0:DIN // 128], "vec", writes=[vres])
        stage = Ring([sb(es, nc, f"p1_st{i}", [128, 4, 512], F32) for i in range(4)], "st")
        act = sb(es, nc, "p1_act", [128, 32, NT], BF16)
        ares = [Res(f"p1_act{i}") for i in range(8)]
        wbf = alloc_wbf(es, nc, "p1", [32])
        obf = Ring([sb(es, nc, f"p1_obf{i}", [128, 512], BF16) for i in range(4)], "obf")
        o32 = Ring([sb(es, nc, f"p1_o32{i}", [128, 512], F32) for i in range(4)], "o32")
        banks = Ring([ps(es, nc, f"p1_ps{i}", [128, 512], F32) for i in range(8)], "ps")

        def evac_p1(bl, col0, tokoff):
            bank, bres = bl[0]
            seg = [s for s in P1_SEGS if s[0] <= col0 < s[1]][0]
            row0 = seg[3] + col0 - seg[0]
            if seg[2] == "kvT":
                dst = dt["kvT"][row0 // cfg.RP]
                row0 = row0 % cfg.RP
            else:
                dst = dt[seg[2]]
            ring = obf if seg[5] == BF16 else o32
            o_ap, o_res, slot = ring.next()
            g = col0 // 128
            func = getattr(AF, seg[4])
            S.add("act", lambda E, o=o_ap[:, :], i=bank[:, :], f=func, b=vec[:, g:g + 1]:
                  E.activation(out=o, in_=i, func=f, bias=b, scale=1.0),
                  reads=[bres, vres], writes=[o_res])
            dma(S, dst[row0:row0 + 128, tokoff:tokoff + 512], o_ap[:, :], f"{ring.name}{slot}", reads=[o_res], q="act")

        for nt in range(TOK // NT):
            load_act(S, cfg, act, ares, x_src, 32, nt * NT, stage, True, NT)
            gemm_phase(S, nc, es, cfg, f"p1_{nt}", [(act, ares, dt["w_in"][L], 32)], DIN,
                       evac_p1, nt * NT, stage, NT, banks, wbf, None, ("dve", "pool"))
        S.barrier()


def phase_tail(S, nc, cfg, dt):
    TOK = cfg.TOK
    with contextlib.ExitStack() as es:
        a = sb(es, nc, "tl_a", [128, 16, 32], F32)
        b = sb(es, nc, "tl_b", [128, 16, 32], F32)
        ra, rb = Res("tl_a"), Res("tl_b")
        if True:
            dma(S, a[:, :, :], dt["valT"][:, TOK - 32:TOK].rearrange("(c p) t -> p c t", p=128), "tla", writes=[ra])
            dma(S, b[:, :, :], dt["sgT"][:, TOK - 32:TOK].rearrange("(c p) t -> p c t", p=128), "tlb", writes=[rb])
            S.add("dve", lambda E: E.tensor_tensor(out=a[:, :, :], in0=a[:, :, :], in1=b[:, :, :], op=ALU.mult),
                  reads=[ra, rb], writes=[ra])
            dma(S, dt["tail"].rearrange("(c p) t -> p c t", p=128), a[:, :, :], "tlo", reads=[ra])
            S.barrier()


def phase_exchange(S, nc, cfg, dt):
    groups = [list(range(g * GROUP, (g + 1) * GROUP)) for g in range(NCORES // GROUP)]
    r_tail = Res("tail_all")
    S.add("pool", lambda E: E.collective_compute("AllGather", ALU.bypass, replica_groups=groups,
                                                 ins=[dt["tail"].opt()], outs=[dt["tail_all"].opt()]),
          dma="cc", writes=[r_tail])
    for i in range(cfg.NPC):
        S.add("pool", lambda E, i=i: E.collective_compute("AllGather", ALU.bypass, replica_groups=groups,
                                                          ins=[dt["kvT"][i].opt()], outs=[dt["kvT_all"][i].opt()]),
              dma="cc")
    return r_tail


def phase_attn(S, nc, cfg, dt):
    TOK, T, NB, NQ, NQT = cfg.TOK, cfg.T, cfg.NB, cfg.NQ, cfg.NQT
    NKC = T // 128
    NKO = TOK // 128
    with contextlib.ExitStack() as es:
        NCB = 256 + NB * 128 + 512
        cst = sb(es, nc, "at_cst", [128, NCB], BF16)
        pm = sb(es, nc, "at_pm", [128, NQ * NB], F32)
        rc = Res("cst")
        dma(S, cst[:, :], dt["cst_bf"][:, :], "cst", writes=[rc])
        dma(S, pm[:, :], dt["pm"][:, :], "pm", writes=[rc])
        ident = cst[:, 0:128]
        ones = cst[:, 128:256]
        sel = cst[:, 256:256 + NB * 128].rearrange("p (n m) -> p n m", m=128)
        tri = cst[:, 256 + NB * 128:256 + NB * 128 + 512].rearrange("p (c q) -> p c q", q=256)

        KT = [sb(es, nc, f"at_KT{i}", [128, T], BF16) for i in range(2)]
        VT = [sb(es, nc, f"at_VT{i}", [128, T], BF16) for i in range(2)]
        QT = [sb(es, nc, f"at_QT{i}", [128, TOK], BF16) for i in range(2)]
        KTo = [sb(es, nc, f"at_KTo{i}", [128, TOK], BF16) for i in range(2)]
        VTo = [sb(es, nc, f"at_VTo{i}", [128, TOK], BF16) for i in range(2)]
        Vsb = [sb(es, nc, f"at_V{i}", [128, NKC, 129], BF16) for i in range(2)]
        Vo = [sb(es, nc, f"at_Vo{i}", [128, NKO, 129], BF16) for i in range(2)]
        mbT = [sb(es, nc, f"at_mbT{i}", [128, TOK], BF16) for i in range(2)]
        kmb = [sb(es, nc, f"at_kmb{i}", [128, NB], BF16) for i in range(2)]
        r_ld = [Res(f"ld{i}") for i in range(2)]
        r_V = [Res(f"V{i}") for i in range(2)]
        r_Vo = [Res(f"Vo{i}") for i in range(2)]
        r_mbT = [Res(f"mbT{i}") for i in range(2)]
        r_kmb = [Res(f"kmb{i}") for i in range(2)]
        km32 = sb(es, nc, "at_km32", [128, NB], F32)
        r_km32 = Res("km32")
        gm = sb(es, nc, "at_gm", [128, NQ * NB], F32)
        r_gm = Res("gm")
        top8 = sb(es, nc, "at_top8", [128, NQ, 8], F32)
        r_top8 = Res("top8")
        thr = sb(es, nc, "at_thr", [128, NQ], F32)
        r_thr = Res("thr")
        mbb = sb(es, nc, "at_mbb", [128, NQ, 128], BF16)
        r_mbb = Res("mbb")
        S.add("pool", lambda E: E.memset(mbb[:, :, :], 0.0), writes=[r_mbb])
        for i in range(2):
            S.add("pool", lambda E, i=i: E.memset(Vsb[i][:, :, 128:129], 1.0), writes=[r_V[i]])
            S.add("pool", lambda E, i=i: E.memset(Vo[i][:, :, 128:129], 1.0), writes=[r_Vo[i]])
        identf = sb(es, nc, "at_identf", [128, 128], F32)
        r_idf = Res("identf")
        S.add("dve", lambda E: E.tensor_copy(out=identf[:, :], in_=ident), reads=[rc], writes=[r_idf])
        rdn = sb(es, nc, "at_rdn", [128, 4], F32)
        r_rdn = Res("rdn")
        attm = sb(es, nc, "at_attm", [128, 4, 128], F32)
        r_attm = Res("attm")
        Pt = Ring([sb(es, nc, f"at_Pt{i}", [128, 512], BF16) for i in range(5)], "Pt")
        zat = Ring([sb(es, nc, f"at_za{i}", [128, 512], BF16) for i in range(2)], "za")
        rd = sb(es, nc, "at_rd", [128, 512], F32)
        r_rd = Res("rd")
        at32 = sb(es, nc, "at_at32", [128, 512], F32)
        r_at32 = Res("at32")
        agb = Ring([sb(es, nc, f"at_agb{i}", [128, 512], BF16) for i in range(2)], "agb")
        Sb = Ring([ps(es, nc, f"at_S{i}", [128, 512], F32) for i in range(3)], "S")
        O2 = [ps(es, nc, "at_O2a", [128, 512], F32), ps(es, nc, "at_O2b", [128, 512], F32)]
        r_O2 = [Res("O2a"), Res("O2b")]
        gps = ps(es, nc, "at_gps", [128, 512], F32)
        r_gps = Res("gps")
        mps = ps(es, nc, "at_mps", [128, 1024], BF16)
        r_mps = Res("mps")
        vtp = Ring([ps(es, nc, f"at_vtp{i}", [128, 1024], BF16) for i in range(1)], "vtp")

        def stageA(h):
            s = h % 2
            RP = cfg.RP

            def kv_all(row):
                return dt["kvT_all"][row // RP].rearrange("(r f) t -> f r t", r=GROUP)[row % RP:row % RP + 128, :, :]

            def kv_own(row):
                return dt["kvT"][row // RP][row % RP:row % RP + 128, :]
            g = ("ld", h)
            dma(S, KT[s][:, :].rearrange("p (r t) -> p r t", r=GROUP), kv_all(h * 128),
                f"ld{s}", writes=[r_ld[s]], grp=g)
            dma(S, VT[s][:, :].rearrange("p (r t) -> p r t", r=GROUP), kv_all(DA + h * 128), f"ld{s}", grp=g)
            dma(S, QT[s][:, :], dt["qT"][h * 128:(h + 1) * 128, :], f"ld{s}", grp=g)
            dma(S, KTo[s][:, :], kv_own(h * 128), f"ld{s}", grp=g)
            dma(S, VTo[s][:, :], kv_own(DA + h * 128), f"ld{s}", grp=g)

        def transposes(src, n, dst, rdst, rsrc):
            k = 0
            for c0 in range(0, n, 8):
                nn = min(8, n - c0)
                vp, rvp, _ = vtp.next()

                def tr(E, vp=vp, c0=c0, nn=nn):
                    ins = None
                    for c in range(nn):
                        ins = E.transpose(out=vp[:, c * 128:(c + 1) * 128],
                                          in_=src[:, (c0 + c) * 128:(c0 + c + 1) * 128], identity=ident)
                    return ins
                S.add("pe", tr, reads=[rsrc, rc], writes=[rvp])
                eng = "act" if k % 2 == 0 else "dve"
                if eng == "act":
                    S.add("act", lambda E, o=dst[:, c0:c0 + nn, 0:128], i=vp[:, 0:nn * 128].rearrange("p (c d) -> p c d", d=128):
                          E.copy(out=o, in_=i), reads=[rvp], writes=[rdst])
                else:
                    S.add("dve", lambda E, o=dst[:, c0:c0 + nn, 0:128], i=vp[:, 0:nn * 128].rearrange("p (c d) -> p c d", d=128):
                          E.tensor_copy(out=o, in_=i), reads=[rvp], writes=[rdst])
                k += 1

        def stageB(h):
            s = h % 2
            transposes(VT[s], NKC, Vsb[s], r_V[s], r_ld[s])
            transposes(VTo[s], NKO, Vo[s], r_Vo[s], r_ld[s])
            S.add("dve", lambda E: E.tensor_reduce(out=km32[:, :], in_=KT[s][:, :].rearrange("p (n k) -> p n k", k=BLK),
                                                   axis=AX.X, op=ALU.add), reads=[r_ld[s]], writes=[r_km32])
            S.add("dve", lambda E: E.tensor_scalar(out=kmb[s][:, :], in0=km32[:, :], scalar1=1.0 / BLK, scalar2=None,
                                                   op0=ALU.mult), reads=[r_km32], writes=[r_kmb[s]])

        def stageC(h):
            s = h % 2

            def gmm(E):
                ins = None
                for j in range(NQ):
                    ins = E.matmul(gps[:, j * NB:(j + 1) * NB], lhsT=QT[s][:, j * 128:(j + 1) * 128],
                                   rhs=kmb[s][:, :], start=True, stop=True)
                return ins
            S.add("pe", gmm, reads=[r_ld[s], r_kmb[s]], writes=[r_gps])
            S.add("dve", lambda E: E.tensor_tensor(out=gm[:, :], in0=gps[:, 0:NQ * NB], in1=pm[:, :], op=ALU.add),
                  reads=[r_gps, rc], writes=[r_gm])
            for j in range(NQ):
                S.add("dve", lambda E, j=j: E.max(out=top8[:, j, :], in_=gm[:, j * NB:(j + 1) * NB]),
                      reads=[r_gm], writes=[r_top8])
            S.add("dve", lambda E: E.tensor_scalar(out=thr[:, :], in0=top8[:, :, 2], scalar1=-1e29, scalar2=None,
                                                   op0=ALU.max), reads=[r_top8], writes=[r_thr])
            for j in range(NQ):
                S.add("dve", lambda E, j=j: E.tensor_scalar(out=mbb[:, j, 0:NB], in0=gm[:, j * NB:(j + 1) * NB],
                                                            scalar1=thr[:, j:j + 1], scalar2=1.0,
                                                            op0=ALU.is_ge, op1=ALU.subtract),
                      reads=[r_gm, r_thr], writes=[r_mbb])

        def stageD(h):
            s = h % 2
            for j0 in range(0, NQ, 8):
                nn = min(8, NQ - j0)

                def tr(E, j0=j0, nn=nn):
                    ins = None
                    for j in range(nn):
                        ins = E.transpose(out=mps[:, j * 128:(j + 1) * 128], in_=mbb[:, j0 + j, :], identity=ident)
                    return ins
                S.add("pe", tr, reads=[r_mbb, rc], writes=[r_mps])
                S.add("act", lambda E, j0=j0, nn=nn: E.copy(out=mbT[s][:, j0 * 128:(j0 + nn) * 128], in_=mps[:, 0:nn * 128]),
                      reads=[r_mps], writes=[r_mbT[s]])

        def main(h, hooks):
            s = h % 2
            for t in range(NQT):
                za_ap, za_res, zslot = zat.next()
                dma(S, za_ap[:, :], dt["zaT"][h * 128:(h + 1) * 128, t * 512:(t + 1) * 512], f"za{zslot}", writes=[za_res])
                q512 = QT[s][:, t * 512:(t + 1) * 512]
                pendq = []
                chunks = []
                for n in range(min(NB, (GROUP - 1) * cfg.NBL + 2 * t + 1)):
                    for c in range(2):
                        chunks.append(("past", n, c))
                for half in range(2):
                    for c in range(2):
                        chunks.append(("own", half, c))
                nch = len(chunks)
                first = [True]

                def pv(kind, a, c, p_ap, p_res, last):
                    st = first[0]
                    first[0] = False
                    if kind == "past":
                        v1 = Vsb[s][:, a * 2 + c, :]
                        rv = r_V[s]

                        def f(E):
                            ins = None
                            for qs in range(4):
                                ins = E.matmul(O2[qs // 2][:, (qs % 2) * 129:(qs % 2) * 129 + 129],
                                               lhsT=p_ap[:, qs * 128:(qs + 1) * 128], rhs=v1,
                                               start=(st and qs % 2 == 0), stop=last)
                            return ins
                        S.add("pe", f, reads=[rv, p_res], writes=[r_O2[0], r_O2[1]])
                    else:
                        lb = 2 * t + a
                        v1 = Vo[s][:, lb * 2 + c, :]
                        rv = r_Vo[s]

                        def f(E):
                            ins = None
                            for qq in range(2):
                                ins = E.matmul(O2[a][:, qq * 129:qq * 129 + 129],
                                               lhsT=p_ap[:, qq * 128:(qq + 1) * 128], rhs=v1, start=False, stop=last)
                            return ins
                        S.add("pe", f, reads=[rv, p_res], writes=[r_O2[a]])

                for ci, (kind, a, c) in enumerate(chunks):
                    s_ap, s_res, _ = Sb.next()
                    p_ap, p_res, _ = Pt.next()
                    if kind == "past":
                        def f(E, s_ap=s_ap, a=a, c=c, q512=q512, t=t):
                            E.matmul(s_ap[:, :], lhsT=KT[s][:, a * 256 + c * 128:a * 256 + (c + 1) * 128], rhs=q512,
                                     start=True, stop=False)
                            return E.matmul(s_ap[:, :], lhsT=sel[:, a, :], rhs=mbT[s][:, t * 512:(t + 1) * 512],
                                            start=False, stop=True)
                        S.add("pe", f, reads=[r_ld[s], r_mbT[s], rc], writes=[s_res])
                        S.add("act", lambda E, o=p_ap[:, :], i=s_ap[:, :]: E.activation(out=o, in_=i, func=AF.Exp, scale=SCALE),
                              reads=[s_res], writes=[p_res])
                    else:
                        lb = 2 * t + a

                        def f(E, s_ap=s_ap, a=a, c=c, lb=lb, t=t):
                            E.matmul(s_ap[:, 0:256], lhsT=KTo[s][:, lb * 256 + c * 128:lb * 256 + (c + 1) * 128],
                                     rhs=QT[s][:, t * 512 + a * 256:t * 512 + (a + 1) * 256], start=True, stop=False)
                            return E.matmul(s_ap[:, 0:256], lhsT=ident, rhs=tri[:, c, :], start=False, stop=True)
                        S.add("pe", f, reads=[r_ld[s], rc], writes=[s_res])
                        S.add("act", lambda E, o=p_ap[:, 0:256], i=s_ap[:, 0:256]: E.activation(out=o, in_=i, func=AF.Exp, scale=SCALE),
                              reads=[s_res], writes=[p_res])
                    pendq.append((kind, a, c, p_ap, p_res))
                    if len(pendq) > 2:
                        pv(*pendq.pop(0), False)
                while pendq:
                    item = pendq.pop(0)
                    pv(*item, len(pendq) == 0)
                for qs in range(4):
                    bk, off = O2[qs // 2], (qs % 2) * 129
                    S.add("dve", lambda E, bk=bk, off=off, qs=qs: E.reciprocal(out=rdn[:, qs:qs + 1], in_=bk[:, off + 128:off + 129]),
                          reads=[r_O2[qs // 2]], writes=[r_rdn])
                    S.add("dve", lambda E, bk=bk, off=off, qs=qs: E.tensor_scalar(out=attm[:, qs, :], in0=bk[:, off:off + 128], scalar1=rdn[:, qs:qs + 1], scalar2=None, op0=ALU.mult),
                          reads=[r_O2[qs // 2], r_rdn], writes=[r_attm])
                tp_ap, tp_res, _ = Sb.next()

                def trb(E, tp_ap=tp_ap):
                    ins = None
                    for qs in range(4):
                        ins = E.transpose(out=tp_ap[:, qs * 128:(qs + 1) * 128], in_=attm[:, qs, :], identity=identf[:, :])
                    return ins
                S.add("pe", trb, reads=[r_attm, r_idf], writes=[tp_res])
                g_ap, g_res, gslot = agb.next()
                S.add("dve", lambda E, o=g_ap[:, :], z=za_ap[:, :], tp=tp_ap: E.tensor_tensor(out=o, in0=tp[:, :], in1=z, op=ALU.mult),
                      reads=[tp_res, za_res], writes=[g_res])
                dma(S, dt["agT"][h * 128:(h + 1) * 128, t * 512:(t + 1) * 512], g_ap[:, :], f"agb{gslot}", reads=[g_res], q="act")
                if hooks:
                    hooks.pop(0)()
            while hooks:
                hooks.pop(0)()

        for hh in range(min(1, NH)):
            stageA(0), stageB(0), stageC(0), stageD(0)
        for h in range(NH):
            hooks = []
            if h + 1 < NH:
                hooks = [lambda h=h: stageA(h + 1), lambda h=h: stageB(h + 1), lambda h=h: stageC(h + 1),
                         lambda h=h: stageD(h + 1)]
                if NQT >= 4:
                    hooks[0]()
                    hooks = hooks[1:]
            main(h, hooks)
        S.barrier()


def phase_conv(S, nc, cfg, dt, L, r_tail=None):
    TOK = cfg.TOK
    HT = min(1024, TOK)
    NH2 = TOK // HT
    NTT = HT // 512
    with contextlib.ExitStack() as es:
        vec = sb(es, nc, "cv_vec", [128, NV - V_WCONV], F32)
        rvec = Res("cv_vec")
        dma(S, vec[:, :], dt["vecs"][L, :, V_WCONV:NV], "vec", writes=[rvec])
        bc = lambda ch: vec[:, V_BCONV - V_WCONV + ch:V_BCONV - V_WCONV + ch + 1]
        lg = lambda ch: vec[:, V_CLNG - V_WCONV + ch:V_CLNG - V_WCONV + ch + 1]
        lb = lambda ch: vec[:, V_CLNB - V_WCONV + ch:V_CLNB - V_WCONV + ch + 1]
        hw = sb(es, nc, "cv_hw", [128, GROUP], F32)
        rhw = Res("hw")
        dma(S, hw[:, :], dt["halo_w"][:, :], "hw", writes=[rhw])
        cst = sb(es, nc, "cv_cst", [128, 256], BF16)
        rcst = Res("cst")
        dma(S, cst[:, :], dt["cst_bf"][:, 0:256], "cst", writes=[rcst])
        ident, ones = cst[:, 0:128], cst[:, 128:256]
        vl = Ring([sb(es, nc, f"cv_vl{i}", [128, 32 + HT], F32) for i in range(2)], "vl")
        sg = Ring([sb(es, nc, f"cv_sg{i}", [128, 32 + HT], F32) for i in range(2)], "sg")
        tl = Ring([sb(es, nc, f"cv_tl{i}", [128, GROUP, 32], F32) for i in range(3)], "tl")
        hl = Ring([sb(es, nc, f"cv_hl{i}", [128, 32], F32) for i in range(2)], "hl")
        ub = Ring([sb(es, nc, f"cv_ub{i}", [128, 32 + HT], BF16) for i in range(2)], "ub")
        dg = Ring([sb(es, nc, f"cv_dg{i}", [128, CW, 128], BF16) for i in range(2)], "dg")
        dgB = [Res("dgB0"), Res("dgB1")]
        c32 = Ring([sb(es, nc, f"cv_c{i}", [128, 512], F32) for i in range(4)], "c")
        cb = Ring([sb(es, nc, f"cv_cb{i}", [128, 512], BF16) for i in range(6)], "cb")
        cq = Ring([sb(es, nc, f"cv_cq{i}", [128, 512], BF16) for i in range(6)], "cq")
        cps = Ring([ps(es, nc, f"cv_ps{i}", [128, 512], F32) for i in range(4)], "cps")
        st_sum = [ps(es, nc, f"cv_ss{i}", [128, 512], F32) for i in range(NTT)]
        st_sq = [ps(es, nc, f"cv_sq{i}", [128, 512], F32) for i in range(NTT)]
        r_st = Res("stats")
        mu = sb(es, nc, "cv_mu", [128, HT], F32)
        rs = sb(es, nc, "cv_rs", [128, HT], F32)
        r_mu, r_rs = Res("mu"), Res("rs")
        ct = Ring([sb(es, nc, f"cv_ct{i}", [128, 512], F32) for i in range(3)], "ct")
        zc = Ring([sb(es, nc, f"cv_zc{i}", [128, 512], BF16) for i in range(3)], "zc")
        og = Ring([sb(es, nc, f"cv_og{i}", [128, 512], BF16) for i in range(3)], "og")
        tail_all = dt["tail_all"].rearrange("(r c p) t -> c p r t", r=GROUP, p=128)
        for half in range(NH2):
            t0 = half * HT
            loaded = {}

            def loads(ch, half=half, t0=t0):
                v_ap, v_res, vs = vl.next()
                s_ap, s_res, ss = sg.next()
                rows = slice(ch * 128, (ch + 1) * 128)
                tinfo = None
                if half == 0:
                    dma(S, v_ap[:, 32:32 + HT], dt["valT"][rows, 0:HT], f"vl{vs}", writes=[v_res])
                    dma(S, s_ap[:, 32:32 + HT], dt["sgT"][rows, 0:HT], f"sg{ss}", writes=[s_res])
                    t_ap, t_res, ts = tl.next()
                    dma(S, t_ap[:, :, :], tail_all[ch], f"tl{ts}", reads=([r_tail] if r_tail is not None else []),
                        writes=[t_res])
                    tinfo = (t_ap, t_res)
                else:
                    dma(S, v_ap[:, :], dt["valT"][rows, t0 - 32:t0 + HT], f"vl{vs}", writes=[v_res])
                    dma(S, s_ap[:, :], dt["sgT"][rows, t0 - 32:t0 + HT], f"sg{ss}", writes=[s_res])
                loaded[ch] = (v_ap, v_res, s_ap, s_res, tinfo)

            pend = []
            loads(0)
            for ch in range(16):
                if ch + 1 < 16:
                    loads(ch + 1)
                v_ap, v_res, s_ap, s_res, tinfo = loaded.pop(ch)
                u_ap, u_res, _ = ub.next()
                d_ap, d_res, dslot = dg.next()
                rows = slice(ch * 128, (ch + 1) * 128)
                if half == 0:
                    t_ap, t_res = tinfo
                    h_ap, h_res, _ = hl.next()
                    S.add("dve", lambda E, h=h_ap, t=t_ap: E.tensor_scalar(out=h[:, :], in0=t[:, 0, :], scalar1=hw[:, 0:1], scalar2=None, op0=ALU.mult),
                          reads=[t_res, rhw], writes=[h_res])
                    for r in range(1, GROUP):
                        S.add("dve", lambda E, h=h_ap, t=t_ap, r=r: E.scalar_tensor_tensor(out=h[:, :], in0=t[:, r, :], scalar=hw[:, r:r + 1], in1=h[:, :], op0=ALU.mult, op1=ALU.add),
                              reads=[t_res, rhw, h_res], writes=[h_res])
                    S.add("dve", lambda E, h=h_ap, u=u_ap: E.tensor_copy(out=u[:, 0:32], in_=h[:, :]), reads=[h_res], writes=[u_res])
                    S.add("dve", lambda E, u=u_ap, v=v_ap, s_=s_ap: E.tensor_tensor(out=u[:, 32:32 + HT], in0=v[:, 32:32 + HT], in1=s_[:, 32:32 + HT], op=ALU.mult),
                          reads=[v_res, s_res], writes=[u_res])
                else:
                    S.add("dve", lambda E, u=u_ap, v=v_ap, s_=s_ap: E.tensor_tensor(out=u[:, :], in0=v[:, :], in1=s_[:, :], op=ALU.mult),
                          reads=[v_res, s_res], writes=[u_res])
                for j in range(CW):
                    S.add("dve", lambda E, d=d_ap, ch=ch, j=j: E.tensor_scalar(out=d[:, j, :], in0=ident, scalar1=vec[:, ch * CW + j:ch * CW + j + 1], scalar2=None, op0=ALU.mult),
                          reads=[rvec, rcst], writes=[d_res])
                newp = []
                for tt in range(NTT):
                    p_ap, p_res, _ = cps.next()

                    def taps(E, p=p_ap, d=d_ap, u=u_ap, tt=tt):
                        ins = None
                        for j in range(CW):
                            ins = E.matmul(p[:, :], lhsT=d[:, j, :], rhs=u[:, 2 + j + tt * 512:2 + j + tt * 512 + 512],
                                           start=(j == 0), stop=(j == CW - 1))
                        return ins
                    S.add("pe", taps, reads=[d_res, u_res], writes=[p_res])
                    c_ap, c_res, cs = c32.next()
                    b_ap, b_res, _ = cb.next()
                    q_ap, q_res, _ = cq.next()
                    S.add("act", lambda E, c=c_ap, p=p_ap, ch=ch: E.activation(out=c[:, :], in_=p[:, :], func=AF.Identity, bias=bc(ch), scale=1.0),
                          reads=[p_res, rvec], writes=[c_res])
                    dma(S, dt["cT"][rows, t0 + tt * 512:t0 + (tt + 1) * 512], c_ap[:, :], f"c{cs}", reads=[c_res], q="act")
                    S.add("pool", lambda E, o=b_ap, c=c_ap: E.tensor_copy(out=o[:, :], in_=c[:, :]), reads=[c_res], writes=[b_res])
                    S.add("act", lambda E, o=q_ap, c=c_ap: E.activation(out=o[:, :], in_=c[:, :], func=AF.Square), reads=[c_res], writes=[q_res])

                    def stat(E, b=b_ap, q=q_ap, ch=ch, tt=tt):
                        E.matmul(st_sum[tt][:, :], lhsT=ones, rhs=b[:, :], start=(ch == 0), stop=(ch == 15))
                        return E.matmul(st_sq[tt][:, :], lhsT=ones, rhs=q[:, :], start=(ch == 0), stop=(ch == 15))
                    newp.append(lambda stat=stat, b_res=b_res, q_res=q_res: S.add("pe", stat, reads=[b_res, q_res, rcst], writes=[r_st]))
                while pend:
                    pend.pop(0)()
                pend = newp
            while pend:
                pend.pop(0)()
            ln_stats(S, es, nc, f"cv{half}", NTT, st_sum, st_sq, r_st, mu, rs, r_mu, r_rs, 1.0 / DA)
            for ch in range(16):
                for tt in range(NTT):
                    c_ap, c_res, cs = ct.next()
                    z_ap, z_res, zs = zc.next()
                    o_ap, o_res, os_ = og.next()
                    tsl = slice(t0 + tt * 512, t0 + (tt + 1) * 512)
                    msl = slice(tt * 512, (tt + 1) * 512)
                    dma(S, c_ap[:, :], dt["cT"][ch * 128:(ch + 1) * 128, tsl], f"ct{cs}", writes=[c_res])
                    dma(S, z_ap[:, :], dt["zcT"][ch * 128:(ch + 1) * 128, tsl], f"zc{zs}", writes=[z_res])
                    S.add("dve", lambda E, c=c_ap, msl=msl: E.tensor_tensor(out=c[:, :], in0=c[:, :], in1=mu[:, msl], op=ALU.subtract),
                          reads=[c_res, r_mu], writes=[c_res])
                    S.add("dve", lambda E, c=c_ap, msl=msl: E.tensor_tensor(out=c[:, :], in0=c[:, :], in1=rs[:, msl], op=ALU.mult),
                          reads=[c_res, r_rs], writes=[c_res])
                    S.add("act", lambda E, c=c_ap, ch=ch: E.activation(out=c[:, :], in_=c[:, :], func=AF.Silu, scale=lg(ch), bias=lb(ch)),
                          reads=[c_res, rvec], writes=[c_res])
                    S.add("pool", lambda E, c=c_ap, z=z_ap, o=o_ap: E.tensor_tensor(out=o[:, :], in0=c[:, :], in1=z[:, :], op=ALU.mult),
                          reads=[c_res, z_res], writes=[o_res])
                    dma(S, dt["cgT"][ch * 128:(ch + 1) * 128, tsl], o_ap[:, :], f"og{os_}", reads=[o_res], q="act")
        S.barrier()


def ln_stats(S, es, nc, name, NTT, st_sum, st_sq, r_st, mu, rs, r_mu, r_rs, inv_n, tmp=None, r_tmp=None):
    for tt in range(NTT):
        tsl = slice(tt * 512, (tt + 1) * 512)
        S.add("act", lambda E, tt=tt, tsl=tsl: E.activation(out=mu[:, tsl], in_=st_sum[tt][:, :], func=AF.Identity, scale=inv_n),
              reads=[r_st], writes=[r_mu])
        S.add("act", lambda E, tt=tt, tsl=tsl: E.activation(out=rs[:, tsl], in_=st_sq[tt][:, :], func=AF.Identity, scale=inv_n),
              reads=[r_st], writes=[r_rs])
    if tmp is None:
        tmp = sb(es, nc, f"{name}_lntmp", [128, NTT * 512], F32)
        r_tmp = Res("lntmp")
    S.add("dve", lambda E: E.tensor_tensor(out=tmp[:, :], in0=mu[:, :], in1=mu[:, :], op=ALU.mult), reads=[r_mu], writes=[r_tmp])
    S.add("dve", lambda E: E.tensor_tensor(out=rs[:, :], in0=rs[:, :], in1=tmp[:, :], op=ALU.subtract), reads=[r_rs, r_tmp], writes=[r_rs])
    S.add("dve", lambda E: E.tensor_scalar(out=rs[:, :], in0=rs[:, :], scalar1=0.0, scalar2=LN_EPS, op0=ALU.max, op1=ALU.add),
          reads=[r_rs], writes=[r_rs])
    S.add("act", lambda E: E.activation(out=rs[:, :], in_=rs[:, :], func=AF.Sqrt), reads=[r_rs], writes=[r_rs])
    S.add("dve", lambda E: E.reciprocal(out=rs[:, :], in_=rs[:, :]), reads=[r_rs], writes=[r_rs])


def phase_p6(S, nc, cfg, dt, L):
    TOK, NT = cfg.TOK, cfg.NT
    with contextlib.ExitStack() as es:
        stage = Ring([sb(es, nc, f"p6_st{i}", [128, 4, 512], F32) for i in range(4)], "st")
        a1 = sb(es, nc, "p6_a1", [128, 16, NT], BF16)
        a2 = sb(es, nc, "p6_a2", [128, 16, NT], BF16)
        r1, r2 = [Res("p6_a1")], [Res("p6_a2")]
        wbf = alloc_wbf(es, nc, "p6", [16, 16])
        gat = Ring([sb(es, nc, f"p6_ga{i}", [128, 512], BF16) for i in range(3)], "ga")
        gct = Ring([sb(es, nc, f"p6_gc{i}", [128, 512], BF16) for i in range(3)], "gc")
        t1 = Ring([sb(es, nc, f"p6_t1{i}", [128, 512], F32) for i in range(2)], "t1")
        t2 = Ring([sb(es, nc, f"p6_t2{i}", [128, 512], F32) for i in range(2)], "t2")
        om = Ring([sb(es, nc, f"p6_om{i}", [128, 512], BF16) for i in range(3)], "om")
        banks = Ring([ps(es, nc, f"p6_ps{i}", [128, 512], F32) for i in range(8)], "ps")

        def evac(bl, col0, tokoff):
            (ba, ra), (bc_, rc_) = bl
            ga_ap, ga_res, gs = gat.next()
            gc_ap, gc_res, cs = gct.next()
            dma(S, ga_ap[:, :], dt["gaT"][col0:col0 + 128, tokoff:tokoff + 512], f"ga{gs}", writes=[ga_res])
            dma(S, gc_ap[:, :], dt["gcT"][col0:col0 + 128, tokoff:tokoff + 512], f"gc{cs}", writes=[gc_res])
            t1a, t1r, _ = t1.next()
            t2a, t2r, _ = t2.next()
            o_ap, o_res, os_ = om.next()
            S.add("dve", lambda E: E.tensor_tensor(out=t1a[:, :], in0=ba[:, :], in1=ga_ap[:, :], op=ALU.mult),
                  reads=[ra, ga_res], writes=[t1r])
            S.add("dve", lambda E: E.tensor_tensor(out=t2a[:, :], in0=bc_[:, :], in1=gc_ap[:, :], op=ALU.mult),
                  reads=[rc_, gc_res], writes=[t2r])
            S.add("pool", lambda E: E.tensor_tensor(out=o_ap[:, :], in0=t1a[:, :], in1=t2a[:, :], op=ALU.add),
                  reads=[t1r, t2r], writes=[o_res])
            dma(S, dt["mT"][col0:col0 + 128, tokoff:tokoff + 512], o_ap[:, :], f"om{os_}", reads=[o_res], q="act")

        for nt in range(TOK // NT):
            load_act(S, cfg, a1, r1, dt["agT"], 16, nt * NT, stage, False, NT)
            load_act(S, cfg, a2, r2, dt["cgT"], 16, nt * NT, stage, False, NT)
            gemm_phase(S, nc, es, cfg, f"p6_{nt}", [(a1, r1, dt["w_o_attn"][L], 16), (a2, r2, dt["w_o_conv"][L], 16)],
                       D, evac, nt * NT, stage, NT, banks, wbf)
        S.barrier()


def phase_p7(S, nc, cfg, dt, L, x_src):
    TOK, NT = cfg.TOK, cfg.NT
    with contextlib.ExitStack() as es:
        stage = Ring([sb(es, nc, f"p7_st{i}", [128, 4, 512], F32) for i in range(4)], "st")
        a1 = sb(es, nc, "p7_a1", [128, 32, NT], BF16)
        r1 = [Res("p7_a1")]
        wbf = alloc_wbf(es, nc, "p7", [32])
        xt = Ring([sb(es, nc, f"p7_xt{i}", [128, 512], F32) for i in range(3)], "xt")
        y32 = Ring([sb(es, nc, f"p7_y{i}", [128, 512], F32) for i in range(3)], "y")
        yb = Ring([sb(es, nc, f"p7_yb{i}", [128, 512], BF16) for i in range(3)], "yb")
        banks = Ring([ps(es, nc, f"p7_ps{i}", [128, 512], F32) for i in range(8)], "ps")

        def evac(bl, col0, tokoff):
            (ba, ra), = bl
            x_ap, x_res, xs = xt.next()
            y_ap, y_res, ys = y32.next()
            b_ap, b_res, bs = yb.next()
            dma(S, x_ap[:, :], x_src[col0:col0 + 128, tokoff:tokoff + 512], f"xt{xs}", writes=[x_res])
            S.add("dve", lambda E: E.scalar_tensor_tensor(out=y_ap[:, :], in0=x_ap[:, :], scalar=float(ALPHA), in1=ba[:, :],
                                                          op0=ALU.mult, op1=ALU.add),
                  reads=[x_res, ra], writes=[y_res])
            dma(S, dt["yT"][col0:col0 + 128, tokoff:tokoff + 512], y_ap[:, :], f"y{ys}", reads=[y_res], q="act")
            S.add("pool", lambda E: E.tensor_copy(out=b_ap[:, :], in_=y_ap[:, :]), reads=[y_res], writes=[b_res])
            dma(S, dt["ybT"][col0:col0 + 128, tokoff:tokoff + 512], b_ap[:, :], f"yb{bs}", reads=[b_res], q="act")

        for nt in range(TOK // NT):
            load_act(S, cfg, a1, r1, dt["mT"], 32, nt * NT, stage, False, NT)
            gemm_phase(S, nc, es, cfg, f"p7_{nt}", [(a1, r1, dt["w_out"][L], 32)], D, evac, nt * NT, stage, NT, banks, wbf)
        S.barrier()


def phase_p8(S, nc, cfg, dt, L, dst):
    TOK = cfg.TOK
    NT = min(1024, TOK)
    NTT = NT // 512
    with contextlib.ExitStack() as es:
        vec = sb(es, nc, "p8_vec", [128, 96], F32)
        rvec = Res("p8_vec")
        dma(S, vec[:, :], dt["vecs"][L, :, V_BPG:V_BPG + 96], "vec", writes=[rvec])
        ones = sb(es, nc, "p8_ones", [128, 128], BF16)
        rones = Res("ones")
        dma(S, ones[:, :], dt["cst_bf"][:, 128:256], "ones", writes=[rones])
        stage = Ring([sb(es, nc, f"p8_st{i}", [128, 4, 512], F32) for i in range(3)], "st")
        a1 = sb(es, nc, "p8_a1", [128, 32, NT], BF16)
        a2 = sb(es, nc, "p8_a2", [128, 2, NT], BF16)
        r1, r2 = [Res("p8_a1")], [Res("p8_a2")]
        wbf = alloc_wbf(es, nc, "p8", [32, 2])
        sgt = Ring([sb(es, nc, f"p8_sg{i}", [128, 512], F32) for i in range(2)], "sg")
        yt = Ring([sb(es, nc, f"p8_y{i}", [128, 512], F32) for i in range(2)], "y")
        zt = Ring([sb(es, nc, f"p8_z{i}", [128, 512], F32) for i in range(2)], "z")
        zb = Ring([sb(es, nc, f"p8_zb{i}", [128, 512], BF16) for i in range(4)], "zb")
        zq = Ring([sb(es, nc, f"p8_zq{i}", [128, 512], BF16) for i in range(4)], "zq")
        banks = Ring([ps(es, nc, f"p8_ps{i}", [128, 512], F32) for i in range(8 - 2 * NTT)], "ps")
        st_sum = [ps(es, nc, f"p8_ss{i}", [128, 512], F32) for i in range(NTT)]
        st_sq = [ps(es, nc, f"p8_sq{i}", [128, 512], F32) for i in range(NTT)]
        r_st = Res("stats")
        pend = []

        def evac(bl, col0, tokoff):
            (bg, rg), (bu, ru) = bl
            g = col0 // 128
            tt = (tokoff % NT) // 512
            s_ap, s_res, _ = sgt.next()
            y_ap, y_res, ys = yt.next()
            z_ap, z_res, zs = zt.next()
            b_ap, b_res, _ = zb.next()
            q_ap, q_res, _ = zq.next()
            dma(S, y_ap[:, :], dt["yT"][col0:col0 + 128, tokoff:tokoff + 512], f"y{ys}", writes=[y_res])
            S.add("act", lambda E: E.activation(out=s_ap[:, :], in_=bg[:, :], func=AF.Sigmoid, bias=vec[:, g:g + 1], scale=1.0),
                  reads=[rg, rvec], writes=[s_res])
            S.add("dve", lambda E: E.tensor_tensor(out=s_ap[:, :], in0=bu[:, :], in1=s_ap[:, :], op=ALU.mult),
                  reads=[ru, s_res], writes=[s_res])
            S.add("pool", lambda E: E.tensor_tensor(out=z_ap[:, :], in0=s_ap[:, :], in1=y_ap[:, :], op=ALU.add),
                  reads=[s_res, y_res], writes=[z_res])
            dma(S, dt["zT"][col0:col0 + 128, tokoff:tokoff + 512], z_ap[:, :], f"z{zs}", reads=[z_res], q="act")
            S.add("act", lambda E: E.copy(out=b_ap[:, :], in_=z_ap[:, :]), reads=[z_res], writes=[b_res])
            S.add("act", lambda E: E.activation(out=q_ap[:, :], in_=z_ap[:, :], func=AF.Square), reads=[z_res], writes=[q_res])

            def stat(E):
                E.matmul(st_sum[tt][:, :], lhsT=ones[:, :], rhs=b_ap[:, :], start=(g == 0), stop=(g == 31))
                return E.matmul(st_sq[tt][:, :], lhsT=ones[:, :], rhs=q_ap[:, :], start=(g == 0), stop=(g == 31))
            pend.append(lambda: S.add("pe", stat, reads=[b_res, q_res, rones], writes=[r_st]))
            while len(pend) > 2:
                pend.pop(0)()

        ot = Ring([sb(es, nc, f"p8_o{i}", [128, 512], F32) for i in range(2)], "o")
        zn = Ring([sb(es, nc, f"p8_zn{i}", [128, 512], F32) for i in range(2)], "zn")
        mu = sb(es, nc, "p8_mu", [128, NT], F32)
        rs = sb(es, nc, "p8_rs", [128, NT], F32)
        r_mu, r_rs = Res("mu"), Res("rs")
        lnt = sb(es, nc, "p8_lnt", [128, NT], F32)
        r_lnt = Res("lnt")

        def norm_unit(nt, f, tt):
            z_ap, z_res, zs = zn.next()
            o_ap, o_res, os_ = ot.next()
            tsl = slice(nt * NT + tt * 512, nt * NT + (tt + 1) * 512)
            msl = slice(tt * 512, (tt + 1) * 512)
            dma(S, z_ap[:, :], dt["zT"][f * 128:(f + 1) * 128, tsl], f"zn{zs}", writes=[z_res])
            S.add("dve", lambda E, z=z_ap: E.tensor_tensor(out=z[:, :], in0=z[:, :], in1=mu[:, msl], op=ALU.subtract),
                  reads=[z_res, r_mu], writes=[z_res])
            S.add("dve", lambda E, z=z_ap: E.tensor_tensor(out=z[:, :], in0=z[:, :], in1=rs[:, msl], op=ALU.mult),
                  reads=[z_res, r_rs], writes=[z_res])
            S.add("act", lambda E, z=z_ap, o=o_ap, f=f: E.activation(out=o[:, :], in_=z[:, :], func=AF.Identity,
                                                                   scale=vec[:, 32 + f:33 + f], bias=vec[:, 64 + f:65 + f]),
                  reads=[z_res, rvec], writes=[o_res])
            dma(S, dst[f * 128:(f + 1) * 128, tsl], o_ap[:, :], f"o{os_}", reads=[o_res], q="act")

        side = []
        for nt in range(TOK // NT):
            load_act(S, cfg, a1, r1, dt["ybT"], 32, nt * NT, stage, False, NT)
            load_act(S, cfg, a2, r2, dt["pT"][L], 2, nt * NT, stage, True, NT)
            gemm_phase(S, nc, es, cfg, f"p8_{nt}", [(a1, r1, dt["w_ple_gate"][L], 32), (a2, r2, dt["w_ple_up"][L], 2)],
                       D, evac, nt * NT, stage, NT, banks, wbf, side)
            while side:
                side.pop(0)()
            while pend:
                pend.pop(0)()
            ln_stats(S, es, nc, f"p8_{nt}", NTT, st_sum, st_sq, r_st, mu, rs, r_mu, r_rs, 1.0 / D, lnt, r_lnt)
            side = [lambda nt=nt, f=f, tt=tt: norm_unit(nt, f, tt) for f in range(32) for tt in range(NTT)]
        while side:
            side.pop(0)()
        S.barrier()


def build_program(tok, depth=DEPTH, phases=None, debug_out=()):
    cfg = Cfg(tok)
    TOK, T, NB, NQ = cfg.TOK, cfg.T, cfg.NB, cfg.NQ
    nc = bass.Bass("TRN2", target_bir_lowering=False)
    dt = {}

    def din(name, shape, dtype):
        dt[name] = nc.dram_tensor(name, list(shape), dtype, kind="ExternalInput").ap()

    def dscr(name, shape, dtype, internal=False):
        if name in debug_out and not internal:
            dt[name] = nc.dram_tensor(name, list(shape), dtype, kind="ExternalOutput").ap()
        else:
            dt[name] = nc.dram_tensor(name, list(shape), dtype).ap()

    din("xT", [D, TOK], F32)
    din("pT", [depth, PLE, TOK], F32)
    din("w_in", [depth, D, DIN], F32)
    din("w_o_attn", [depth, DA, D], F32)
    din("w_o_conv", [depth, DA, D], F32)
    din("w_out", [depth, D, D], F32)
    din("w_ple_up", [depth, PLE, D], F32)
    din("w_ple_gate", [depth, D, D], F32)
    din("vecs", [depth, 128, NV], F32)
    din("cst_bf", [128, 256 + NB * 128 + 512], BF16)
    din("pm", [128, NQ * NB], F32)
    din("halo_w", [128, GROUP], F32)
    dt["outT"] = nc.dram_tensor("outT", [D, TOK], F32, kind="ExternalOutput").ap()
    dscr("qT", [DA, TOK], BF16)
    dt["kvT"] = [nc.dram_tensor(f"kvT{i}", [cfg.RP, TOK], BF16).ap() for i in range(cfg.NPC)]
    dt["kvT_all"] = [nc.dram_tensor(f"kvT_all{i}", [GROUP * cfg.RP, TOK], BF16).ap() for i in range(cfg.NPC)]
    dscr("zaT", [DA, TOK], BF16)
    dscr("valT", [DA, TOK], F32)
    dscr("sgT", [DA, TOK], F32)
    dscr("zcT", [DA, TOK], BF16)
    dscr("gaT", [D, TOK], BF16)
    dscr("gcT", [D, TOK], BF16)
    dscr("agT", [DA, TOK], BF16)
    dscr("cT", [DA, TOK], F32)
    dscr("cgT", [DA, TOK], BF16)
    dscr("mT", [D, TOK], BF16)
    dscr("yT", [D, TOK], F32)
    dscr("ybT", [D, TOK], BF16)
    dscr("zT", [D, TOK], F32)
    dscr("x1T", [D, TOK], F32)
    dscr("tail", [DA, 32], F32, internal=True)
    dscr("tail_all", [GROUP * DA, 32], F32, internal=True)

    def on(p):
        return phases is None or p in phases

    with contextlib.ExitStack() as top:
        esems = {k: top.enter_context(nc.semaphore(f"s_{k}")) for k in Sched.CE}
        dsems = [top.enter_context(nc.semaphore(f"s_d{i}")) for i in range(NDS)]
        ccsem = top.enter_context(nc.semaphore("s_cc"))
        S = Sched(nc, esems, dsems, ccsem)
        for L in range(depth):
            x_src = dt["xT"] if L == 0 else dt["x1T"]
            dst = dt["outT"] if L == depth - 1 else dt["x1T"]
            if on("p1"):
                phase_p1(S, nc, cfg, dt, L, x_src)
            r_tail = None
            if on("xch"):
                phase_tail(S, nc, cfg, dt)
                r_tail = phase_exchange(S, nc, cfg, dt)
            if on("conv"):
                phase_conv(S, nc, cfg, dt, L, r_tail)
            elif on("xch"):
                S.barrier()
            if on("attn"):
                phase_attn(S, nc, cfg, dt)
            if on("p6"):
                phase_p6(S, nc, cfg, dt, L)
            if on("p7"):
                phase_p7(S, nc, cfg, dt, L, x_src)
            if on("p8"):
                phase_p8(S, nc, cfg, dt, L, dst)
        S.barrier()
        print("ops:", len(S.ops), "instructions:", S.n_inst)
    return nc, cfg


def host_constants(cfg, rank):
    NB, NQ = cfg.NB, cfg.NQ
    bf = ml_dtypes.bfloat16
    ident = np.eye(128, dtype=np.float32)
    ones = np.ones((128, 128), np.float32)
    sel = np.zeros((128, NB, 128), np.float32)
    for n in range(NB):
        sel[n, n, :] = BIG
    tri = np.zeros((128, 2, 256), np.float32)
    for c in range(2):
        k = c * 128 + np.arange(128)[:, None]
        q = np.arange(256)[None, :]
        tri[:, c, :] = np.where(k <= q, 0.0, -BIG)
    cst = np.concatenate([ident, ones, sel.reshape(128, -1), tri.reshape(128, -1)], axis=1).astype(bf)
    pm = np.zeros((128, NQ, NB), np.float32)
    for j in range(NQ):
        own = rank * cfg.NBL + j // 2
        pm[:, j, own:] = -1e30
    hw = np.zeros((128, GROUP), np.float32)
    if rank > 0:
        hw[:, rank - 1] = 1.0
    return cst, pm.reshape(128, -1), hw


def pack_vecs(inp, depth):
    out = []
    for l in range(depth):
        cols = [inp["b_in"][l].reshape(-1, 128).T]
        cols.append(inp["w_conv"][l].T.reshape(16, 128, CW).transpose(1, 0, 2).reshape(128, 16 * CW))
        cols.append(inp["b_conv"][l].reshape(16, 128).T)
        cols.append(inp["conv_ln_g"][l].reshape(16, 128).T)
        cols.append(inp["conv_ln_b"][l].reshape(16, 128).T)
        cols.append(inp["b_ple_gate"][l].reshape(32, 128).T)
        cols.append(inp["ln_g"][l].reshape(32, 128).T)
        cols.append(inp["ln_b"][l].reshape(32, 128).T)
        out.append(np.concatenate(cols, axis=1))
    return np.ascontiguousarray(np.stack(out).astype(np.float32))


def make_in_maps(inp, cfg, depth, seq_len=None):
    TOK = cfg.TOK
    vecs = pack_vecs(inp, depth)
    maps = []
    for c in range(NCORES):
        b, r = c // GROUP, c % GROUP
        sl = slice(r * TOK, (r + 1) * TOK)
        cst, pm, hs = host_constants(cfg, r)
        m = {
            "xT": np.ascontiguousarray(inp["x"][b, sl, :].T),
            "pT": np.ascontiguousarray(inp["p"][:depth, b, sl, :].transpose(0, 2, 1)),
            "w_in": inp["w_in"][:depth], "w_o_attn": inp["w_o_attn"][:depth],
            "w_o_conv": inp["w_o_conv"][:depth], "w_out": inp["w_out"][:depth],
            "w_ple_up": inp["w_ple_up"][:depth], "w_ple_gate": inp["w_ple_gate"][:depth],
            "vecs": vecs, "cst_bf": cst, "pm": pm, "halo_w": hs,
        }
        maps.append(m)
    return maps


_CACHE = {}


def kernel(**inputs):
    inp = {k: np.asarray(v) for k, v in inputs.items()}
    B, Tfull, _ = inp["x"].shape
    tok = Tfull // GROUP
    if tok not in _CACHE:
        _CACHE[tok] = build_program(tok)
    nc, cfg = _CACHE[tok]
    maps = make_in_maps(inp, cfg, DEPTH)
    res = run_bass_kernel_spmd(nc, maps, core_ids=list(range(NCORES)))
    out = np.empty((B, Tfull, D), np.float32)
    for c in range(NCORES):
        b, r = c // GROUP, c % GROUP
        out[b, r * tok:(r + 1) * tok, :] = res.results[c]["outT"].T
    return out
```

```python
import contextlib
import numpy as np
import ml_dtypes
import concourse.bass as bass
import concourse.mybir as mybir
from concourse.bass_utils import run_bass_kernel_spmd

F32 = mybir.dt.float32
BF16 = mybir.dt.bfloat16
AF = mybir.ActivationFunctionType
ALU = mybir.AluOpType
AX = mybir.AxisListType

NCORES = 8
GROUP = 4
D = 4096
DA = 2048
NH = 16
DH = 128
PLE = 256
CW = 31
BLK = 256
DIN = 22528
DEPTH = 2
LN_EPS = 1e-5
ALPHA = (2.0 * DEPTH) ** 0.25
SCALE = DH ** -0.5
BIG = 30000.0
NDS = 40


class Res:
    __slots__ = ("name", "lw", "rd")

    def __init__(self, name):
        self.name = name
        self.lw = None
        self.rd = []


class Op:
    __slots__ = ("eng", "fn", "deps", "dma", "sem", "ticket", "inc", "grp")

    def __init__(self, eng, fn, deps, dma, grp=None):
        self.grp = grp
        self.eng = eng
        self.fn = fn
        self.deps = deps
        self.dma = dma
        self.sem = None
        self.ticket = 0
        self.inc = 0


class Sched:
    CE = ("pe", "act", "dve", "pool")

    def __init__(self, nc, esems, dsems, ccsem):
        self.nc = nc
        self.e = {"pe": nc.tensor, "act": nc.scalar, "dve": nc.vector, "pool": nc.gpsimd,
                  "sp": nc.sync}
        self.esem = esems
        self.dsem = dsems
        self.ccsem = ccsem
        self.ops = []
        self.emitted = 0
        self.barrier_idx = 0
        self.ecnt = {k: 0 for k in self.CE}
        self.dcnt = [0] * len(dsems)
        self.cccnt = 0
        self.dmap = {}
        self.waited = {}
        self.n_inst = 0

    def dkey(self, name):
        if name not in self.dmap:
            idx = len(self.dmap)
            assert idx < len(self.dsem), "out of dma semaphores"
            self.dmap[name] = idx
        return self.dmap[name]

    def add(self, eng, fn, reads=(), writes=(), dma=None, extra_deps=(), grp=None):
        i = len(self.ops)
        deps = set(extra_deps)
        for r in reads:
            if r.lw is not None:
                deps.add(r.lw)
        for w in writes:
            if w.lw is not None:
                deps.add(w.lw)
            deps.update(w.rd)
        for r in reads:
            r.rd.append(i)
        for w in writes:
            w.lw = i
            w.rd = []
        bi = self.barrier_idx
        deps = {d for d in deps if d >= bi}
        if dma is not None and dma != "cc":
            dma = self.dkey(dma)
        self.ops.append(Op(eng, fn, deps, dma, grp))
        return i

    def barrier(self):
        last = {}
        for i in range(self.barrier_idx, len(self.ops)):
            op = self.ops[i]
            if op.fn is None:
                continue
            key = op.eng if op.dma is None else ("d", op.dma)
            last[key] = i
        deps = set(last.values())
        for eng in ("pe", "act", "dve", "pool", "sp"):
            self.ops.append(Op(eng, None, set(deps), None))
        self.emit()
        self.barrier_idx = len(self.ops)
        self.dmap = {}

    def emit(self):
        ops = self.ops
        start = self.emitted
        needed = set()
        for i in range(start, len(ops)):
            needed |= ops[i].deps
        for i in range(start, len(ops)):
            op = ops[i]
            if op.fn is None:
                continue
            if op.dma == "cc":
                self.cccnt += 1
                op.sem, op.ticket, op.inc = self.ccsem, self.cccnt, 1
            elif op.dma is not None:
                self.dcnt[op.dma] += 16
                op.sem, op.ticket, op.inc = self.dsem[op.dma], self.dcnt[op.dma], 16
            elif i in needed:
                self.ecnt[op.eng] += 1
                op.sem, op.ticket, op.inc = self.esem[op.eng], self.ecnt[op.eng], 1
        gmax = {}
        for i in range(start, len(ops)):
            op = ops[i]
            if op.grp is not None and op.sem is not None:
                gmax[op.grp] = max(gmax.get(op.grp, 0), op.ticket)
        for i in range(start, len(ops)):
            op = ops[i]
            if op.grp is not None and op.sem is not None:
                op.ticket = gmax[op.grp]
        for i in range(start, len(ops)):
            op = ops[i]
            E = self.e[op.eng]
            w = {}
            for d in op.deps:
                dop = ops[d]
                if dop.sem is None:
                    continue
                if op.eng == "pe" and dop.eng == "pe" and dop.dma is None and op.dma is None:
                    continue
                k = id(dop.sem)
                if k not in w or w[k][1] < dop.ticket:
                    w[k] = (dop.sem, dop.ticket)
            for k, (sem, val) in w.items():
                wk = (op.eng, k)
                if self.waited.get(wk, 0) >= val:
                    continue
                E.wait_ge(sem, val)
                self.waited[wk] = val
                self.n_inst += 1
            if op.fn is not None:
                ins = op.fn(E)
                self.n_inst += 1
                if op.sem is not None:
                    ins.then_inc(op.sem, op.inc)
            op.fn = None if op.fn is None else 0
        self.emitted = len(ops)


class Ring:
    def __init__(self, aps, name):
        self.aps = aps
        self.res = [Res(f"{name}{i}") for i in range(len(aps))]
        self.i = 0
        self.name = name

    def next(self):
        k = self.i % len(self.aps)
        self.i += 1
        return self.aps[k], self.res[k], k


class Cfg:
    def __init__(self, tok):
        self.TOK = tok
        self.T = tok * GROUP
        self.NB = self.T // BLK
        self.NBL = tok // BLK
        self.NT = min(1024, tok)
        self.NQ = tok // 128
        self.NQT = tok // 512
        self.NPC = max(1, (2 * DA * tok * 2) // (1 << 20))
        self.RP = 2 * DA // self.NPC


def gemm_phase(S, nc, es, cfg, name, streams, ncols, evac, tok0, stage, NT, banks, wbf, side=None, cast_eng=("dve", "act")):
    ntt = NT // 512
    ncg = ncols // 512
    ns = len(streams)
    pieces = []
    for cg in range(ncg):
        for si, (_, _, W, KC) in enumerate(streams):
            for k0 in range(0, KC, 4):
                pieces.append((cg, si, k0, min(4, KC - k0)))
    per_cg = len(pieces) // ncg
    state = {"dma": 0, "cast": 0}

    def emit_dma(p):
        cg, si, k0, nk = pieces[p]
        W = streams[si][2]
        st_ap, st_res, slot = stage.aps[p % len(stage.aps)], stage.res[p % len(stage.aps)], p % len(stage.aps)
        src = W[k0 * 128:(k0 + nk) * 128, cg * 512:(cg + 1) * 512].rearrange("(kc p) c -> p kc c", p=128)
        S.add("sp", lambda E, o=st_ap[:, 0:nk, :], i=src: E.dma_start(out=o, in_=i),
              writes=[st_res], dma=f"stage{slot}")

    def emit_cast(p):
        cg, si, k0, nk = pieces[p]
        st_ap, st_res = stage.aps[p % len(stage.aps)], stage.res[p % len(stage.aps)]
        wt, wres = wbf[si]
        eng = cast_eng[p % 2]
        if eng == "act":
            S.add(eng, lambda E, o=wt[:, cg % 2, k0:k0 + nk, :], i=st_ap[:, 0:nk, :]: E.copy(out=o, in_=i),
                  reads=[st_res], writes=[wres[cg % 2]])
        else:
            S.add(eng, lambda E, o=wt[:, cg % 2, k0:k0 + nk, :], i=st_ap[:, 0:nk, :]: E.tensor_copy(out=o, in_=i),
                  reads=[st_res], writes=[wres[cg % 2]])

    def advance(cast_to):
        cast_to = min(cast_to, len(pieces))
        while state["cast"] < cast_to:
            while state["dma"] < min(state["cast"] + len(stage.aps), len(pieces)):
                emit_dma(state["dma"])
                state["dma"] += 1
            emit_cast(state["cast"])
            state["cast"] += 1
        while state["dma"] < min(state["cast"] + len(stage.aps) - 1, len(pieces)):
            emit_dma(state["dma"])
            state["dma"] += 1

    advance(per_cg)
    nsteps = ntt * 4
    for cg in range(ncg):
        step = 0
        for tt in range(ntt):
            for sub in range(4):
                bl = []
                for si, (act, ares, W, KC) in enumerate(streams):
                    bank, bres, _ = banks.next()
                    wt, wres = wbf[si]

                    def mm(E, bank=bank, wt=wt, act=act, KC=KC, cg=cg, sub=sub, tt=tt):
                        ins = None
                        for kc in range(KC):
                            ins = E.matmul(bank[:, :], lhsT=wt[:, cg % 2, kc, sub * 128:(sub + 1) * 128],
                                           rhs=act[:, kc, tt * 512:(tt + 1) * 512],
                                           start=(kc == 0), stop=(kc == KC - 1))
                        return ins
                    S.add("pe", mm, reads=[wres[cg % 2]] + list(ares), writes=[bres])
                    bl.append((bank, bres))
                evac(bl, cg * 512 + sub * 128, tok0 + tt * 512)
                if side:
                    side.pop(0)()
                step += 1
                advance(min((cg + 2) * per_cg, (cg + 1) * per_cg + ((step + 2) * per_cg + nsteps - 1) // nsteps))


S_UNIQ = []


def alloc_wbf(es, nc, name, kcs):
    out = []
    for si, KC in enumerate(kcs):
        t = es.enter_context(nc.sbuf_tensor(f"{name}_wbf{si}_L{len(S_UNIQ)}", [128, 2, KC, 512], BF16))
        S_UNIQ.append(0)
        out.append((t, [Res(f"{name}_wbf{si}_0"), Res(f"{name}_wbf{si}_1")]))
    return out


def load_act(S, cfg, act, ares, src, KC, tokoff, stage, is_f32, NT):
    if not is_f32:
        g = ("act", id(act), tokoff, S.barrier_idx, len(S.ops))
        n = 0
        for k0 in range(0, KC, 8):
            nk = min(8, KC - k0)
            s = src[k0 * 128:(k0 + nk) * 128, tokoff:tokoff + NT].rearrange("(kc p) t -> p kc t", p=128)
            S.add("sp", lambda E, o=act[:, k0:k0 + nk, :], i=s: E.dma_start(out=o, in_=i),
                  writes=(list(ares) if n == 0 else []), dma=f"act{id(act) % 97}", grp=g)
            n += 1
        return
    per = max(1, 2048 // NT)
    n = 0
    for k0 in range(0, KC, per):
        nk = min(per, KC - k0)
        st_ap, st_res, slot = stage.next()
        sv = st_ap[:, :, :].rearrange("p a b -> p (a b)")[:, 0:nk * NT].rearrange("p (k t) -> p k t", t=NT)
        s = src[k0 * 128:(k0 + nk) * 128, tokoff:tokoff + NT].rearrange("(kc p) t -> p kc t", p=128)
        S.add("sp", lambda E, o=sv, i=s: E.dma_start(out=o, in_=i), writes=[st_res], dma=f"stage{slot}")
        if n % 2 == 0:
            S.add("dve", lambda E, o=act[:, k0:k0 + nk, :], i=sv: E.tensor_copy(out=o, in_=i),
                  reads=[st_res], writes=[ares[n % len(ares)]])
        else:
            S.add("act", lambda E, o=act[:, k0:k0 + nk, :], i=sv: E.copy(out=o, in_=i),
                  reads=[st_res], writes=[ares[n % len(ares)]])
        n += 1


P1_SEGS = [
    (0, 2048, "qT", 0, "Identity", BF16),
    (2048, 4096, "kvT", 0, "Identity", BF16),
    (4096, 6144, "kvT", 2048, "Identity", BF16),
    (6144, 8192, "zaT", 0, "Silu", BF16),
    (8192, 10240, "valT", 0, "Identity", F32),
    (10240, 12288, "sgT", 0, "Sigmoid", F32),
    (12288, 14336, "zcT", 0, "Silu", BF16),
    (14336, 18432, "gaT", 0, "Sigmoid", BF16),
    (18432, 22528, "gcT", 0, "Sigmoid", BF16),
]
V_BIN = 0
V_WCONV = DIN // 128
V_BCONV = V_WCONV + 16 * CW
V_CLNG = V_BCONV + 16
V_CLNB = V_CLNG + 16
V_BPG = V_CLNB + 16
V_LNG = V_BPG + 32
V_LNB = V_LNG + 32
NV = V_LNB + 32


_UNIQ = [0]


def uniq(name):
    _UNIQ[0] += 1
    return f"{name}_u{_UNIQ[0]}"


def sb(es, nc, name, shape, dtype):
    return es.enter_context(nc.sbuf_tensor(uniq(name), list(shape), dtype))


def ps(es, nc, name, shape, dtype):
    return es.enter_context(nc.psum_tensor(uniq(name), list(shape), dtype))


def dma(S, out, in_, key, reads=(), writes=(), grp=None, q="sp"):
    return S.add(q, lambda E, o=out, i=in_: E.dma_start(out=o, in_=i), reads=reads, writes=writes,
                 dma=key, grp=grp)


def phase_p1(S, nc, cfg, dt, L, x_src):
    TOK, NT = cfg.TOK, cfg.NT
    with contextlib.ExitStack() as es:
        vec = sb(es, nc, "p1_vec", [128, DIN // 128], F32)
        vres = Res("vec")
        dma(S, vec[:, :], dt["vecs"][L, :, 0:DIN // 128], "vec", writes=[vres])
        stage = Ring([sb(es, nc, f"p1_st{i}", [128, 4, 512], F32) for i in range(4)], "st")
        act = sb(es, nc, "p1_act", [128, 32, NT], BF16)
        ares = [Res(f"p1_act{i}") for i in range(8)]
        wbf = alloc_wbf(es, nc, "p1", [32])
        obf = Ring([sb(es, nc, f"p1_obf{i}", [128, 512], BF16) for i in range(4)], "obf")
        o32 = Ring([sb(es, nc, f"p1_o32{i}", [128, 512], F32) for i in range(4)], "o32")
        banks = Ring([ps(es, nc, f"p1_ps{i}", [128, 512], F32) for i in range(8)], "ps")

        def evac_p1(bl, col0, tokoff):
            bank, bres = bl[0]
            seg = [s for s in P1_SEGS if s[0] <= col0 < s[1]][0]
            row0 = seg[3] + col0 - seg[0]
            if seg[2] == "kvT":
                dst = dt["kvT"][row0 // cfg.RP]
                row0 = row0 % cfg.RP
            else:
                dst = dt[seg[2]]
            ring = obf if seg[5] == BF16 else o32
            o_ap, o_res, slot = ring.next()
            g = col0 // 128
            func = getattr(AF, seg[4])
            S.add("act", lambda E, o=o_ap[:, :], i=bank[:, :], f=func, b=vec[:, g:g + 1]:
                  E.activation(out=o, in_=i, func=f, bias=b, scale=1.0),
                  reads=[bres, vres], writes=[o_res])
            dma(S, dst[row0:row0 + 128, tokoff:tokoff + 512], o_ap[:, :], f"{ring.name}{slot}", reads=[o_res], q="act")

        for nt in range(TOK // NT):
            load_act(S, cfg, act, ares, x_src, 32, nt * NT, stage, True, NT)
            gemm_phase(S, nc, es, cfg, f"p1_{nt}", [(act, ares, dt["w_in"][L], 32)], DIN,
                       evac_p1, nt * NT, stage, NT, banks, wbf, None, ("dve", "pool"))
        S.barrier()


def phase_tail(S, nc, cfg, dt):
    TOK = cfg.TOK
    with contextlib.ExitStack() as es:
        a = sb(es, nc, "tl_a", [128, 16, 32], F32)
        b = sb(es, nc, "tl_b", [128, 16, 32], F32)
        ra, rb = Res("tl_a"), Res("tl_b")
        if True:
            dma(S, a[:, :, :], dt["valT"][:, TOK - 32:TOK].rearrange("(c p) t -> p c t", p=128), "tla", writes=[ra])
            dma(S, b[:, :, :], dt["sgT"][:, TOK - 32:TOK].rearrange("(c p) t -> p c t", p=128), "tlb", writes=[rb])
            S.add("dve", lambda E: E.tensor_tensor(out=a[:, :, :], in0=a[:, :, :], in1=b[:, :, :], op=ALU.mult),
                  reads=[ra, rb], writes=[ra])
            dma(S, dt["tail"].rearrange("(c p) t -> p c t", p=128), a[:, :, :], "tlo", reads=[ra])
            S.barrier()


def phase_exchange(S, nc, cfg, dt):
    groups = [list(range(g * GROUP, (g + 1) * GROUP)) for g in range(NCORES // GROUP)]
    r_tail = Res("tail_all")
    S.add("pool", lambda E: E.collective_compute("AllGather", ALU.bypass, replica_groups=groups,
                                                 ins=[dt["tail"].opt()], outs=[dt["tail_all"].opt()]),
          dma="cc", writes=[r_tail])
    for i in range(cfg.NPC):
        S.add("pool", lambda E, i=i: E.collective_compute("AllGather", ALU.bypass, replica_groups=groups,
                                                          ins=[dt["kvT"][i].opt()], outs=[dt["kvT_all"][i].opt()]),
              dma="cc")
    return r_tail


def phase_attn(S, nc, cfg, dt):
    TOK, T, NB, NQ, NQT = cfg.TOK, cfg.T, cfg.NB, cfg.NQ, cfg.NQT
    NKC = T // 128
    NKO = TOK // 128
    with contextlib.ExitStack() as es:
        NCB = 256 + NB * 128 + 512
        cst = sb(es, nc, "at_cst", [128, NCB], BF16)
        pm = sb(es, nc, "at_pm", [128, NQ * NB], F32)
        rc = Res("cst")
        dma(S, cst[:, :], dt["cst_bf"][:, :], "cst", writes=[rc])
        dma(S, pm[:, :], dt["pm"][:, :], "pm", writes=[rc])
        ident = cst[:, 0:128]
        ones = cst[:, 128:256]
        sel = cst[:, 256:256 + NB * 128].rearrange("p (n m) -> p n m", m=128)
        tri = cst[:, 256 + NB * 128:256 + NB * 128 + 512].rearrange("p (c q) -> p c q", q=256)

        KT = [sb(es, nc, f"at_KT{i}", [128, T], BF16) for i in range(2)]
        VT = [sb(es, nc, f"at_VT{i}", [128, T], BF16) for i in range(2)]
        QT = [sb(es, nc, f"at_QT{i}", [128, TOK], BF16) for i in range(2)]
        KTo = [sb(es, nc, f"at_KTo{i}", [128, TOK], BF16) for i in range(2)]
        VTo = [sb(es, nc, f"at_VTo{i}", [128, TOK], BF16) for i in range(2)]
        Vsb = [sb(es, nc, f"at_V{i}", [128, NKC, 129], BF16) for i in range(2)]
        Vo = [sb(es, nc, f"at_Vo{i}", [128, NKO, 129], BF16) for i in range(2)]
        mbT = [sb(es, nc, f"at_mbT{i}", [128, TOK], BF16) for i in range(2)]
        kmb = [sb(es, nc, f"at_kmb{i}", [128, NB], BF16) for i in range(2)]
        r_ld = [Res(f"ld{i}") for i in range(2)]
        r_V = [Res(f"V{i}") for i in range(2)]
        r_Vo = [Res(f"Vo{i}") for i in range(2)]
        r_mbT = [Res(f"mbT{i}") for i in range(2)]
        r_kmb = [Res(f"kmb{i}") for i in range(2)]
        km32 = sb(es, nc, "at_km32", [128, NB], F32)
        r_km32 = Res("km32")
        gm = sb(es, nc, "at_gm", [128, NQ * NB], F32)
        r_gm = Res("gm")
        top8 = sb(es, nc, "at_top8", [128, NQ, 8], F32)
        r_top8 = Res("top8")
        thr = sb(es, nc, "at_thr", [128, NQ], F32)
        r_thr = Res("thr")
        mbb = sb(es, nc, "at_mbb", [128, NQ, 128], BF16)
        r_mbb = Res("mbb")
        S.add("pool", lambda E: E.memset(mbb[:, :, :], 0.0), writes=[r_mbb])
        for i in range(2):
            S.add("pool", lambda E, i=i: E.memset(Vsb[i][:, :, 128:129], 1.0), writes=[r_V[i]])
            S.add("pool", lambda E, i=i: E.memset(Vo[i][:, :, 128:129], 1.0), writes=[r_Vo[i]])
        identf = sb(es, nc, "at_identf", [128, 128], F32)
        r_idf = Res("identf")
        S.add("dve", lambda E: E.tensor_copy(out=identf[:, :], in_=ident), reads=[rc], writes=[r_idf])
        rdn = sb(es, nc, "at_rdn", [128, 4], F32)
        r_rdn = Res("rdn")
        attm = sb(es, nc, "at_attm", [128, 4, 128], F32)
        r_attm = Res("attm")
        Pt = Ring([sb(es, nc, f"at_Pt{i}", [128, 512], BF16) for i in range(5)], "Pt")
        zat = Ring([sb(es, nc, f"at_za{i}", [128, 512], BF16) for i in range(2)], "za")
        rd = sb(es, nc, "at_rd", [128, 512], F32)
        r_rd = Res("rd")
        at32 = sb(es, nc, "at_at32", [128, 512], F32)
        r_at32 = Res("at32")
        agb = Ring([sb(es, nc, f"at_agb{i}", [128, 512], BF16) for i in range(2)], "agb")
        Sb = Ring([ps(es, nc, f"at_S{i}", [128, 512], F32) for i in range(3)], "S")
        O2 = [ps(es, nc, "at_O2a", [128, 512], F32), ps(es, nc, "at_O2b", [128, 512], F32)]
        r_O2 = [Res("O2a"), Res("O2b")]
        gps = ps(es, nc, "at_gps", [128, 512], F32)
        r_gps = Res("gps")
        mps = ps(es, nc, "at_mps", [128, 1024], BF16)
        r_mps = Res("mps")
        vtp = Ring([ps(es, nc, f"at_vtp{i}", [128, 1024], BF16) for i in range(1)], "vtp")

        def stageA(h):
            s = h % 2
            RP = cfg.RP

            def kv_all(row):
                return dt["kvT_all"][row // RP].rearrange("(r f) t -> f r t", r=GROUP)[row % RP:row % RP + 128, :, :]

            def kv_own(row):
                return dt["kvT"][row // RP][row % RP:row % RP + 128, :]
            g = ("ld", h)
            dma(S, KT[s][:, :].rearrange("p (r t) -> p r t", r=GROUP), kv_all(h * 128),
                f"ld{s}", writes=[r_ld[s]], grp=g)
            dma(S, VT[s][:, :].rearrange("p (r t) -> p r t", r=GROUP), kv_all(DA + h * 128), f"ld{s}", grp=g)
            dma(S, QT[s][:, :], dt["qT"][h * 128:(h + 1) * 128, :], f"ld{s}", grp=g)
            dma(S, KTo[s][:, :], kv_own(h * 128), f"ld{s}", grp=g)
            dma(S, VTo[s][:, :], kv_own(DA + h * 128), f"ld{s}", grp=g)

        def transposes(src, n, dst, rdst, rsrc):
            k = 0
            for c0 in range(0, n, 8):
                nn = min(8, n - c0)
                vp, rvp, _ = vtp.next()

                def tr(E, vp=vp, c0=c0, nn=nn):
                    ins = None
                    for c in range(nn):
                        ins = E.transpose(out=vp[:, c * 128:(c + 1) * 128],
                                          in_=src[:, (c0 + c) * 128:(c0 + c + 1) * 128], identity=ident)
                    return ins
                S.add("pe", tr, reads=[rsrc, rc], writes=[rvp])
                eng = "act" if k % 2 == 0 else "dve"
                if eng == "act":
                    S.add("act", lambda E, o=dst[:, c0:c0 + nn, 0:128], i=vp[:, 0:nn * 128].rearrange("p (c d) -> p c d", d=128):
                          E.copy(out=o, in_=i), reads=[rvp], writes=[rdst])
                else:
                    S.add("dve", lambda E, o=dst[:, c0:c0 + nn, 0:128], i=vp[:, 0:nn * 128].rearrange("p (c d) -> p c d", d=128):
                          E.tensor_copy(out=o, in_=i), reads=[rvp], writes=[rdst])
                k += 1

        def stageB(h):
            s = h % 2
            transposes(VT[s], NKC, Vsb[s], r_V[s], r_ld[s])
            transposes(VTo[s], NKO, Vo[s], r_Vo[s], r_ld[s])
            S.add("dve", lambda E: E.tensor_reduce(out=km32[:, :], in_=KT[s][:, :].rearrange("p (n k) -> p n k", k=BLK),
                                                   axis=AX.X, op=ALU.add), reads=[r_ld[s]], writes=[r_km32])
            S.add("dve", lambda E: E.tensor_scalar(out=kmb[s][:, :], in0=km32[:, :], scalar1=1.0 / BLK, scalar2=None,
                                                   op0=ALU.mult), reads=[r_km32], writes=[r_kmb[s]])

        def stageC(h):
            s = h % 2

            def gmm(E):
                ins = None
                for j in range(NQ):
                    ins = E.matmul(gps[:, j * NB:(j + 1) * NB], lhsT=QT[s][:, j * 128:(j + 1) * 128],
                                   rhs=kmb[s][:, :], start=True, stop=True)
                return ins
            S.add("pe", gmm, reads=[r_ld[s], r_kmb[s]], writes=[r_gps])
            S.add("dve", lambda E: E.tensor_tensor(out=gm[:, :], in0=gps[:, 0:NQ * NB], in1=pm[:, :], op=ALU.add),
                  reads=[r_gps, rc], writes=[r_gm])
            for j in range(NQ):
                S.add("dve", lambda E, j=j: E.max(out=top8[:, j, :], in_=gm[:, j * NB:(j + 1) * NB]),
                      reads=[r_gm], writes=[r_top8])
            S.add("dve", lambda E: E.tensor_scalar(out=thr[:, :], in0=top8[:, :, 2], scalar1=-1e29, scalar2=None,
                                                   op0=ALU.max), reads=[r_top8], writes=[r_thr])
            for j in range(NQ):
                S.add("dve", lambda E, j=j: E.tensor_scalar(out=mbb[:, j, 0:NB], in0=gm[:, j * NB:(j + 1) * NB],
                                                            scalar1=thr[:, j:j + 1], scalar2=1.0,
                                                            op0=ALU.is_ge, op1=ALU.subtract),
                      reads=[r_gm, r_thr], writes=[r_mbb])

        def stageD(h):
            s = h % 2
            for j0 in range(0, NQ, 8):
                nn = min(8, NQ - j0)

                def tr(E, j0=j0, nn=nn):
                    ins = None
                    for j in range(nn):
                        ins = E.transpose(out=mps[:, j * 128:(j + 1) * 128], in_=mbb[:, j0 + j, :], identity=ident)
                    return ins
                S.add("pe", tr, reads=[r_mbb, rc], writes=[r_mps])
                S.add("act", lambda E, j0=j0, nn=nn: E.copy(out=mbT[s][:, j0 * 128:(j0 + nn) * 128], in_=mps[:, 0:nn * 128]),
                      reads=[r_mps], writes=[r_mbT[s]])

        def main(h, hooks):
            s = h % 2
            for t in range(NQT):
                za_ap, za_res, zslot = zat.next()
                dma(S, za_ap[:, :], dt["zaT"][h * 128:(h + 1) * 128, t * 512:(t + 1) * 512], f"za{zslot}", writes=[za_res])
                q512 = QT[s][:, t * 512:(t + 1) * 512]
                pendq = []
                chunks = []
                for n in range(min(NB, (GROUP - 1) * cfg.NBL + 2 * t + 1)):
                    for c in range(2):
                        chunks.append(("past", n, c))
                for half in range(2):
                    for c in range(2):
                        chunks.append(("own", half, c))
                nch = len(chunks)
                first = [True]

                def pv(kind, a, c, p_ap, p_res, last):
                    st = first[0]
                    first[0] = False
                    if kind == "past":
                        v1 = Vsb[s][:, a * 2 + c, :]
                        rv = r_V[s]

                        def f(E):
                            ins = None
                            for qs in range(4):
                                ins = E.matmul(O2[qs // 2][:, (qs % 2) * 129:(qs % 2) * 129 + 129],
                                               lhsT=p_ap[:, qs * 128:(qs + 1) * 128], rhs=v1,
                                               start=(st and qs % 2 == 0), stop=last)
                            return ins
                        S.add("pe", f, reads=[rv, p_res], writes=[r_O2[0], r_O2[1]])
                    else:
                        lb = 2 * t + a
                        v1 = Vo[s][:, lb * 2 + c, :]
                        rv = r_Vo[s]

                        def f(E):
                            ins = None
                            for qq in range(2):
                                ins = E.matmul(O2[a][:, qq * 129:qq * 129 + 129],
                                               lhsT=p_ap[:, qq * 128:(qq + 1) * 128], rhs=v1, start=False, stop=last)
                            return ins
                        S.add("pe", f, reads=[rv, p_res], writes=[r_O2[a]])

                for ci, (kind, a, c) in enumerate(chunks):
                    s_ap, s_res, _ = Sb.next()
                    p_ap, p_res, _ = Pt.next()
                    if kind == "past":
                        def f(E, s_ap=s_ap, a=a, c=c, q512=q512, t=t):
                            E.matmul(s_ap[:, :], lhsT=KT[s][:, a * 256 + c * 128:a * 256 + (c + 1) * 128], rhs=q512,
                                     start=True, stop=False)
                            return E.matmul(s_ap[:, :], lhsT=sel[:, a, :], rhs=mbT[s][:, t * 512:(t + 1) * 512],
                                            start=False, stop=True)
                        S.add("pe", f, reads=[r_ld[s], r_mbT[s], rc], writes=[s_res])
                        S.add("act", lambda E, o=p_ap[:, :], i=s_ap[:, :]: E.activation(out=o, in_=i, func=AF.Exp, scale=SCALE),
                              reads=[s_res], writes=[p_res])
                    else:
                        lb = 2 * t + a

                        def f(E, s_ap=s_ap, a=a, c=c, lb=lb, t=t):
                            E.matmul(s_ap[:, 0:256], lhsT=KTo[s][:, lb * 256 + c * 128:lb * 256 + (c + 1) * 128],
                                     rhs=QT[s][:, t * 512 + a * 256:t * 512 + (a + 1) * 256], start=True, stop=False)
                            return E.matmul(s_ap[:, 0:256], lhsT=ident, rhs=tri[:, c, :], start=False, stop=True)
                        S.add("pe", f, reads=[r_ld[s], rc], writes=[s_res])
                        S.add("act", lambda E, o=p_ap[:, 0:256], i=s_ap[:, 0:256]: E.activation(out=o, in_=i, func=AF.Exp, scale=SCALE),
                              reads=[s_res], writes=[p_res])
                    pendq.append((kind, a, c, p_ap, p_res))
                    if len(pendq) > 2:
                        pv(*pendq.pop(0), False)
                while pendq:
                    item = pendq.pop(0)
                    pv(*item, len(pendq) == 0)
                for qs in range(4):
                    bk, off = O2[qs // 2], (qs % 2) * 129
                    S.add("dve", lambda E, bk=bk, off=off, qs=qs: E.reciprocal(out=rdn[:, qs:qs + 1], in_=bk[:, off + 128:off + 129]),
                          reads=[r_O2[qs // 2]], writes=[r_rdn])
                    S.add("dve", lambda E, bk=bk, off=off, qs=qs: E.tensor_scalar(out=attm[:, qs, :], in0=bk[:, off:off + 128], scalar1=rdn[:, qs:qs + 1], scalar2=None, op0=ALU.mult),
                          reads=[r_O2[qs // 2], r_rdn], writes=[r_attm])
                tp_ap, tp_res, _ = Sb.next()

                def trb(E, tp_ap=tp_ap):
                    ins = None
                    for qs in range(4):
                        ins = E.transpose(out=tp_ap[:, qs * 128:(qs + 1) * 128], in_=attm[:, qs, :], identity=identf[:, :])
                    return ins
                S.add("pe", trb, reads=[r_attm, r_idf], writes=[tp_res])
                g_ap, g_res, gslot = agb.next()
                S.add("dve", lambda E, o=g_ap[:, :], z=za_ap[:, :], tp=tp_ap: E.tensor_tensor(out=o, in0=tp[:, :], in1=z, op=ALU.mult),
                      reads=[tp_res, za_res], writes=[g_res])
                dma(S, dt["agT"][h * 128:(h + 1) * 128, t * 512:(t + 1) * 512], g_ap[:, :], f"agb{gslot}", reads=[g_res], q="act")
                if hooks:
                    hooks.pop(0)()
            while hooks:
                hooks.pop(0)()

        for hh in range(min(1, NH)):
            stageA(0), stageB(0), stageC(0), stageD(0)
        for h in range(NH):
            hooks = []
            if h + 1 < NH:
                hooks = [lambda h=h: stageA(h + 1), lambda h=h: stageB(h + 1), lambda h=h: stageC(h + 1),
                         lambda h=h: stageD(h + 1)]
                if NQT >= 4:
                    hooks[0]()
                    hooks = hooks[1:]
            main(h, hooks)
        S.barrier()


def phase_conv(S, nc, cfg, dt, L, r_tail=None):
    TOK = cfg.TOK
    HT = min(1024, TOK)
    NH2 = TOK // HT
    NTT = HT // 512
    with contextlib.ExitStack() as es:
        vec = sb(es, nc, "cv_vec", [128, NV - V_WCONV], F32)
        rvec = Res("cv_vec")
        dma(S, vec[:, :], dt["vecs"][L, :, V_WCONV:NV], "vec", writes=[rvec])
        bc = lambda ch: vec[:, V_BCONV - V_WCONV + ch:V_BCONV - V_WCONV + ch + 1]
        lg = lambda ch: vec[:, V_CLNG - V_WCONV + ch:V_CLNG - V_WCONV + ch + 1]
        lb = lambda ch: vec[:, V_CLNB - V_WCONV + ch:V_CLNB - V_WCONV + ch + 1]
        hw = sb(es, nc, "cv_hw", [128, GROUP], F32)
        rhw = Res("hw")
        dma(S, hw[:, :], dt["halo_w"][:, :], "hw", writes=[rhw])
        cst = sb(es, nc, "cv_cst", [128, 256], BF16)
        rcst = Res("cst")
        dma(S, cst[:, :], dt["cst_bf"][:, 0:256], "cst", writes=[rcst])
        ident, ones = cst[:, 0:128], cst[:, 128:256]
        vl = Ring([sb(es, nc, f"cv_vl{i}", [128, 32 + HT], F32) for i in range(2)], "vl")
        sg = Ring([sb(es, nc, f"cv_sg{i}", [128, 32 + HT], F32) for i in range(2)], "sg")
        tl = Ring([sb(es, nc, f"cv_tl{i}", [128, GROUP, 32], F32) for i in range(3)], "tl")
        hl = Ring([sb(es, nc, f"cv_hl{i}", [128, 32], F32) for i in range(2)], "hl")
        ub = Ring([sb(es, nc, f"cv_ub{i}", [128, 32 + HT], BF16) for i in range(2)], "ub")
        dg = Ring([sb(es, nc, f"cv_dg{i}", [128, CW, 128], BF16) for i in range(2)], "dg")
        dgB = [Res("dgB0"), Res("dgB1")]
        c32 = Ring([sb(es, nc, f"cv_c{i}", [128, 512], F32) for i in range(4)], "c")
        cb = Ring([sb(es, nc, f"cv_cb{i}", [128, 512], BF16) for i in range(6)], "cb")
        cq = Ring([sb(es, nc, f"cv_cq{i}", [128, 512], BF16) for i in range(6)], "cq")
        cps = Ring([ps(es, nc, f"cv_ps{i}", [128, 512], F32) for i in range(4)], "cps")
        st_sum = [ps(es, nc, f"cv_ss{i}", [128, 512], F32) for i in range(NTT)]
        st_sq = [ps(es, nc, f"cv_sq{i}", [128, 512], F32) for i in range(NTT)]
        r_st = Res("stats")
        mus = [sb(es, nc, f"cv_mu{i}", [128, HT], F32) for i in range(2)]
        rss = [sb(es, nc, f"cv_rs{i}", [128, HT], F32) for i in range(2)]
        r_mus = [Res("mu0"), Res("mu1")]
        r_rss = [Res("rs0"), Res("rs1")]
        lnt = sb(es, nc, "cv_lnt", [128, HT], F32)
        r_lnt = Res("cv_lnt")
        side = []
        ct = Ring([sb(es, nc, f"cv_ct{i}", [128, 512], F32) for i in range(3)], "ct")
        zc = Ring([sb(es, nc, f"cv_zc{i}", [128, 512], BF16) for i in range(3)], "zc")
        og = Ring([sb(es, nc, f"cv_og{i}", [128, 512], BF16) for i in range(3)], "og")
        tail_all = dt["tail_all"].rearrange("(r c p) t -> c p r t", r=GROUP, p=128)
        for half in range(NH2):
            t0 = half * HT
            loaded = {}

            def loads(ch, half=half, t0=t0):
                v_ap, v_res, vs = vl.next()
                s_ap, s_res, ss = sg.next()
                rows = slice(ch * 128, (ch + 1) * 128)
                tinfo = None
                if half == 0:
                    dma(S, v_ap[:, 32:32 + HT], dt["valT"][rows, 0:HT], f"vl{vs}", writes=[v_res])
                    dma(S, s_ap[:, 32:32 + HT], dt["sgT"][rows, 0:HT], f"sg{ss}", writes=[s_res])
                    t_ap, t_res, ts = tl.next()
                    dma(S, t_ap[:, :, :], tail_all[ch], f"tl{ts}", reads=([r_tail] if r_tail is not None else []),
                        writes=[t_res])
                    tinfo = (t_ap, t_res)
                else:
                    dma(S, v_ap[:, :], dt["valT"][rows, t0 - 32:t0 + HT], f"vl{vs}", writes=[v_res])
                    dma(S, s_ap[:, :], dt["sgT"][rows, t0 - 32:t0 + HT], f"sg{ss}", writes=[s_res])
                loaded[ch] = (v_ap, v_res, s_ap, s_res, tinfo)

            pend = []
            loads(0)
            for ch in range(16):
                if ch + 1 < 16:
                    loads(ch + 1)
                v_ap, v_res, s_ap, s_res, tinfo = loaded.pop(ch)
                u_ap, u_res, _ = ub.next()
                d_ap, d_res, dslot = dg.next()
                rows = slice(ch * 128, (ch + 1) * 128)
                if half == 0:
                    t_ap, t_res = tinfo
                    h_ap, h_res, _ = hl.next()
                    S.add("dve", lambda E, h=h_ap, t=t_ap: E.tensor_scalar(out=h[:, :], in0=t[:, 0, :], scalar1=hw[:, 0:1], scalar2=None, op0=ALU.mult),
                          reads=[t_res, rhw], writes=[h_res])
                    for r in range(1, GROUP):
                        S.add("dve", lambda E, h=h_ap, t=t_ap, r=r: E.scalar_tensor_tensor(out=h[:, :], in0=t[:, r, :], scalar=hw[:, r:r + 1], in1=h[:, :], op0=ALU.mult, op1=ALU.add),
                              reads=[t_res, rhw, h_res], writes=[h_res])
                    S.add("dve", lambda E, h=h_ap, u=u_ap: E.tensor_copy(out=u[:, 0:32], in_=h[:, :]), reads=[h_res], writes=[u_res])
                    S.add("dve", lambda E, u=u_ap, v=v_ap, s_=s_ap: E.tensor_tensor(out=u[:, 32:32 + HT], in0=v[:, 32:32 + HT], in1=s_[:, 32:32 + HT], op=ALU.mult),
                          reads=[v_res, s_res], writes=[u_res])
                else:
                    S.add("dve", lambda E, u=u_ap, v=v_ap, s_=s_ap: E.tensor_tensor(out=u[:, :], in0=v[:, :], in1=s_[:, :], op=ALU.mult),
                          reads=[v_res, s_res], writes=[u_res])
                for j in range(CW):
                    S.add("dve", lambda E, d=d_ap, ch=ch, j=j: E.tensor_scalar(out=d[:, j, :], in0=ident, scalar1=vec[:, ch * CW + j:ch * CW + j + 1], scalar2=None, op0=ALU.mult),
                          reads=[rvec, rcst], writes=[d_res])
                newp = []
                for tt in range(NTT):
                    p_ap, p_res, _ = cps.next()

                    def taps(E, p=p_ap, d=d_ap, u=u_ap, tt=tt):
                        ins = None
                        for j in range(CW):
                            ins = E.matmul(p[:, :], lhsT=d[:, j, :], rhs=u[:, 2 + j + tt * 512:2 + j + tt * 512 + 512],
                                           start=(j == 0), stop=(j == CW - 1))
                        return ins
                    S.add("pe", taps, reads=[d_res, u_res], writes=[p_res])
                    c_ap, c_res, cs = c32.next()
                    b_ap, b_res, _ = cb.next()
                    q_ap, q_res, _ = cq.next()
                    S.add("act", lambda E, c=c_ap, p=p_ap, ch=ch: E.activation(out=c[:, :], in_=p[:, :], func=AF.Identity, bias=bc(ch), scale=1.0),
                          reads=[p_res, rvec], writes=[c_res])
                    dma(S, dt["cT"][rows, t0 + tt * 512:t0 + (tt + 1) * 512], c_ap[:, :], f"c{cs}", reads=[c_res], q="act")
                    S.add("pool", lambda E, o=b_ap, c=c_ap: E.tensor_copy(out=o[:, :], in_=c[:, :]), reads=[c_res], writes=[b_res])
                    S.add("act", lambda E, o=q_ap, c=c_ap: E.activation(out=o[:, :], in_=c[:, :], func=AF.Square), reads=[c_res], writes=[q_res])

                    def stat(E, b=b_ap, q=q_ap, ch=ch, tt=tt):
                        E.matmul(st_sum[tt][:, :], lhsT=ones, rhs=b[:, :], start=(ch == 0), stop=(ch == 15))
                        return E.matmul(st_sq[tt][:, :], lhsT=ones, rhs=q[:, :], start=(ch == 0), stop=(ch == 15))
                    newp.append(lambda stat=stat, b_res=b_res, q_res=q_res: S.add("pe", stat, reads=[b_res, q_res, rcst], writes=[r_st]))
                while pend:
                    pend.pop(0)()
                pend = newp
                for _ in range(2 * NTT):
                    if side:
                        side.pop(0)()
            while pend:
                pend.pop(0)()
            while side:
                side.pop(0)()
            mu, rs, r_mu, r_rs = mus[half % 2], rss[half % 2], r_mus[half % 2], r_rss[half % 2]
            ln_stats(S, es, nc, f"cv{half}", NTT, st_sum, st_sq, r_st, mu, rs, r_mu, r_rs, 1.0 / DA, lnt, r_lnt)

            def norm_unit(ch, tt, t0=t0, mu=mu, rs=rs, r_mu=r_mu, r_rs=r_rs):
                c_ap, c_res, cs = ct.next()
                z_ap, z_res, zs = zc.next()
                o_ap, o_res, os_ = og.next()
                tsl = slice(t0 + tt * 512, t0 + (tt + 1) * 512)
                msl = slice(tt * 512, (tt + 1) * 512)
                dma(S, c_ap[:, :], dt["cT"][ch * 128:(ch + 1) * 128, tsl], f"ct{cs}", writes=[c_res])
                dma(S, z_ap[:, :], dt["zcT"][ch * 128:(ch + 1) * 128, tsl], f"zc{zs}", writes=[z_res])
                S.add("dve", lambda E, c=c_ap, msl=msl: E.tensor_tensor(out=c[:, :], in0=c[:, :], in1=mu[:, msl], op=ALU.subtract),
                      reads=[c_res, r_mu], writes=[c_res])
                S.add("dve", lambda E, c=c_ap, msl=msl: E.tensor_tensor(out=c[:, :], in0=c[:, :], in1=rs[:, msl], op=ALU.mult),
                      reads=[c_res, r_rs], writes=[c_res])
                S.add("act", lambda E, c=c_ap, ch=ch: E.activation(out=c[:, :], in_=c[:, :], func=AF.Silu, scale=lg(ch), bias=lb(ch)),
                      reads=[c_res, rvec], writes=[c_res])
                S.add("pool", lambda E, c=c_ap, z=z_ap, o=o_ap: E.tensor_tensor(out=o[:, :], in0=c[:, :], in1=z[:, :], op=ALU.mult),
                      reads=[c_res, z_res], writes=[o_res])
                dma(S, dt["cgT"][ch * 128:(ch + 1) * 128, tsl], o_ap[:, :], f"og{os_}", reads=[o_res], q="act")
            side = [lambda ch=ch, tt=tt, nu=norm_unit: nu(ch, tt) for ch in range(16) for tt in range(NTT)]
        while side:
            side.pop(0)()
        S.barrier()


def ln_stats(S, es, nc, name, NTT, st_sum, st_sq, r_st, mu, rs, r_mu, r_rs, inv_n, tmp=None, r_tmp=None):
    for tt in range(NTT):
        tsl = slice(tt * 512, (tt + 1) * 512)
        S.add("act", lambda E, tt=tt, tsl=tsl: E.activation(out=mu[:, tsl], in_=st_sum[tt][:, :], func=AF.Identity, scale=inv_n),
              reads=[r_st], writes=[r_mu])
        S.add("act", lambda E, tt=tt, tsl=tsl: E.activation(out=rs[:, tsl], in_=st_sq[tt][:, :], func=AF.Identity, scale=inv_n),
              reads=[r_st], writes=[r_rs])
    if tmp is None:
        tmp = sb(es, nc, f"{name}_lntmp", [128, NTT * 512], F32)
        r_tmp = Res("lntmp")
    S.add("dve", lambda E: E.tensor_tensor(out=tmp[:, :], in0=mu[:, :], in1=mu[:, :], op=ALU.mult), reads=[r_mu], writes=[r_tmp])
    S.add("dve", lambda E: E.tensor_tensor(out=rs[:, :], in0=rs[:, :], in1=tmp[:, :], op=ALU.subtract), reads=[r_rs, r_tmp], writes=[r_rs])
    S.add("dve", lambda E: E.tensor_scalar(out=rs[:, :], in0=rs[:, :], scalar1=0.0, scalar2=LN_EPS, op0=ALU.max, op1=ALU.add),
          reads=[r_rs], writes=[r_rs])
    S.add("act", lambda E: E.activation(out=rs[:, :], in_=rs[:, :], func=AF.Sqrt), reads=[r_rs], writes=[r_rs])
    S.add("dve", lambda E: E.reciprocal(out=rs[:, :], in_=rs[:, :]), reads=[r_rs], writes=[r_rs])


def phase_p6(S, nc, cfg, dt, L):
    TOK, NT = cfg.TOK, cfg.NT
    with contextlib.ExitStack() as es:
        stage = Ring([sb(es, nc, f"p6_st{i}", [128, 4, 512], F32) for i in range(4)], "st")
        a1 = sb(es, nc, "p6_a1", [128, 16, NT], BF16)
        a2 = sb(es, nc, "p6_a2", [128, 16, NT], BF16)
        r1, r2 = [Res("p6_a1")], [Res("p6_a2")]
        wbf = alloc_wbf(es, nc, "p6", [16, 16])
        gat = Ring([sb(es, nc, f"p6_ga{i}", [128, 512], BF16) for i in range(3)], "ga")
        gct = Ring([sb(es, nc, f"p6_gc{i}", [128, 512], BF16) for i in range(3)], "gc")
        t1 = Ring([sb(es, nc, f"p6_t1{i}", [128, 512], F32) for i in range(2)], "t1")
        t2 = Ring([sb(es, nc, f"p6_t2{i}", [128, 512], F32) for i in range(2)], "t2")
        om = Ring([sb(es, nc, f"p6_om{i}", [128, 512], BF16) for i in range(3)], "om")
        banks = Ring([ps(es, nc, f"p6_ps{i}", [128, 512], F32) for i in range(8)], "ps")

        def evac(bl, col0, tokoff):
            (ba, ra), (bc_, rc_) = bl
            ga_ap, ga_res, gs = gat.next()
            gc_ap, gc_res, cs = gct.next()
            dma(S, ga_ap[:, :], dt["gaT"][col0:col0 + 128, tokoff:tokoff + 512], f"ga{gs}", writes=[ga_res])
            dma(S, gc_ap[:, :], dt["gcT"][col0:col0 + 128, tokoff:tokoff + 512], f"gc{cs}", writes=[gc_res])
            t1a, t1r, _ = t1.next()
            t2a, t2r, _ = t2.next()
            o_ap, o_res, os_ = om.next()
            S.add("dve", lambda E: E.tensor_tensor(out=t1a[:, :], in0=ba[:, :], in1=ga_ap[:, :], op=ALU.mult),
                  reads=[ra, ga_res], writes=[t1r])
            S.add("dve", lambda E: E.tensor_tensor(out=t2a[:, :], in0=bc_[:, :], in1=gc_ap[:, :], op=ALU.mult),
                  reads=[rc_, gc_res], writes=[t2r])
            S.add("pool", lambda E: E.tensor_tensor(out=o_ap[:, :], in0=t1a[:, :], in1=t2a[:, :], op=ALU.add),
                  reads=[t1r, t2r], writes=[o_res])
            dma(S, dt["mT"][col0:col0 + 128, tokoff:tokoff + 512], o_ap[:, :], f"om{os_}", reads=[o_res], q="act")

        for nt in range(TOK // NT):
            load_act(S, cfg, a1, r1, dt["agT"], 16, nt * NT, stage, False, NT)
            load_act(S, cfg, a2, r2, dt["cgT"], 16, nt * NT, stage, False, NT)
            gemm_phase(S, nc, es, cfg, f"p6_{nt}", [(a1, r1, dt["w_o_attn"][L], 16), (a2, r2, dt["w_o_conv"][L], 16)],
                       D, evac, nt * NT, stage, NT, banks, wbf)
        S.barrier()


def phase_p7(S, nc, cfg, dt, L, x_src):
    TOK, NT = cfg.TOK, cfg.NT
    with contextlib.ExitStack() as es:
        stage = Ring([sb(es, nc, f"p7_st{i}", [128, 4, 512], F32) for i in range(4)], "st")
        a1 = sb(es, nc, "p7_a1", [128, 32, NT], BF16)
        r1 = [Res("p7_a1")]
        wbf = alloc_wbf(es, nc, "p7", [32])
        xt = Ring([sb(es, nc, f"p7_xt{i}", [128, 512], F32) for i in range(3)], "xt")
        y32 = Ring([sb(es, nc, f"p7_y{i}", [128, 512], F32) for i in range(3)], "y")
        yb = Ring([sb(es, nc, f"p7_yb{i}", [128, 512], BF16) for i in range(3)], "yb")
        banks = Ring([ps(es, nc, f"p7_ps{i}", [128, 512], F32) for i in range(8)], "ps")

        def evac(bl, col0, tokoff):
            (ba, ra), = bl
            x_ap, x_res, xs = xt.next()
            y_ap, y_res, ys = y32.next()
            b_ap, b_res, bs = yb.next()
            dma(S, x_ap[:, :], x_src[col0:col0 + 128, tokoff:tokoff + 512], f"xt{xs}", writes=[x_res])
            S.add("dve", lambda E: E.scalar_tensor_tensor(out=y_ap[:, :], in0=x_ap[:, :], scalar=float(ALPHA), in1=ba[:, :],
                                                          op0=ALU.mult, op1=ALU.add),
                  reads=[x_res, ra], writes=[y_res])
            dma(S, dt["yT"][col0:col0 + 128, tokoff:tokoff + 512], y_ap[:, :], f"y{ys}", reads=[y_res], q="act")
            S.add("pool", lambda E: E.tensor_copy(out=b_ap[:, :], in_=y_ap[:, :]), reads=[y_res], writes=[b_res])
            dma(S, dt["ybT"][col0:col0 + 128, tokoff:tokoff + 512], b_ap[:, :], f"yb{bs}", reads=[b_res], q="act")

        for nt in range(TOK // NT):
            load_act(S, cfg, a1, r1, dt["mT"], 32, nt * NT, stage, False, NT)
            gemm_phase(S, nc, es, cfg, f"p7_{nt}", [(a1, r1, dt["w_out"][L], 32)], D, evac, nt * NT, stage, NT, banks, wbf)
        S.barrier()


def phase_p8(S, nc, cfg, dt, L, dst):
    TOK = cfg.TOK
    NT = min(1024, TOK)
    NTT = NT // 512
    with contextlib.ExitStack() as es:
        vec = sb(es, nc, "p8_vec", [128, 96], F32)
        rvec = Res("p8_vec")
        dma(S, vec[:, :], dt["vecs"][L, :, V_BPG:V_BPG + 96], "vec", writes=[rvec])
        ones = sb(es, nc, "p8_ones", [128, 128], BF16)
        rones = Res("ones")
        dma(S, ones[:, :], dt["cst_bf"][:, 128:256], "ones", writes=[rones])
        stage = Ring([sb(es, nc, f"p8_st{i}", [128, 4, 512], F32) for i in range(3)], "st")
        a1 = sb(es, nc, "p8_a1", [128, 32, NT], BF16)
        a2 = sb(es, nc, "p8_a2", [128, 2, NT], BF16)
        r1, r2 = [Res("p8_a1")], [Res("p8_a2")]
        wbf = alloc_wbf(es, nc, "p8", [32, 2])
        sgt = Ring([sb(es, nc, f"p8_sg{i}", [128, 512], F32) for i in range(2)], "sg")
        yt = Ring([sb(es, nc, f"p8_y{i}", [128, 512], F32) for i in range(2)], "y")
        zt = Ring([sb(es, nc, f"p8_z{i}", [128, 512], F32) for i in range(2)], "z")
        zb = Ring([sb(es, nc, f"p8_zb{i}", [128, 512], BF16) for i in range(4)], "zb")
        zq = Ring([sb(es, nc, f"p8_zq{i}", [128, 512], BF16) for i in range(4)], "zq")
        banks = Ring([ps(es, nc, f"p8_ps{i}", [128, 512], F32) for i in range(8 - 2 * NTT)], "ps")
        st_sum = [ps(es, nc, f"p8_ss{i}", [128, 512], F32) for i in range(NTT)]
        st_sq = [ps(es, nc, f"p8_sq{i}", [128, 512], F32) for i in range(NTT)]
        r_st = Res("stats")
        pend = []

        def evac(bl, col0, tokoff):
            (bg, rg), (bu, ru) = bl
            g = col0 // 128
            tt = (tokoff % NT) // 512
            s_ap, s_res, _ = sgt.next()
            y_ap, y_res, ys = yt.next()
            z_ap, z_res, zs = zt.next()
            b_ap, b_res, _ = zb.next()
            q_ap, q_res, _ = zq.next()
            dma(S, y_ap[:, :], dt["yT"][col0:col0 + 128, tokoff:tokoff + 512], f"y{ys}", writes=[y_res])
            S.add("act", lambda E: E.activation(out=s_ap[:, :], in_=bg[:, :], func=AF.Sigmoid, bias=vec[:, g:g + 1], scale=1.0),
                  reads=[rg, rvec], writes=[s_res])
            S.add("dve", lambda E: E.tensor_tensor(out=s_ap[:, :], in0=bu[:, :], in1=s_ap[:, :], op=ALU.mult),
                  reads=[ru, s_res], writes=[s_res])
            S.add("pool", lambda E: E.tensor_tensor(out=z_ap[:, :], in0=s_ap[:, :], in1=y_ap[:, :], op=ALU.add),
                  reads=[s_res, y_res], writes=[z_res])
            dma(S, dt["zT"][col0:col0 + 128, tokoff:tokoff + 512], z_ap[:, :], f"z{zs}", reads=[z_res], q="act")
            S.add("act", lambda E: E.copy(out=b_ap[:, :], in_=z_ap[:, :]), reads=[z_res], writes=[b_res])
            S.add("act", lambda E: E.activation(out=q_ap[:, :], in_=z_ap[:, :], func=AF.Square), reads=[z_res], writes=[q_res])

            def stat(E):
                E.matmul(st_sum[tt][:, :], lhsT=ones[:, :], rhs=b_ap[:, :], start=(g == 0), stop=(g == 31))
                return E.matmul(st_sq[tt][:, :], lhsT=ones[:, :], rhs=q_ap[:, :], start=(g == 0), stop=(g == 31))
            pend.append(lambda: S.add("pe", stat, reads=[b_res, q_res, rones], writes=[r_st]))
            while len(pend) > 2:
                pend.pop(0)()

        ot = Ring([sb(es, nc, f"p8_o{i}", [128, 512], F32) for i in range(2)], "o")
        zn = Ring([sb(es, nc, f"p8_zn{i}", [128, 512], F32) for i in range(2)], "zn")
        mu = sb(es, nc, "p8_mu", [128, NT], F32)
        rs = sb(es, nc, "p8_rs", [128, NT], F32)
        r_mu, r_rs = Res("mu"), Res("rs")
        lnt = sb(es, nc, "p8_lnt", [128, NT], F32)
        r_lnt = Res("lnt")

        def norm_unit(nt, f, tt):
            z_ap, z_res, zs = zn.next()
            o_ap, o_res, os_ = ot.next()
            tsl = slice(nt * NT + tt * 512, nt * NT + (tt + 1) * 512)
            msl = slice(tt * 512, (tt + 1) * 512)
            dma(S, z_ap[:, :], dt["zT"][f * 128:(f + 1) * 128, tsl], f"zn{zs}", writes=[z_res])
            S.add("dve", lambda E, z=z_ap: E.tensor_tensor(out=z[:, :], in0=z[:, :], in1=mu[:, msl], op=ALU.subtract),
                  reads=[z_res, r_mu], writes=[z_res])
            S.add("dve", lambda E, z=z_ap: E.tensor_tensor(out=z[:, :], in0=z[:, :], in1=rs[:, msl], op=ALU.mult),
                  reads=[z_res, r_rs], writes=[z_res])
            S.add("act", lambda E, z=z_ap, o=o_ap, f=f: E.activation(out=o[:, :], in_=z[:, :], func=AF.Identity,
                                                                   scale=vec[:, 32 + f:33 + f], bias=vec[:, 64 + f:65 + f]),
                  reads=[z_res, rvec], writes=[o_res])
            dma(S, dst[f * 128:(f + 1) * 128, tsl], o_ap[:, :], f"o{os_}", reads=[o_res], q="act")

        side = []
        for nt in range(TOK // NT):
            load_act(S, cfg, a1, r1, dt["ybT"], 32, nt * NT, stage, False, NT)
            load_act(S, cfg, a2, r2, dt["pT"][L], 2, nt * NT, stage, True, NT)
            gemm_phase(S, nc, es, cfg, f"p8_{nt}", [(a1, r1, dt["w_ple_gate"][L], 32), (a2, r2, dt["w_ple_up"][L], 2)],
                       D, evac, nt * NT, stage, NT, banks, wbf, side)
            while side:
                side.pop(0)()
            while pend:
                pend.pop(0)()
            ln_stats(S, es, nc, f"p8_{nt}", NTT, st_sum, st_sq, r_st, mu, rs, r_mu, r_rs, 1.0 / D, lnt, r_lnt)
            side = [lambda nt=nt, f=f, tt=tt: norm_unit(nt, f, tt) for f in range(32) for tt in range(NTT)]
        while side:
            side.pop(0)()
        S.barrier()


def build_program(tok, depth=DEPTH, phases=None, debug_out=()):
    cfg = Cfg(tok)
    TOK, T, NB, NQ = cfg.TOK, cfg.T, cfg.NB, cfg.NQ
    nc = bass.Bass("TRN2", target_bir_lowering=False)
    dt = {}

    def din(name, shape, dtype):
        dt[name] = nc.dram_tensor(name, list(shape), dtype, kind="ExternalInput").ap()

    def dscr(name, shape, dtype, internal=False):
        if name in debug_out and not internal:
            dt[name] = nc.dram_tensor(name, list(shape), dtype, kind="ExternalOutput").ap()
        else:
            dt[name] = nc.dram_tensor(name, list(shape), dtype).ap()

    din("xT", [D, TOK], F32)
    din("pT", [depth, PLE, TOK], F32)
    din("w_in", [depth, D, DIN], F32)
    din("w_o_attn", [depth, DA, D], F32)
    din("w_o_conv", [depth, DA, D], F32)
    din("w_out", [depth, D, D], F32)
    din("w_ple_up", [depth, PLE, D], F32)
    din("w_ple_gate", [depth, D, D], F32)
    din("vecs", [depth, 128, NV], F32)
    din("cst_bf", [128, 256 + NB * 128 + 512], BF16)
    din("pm", [128, NQ * NB], F32)
    din("halo_w", [128, GROUP], F32)
    dt["outT"] = nc.dram_tensor("outT", [D, TOK], F32, kind="ExternalOutput").ap()
    dscr("qT", [DA, TOK], BF16)
    dt["kvT"] = [nc.dram_tensor(f"kvT{i}", [cfg.RP, TOK], BF16).ap() for i in range(cfg.NPC)]
    dt["kvT_all"] = [nc.dram_tensor(f"kvT_all{i}", [GROUP * cfg.RP, TOK], BF16).ap() for i in range(cfg.NPC)]
    dscr("zaT", [DA, TOK], BF16)
    dscr("valT", [DA, TOK], F32)
    dscr("sgT", [DA, TOK], F32)
    dscr("zcT", [DA, TOK], BF16)
    dscr("gaT", [D, TOK], BF16)
    dscr("gcT", [D, TOK], BF16)
    dscr("agT", [DA, TOK], BF16)
    dscr("cT", [DA, TOK], F32)
    dscr("cgT", [DA, TOK], BF16)
    dscr("mT", [D, TOK], BF16)
    dscr("yT", [D, TOK], F32)
    dscr("ybT", [D, TOK], BF16)
    dscr("zT", [D, TOK], F32)
    dscr("x1T", [D, TOK], F32)
    dscr("tail", [DA, 32], F32, internal=True)
    dscr("tail_all", [GROUP * DA, 32], F32, internal=True)

    def on(p):
        return phases is None or p in phases

    with contextlib.ExitStack() as top:
        esems = {k: top.enter_context(nc.semaphore(f"s_{k}")) for k in Sched.CE}
        dsems = [top.enter_context(nc.semaphore(f"s_d{i}")) for i in range(NDS)]
        ccsem = top.enter_context(nc.semaphore("s_cc"))
        S = Sched(nc, esems, dsems, ccsem)
        for L in range(depth):
            x_src = dt["xT"] if L == 0 else dt["x1T"]
            dst = dt["outT"] if L == depth - 1 else dt["x1T"]
            if on("p1"):
                phase_p1(S, nc, cfg, dt, L, x_src)
            r_tail = None
            if on("xch"):
                phase_tail(S, nc, cfg, dt)
                r_tail = phase_exchange(S, nc, cfg, dt)
            if on("conv"):
                phase_conv(S, nc, cfg, dt, L, r_tail)
            elif on("xch"):
                S.barrier()
            if on("attn"):
                phase_attn(S, nc, cfg, dt)
            if on("p6"):
                phase_p6(S, nc, cfg, dt, L)
            if on("p7"):
                phase_p7(S, nc, cfg, dt, L, x_src)
            if on("p8"):
                phase_p8(S, nc, cfg, dt, L, dst)
        S.barrier()
        print("ops:", len(S.ops), "instructions:", S.n_inst)
    return nc, cfg


def host_constants(cfg, rank):
    NB, NQ = cfg.NB, cfg.NQ
    bf = ml_dtypes.bfloat16
    ident = np.eye(128, dtype=np.float32)
    ones = np.ones((128, 128), np.float32)
    sel = np.zeros((128, NB, 128), np.float32)
    for n in range(NB):
        sel[n, n, :] = BIG
    tri = np.zeros((128, 2, 256), np.float32)
    for c in range(2):
        k = c * 128 + np.arange(128)[:, None]
        q = np.arange(256)[None, :]
        tri[:, c, :] = np.where(k <= q, 0.0, -BIG)
    cst = np.concatenate([ident, ones, sel.reshape(128, -1), tri.reshape(128, -1)], axis=1).astype(bf)
    pm = np.zeros((128, NQ, NB), np.float32)
    for j in range(NQ):
        own = rank * cfg.NBL + j // 2
        pm[:, j, own:] = -1e30
    hw = np.zeros((128, GROUP), np.float32)
    if rank > 0:
        hw[:, rank - 1] = 1.0
    return cst, pm.reshape(128, -1), hw


def pack_vecs(inp, depth):
    out = []
    for l in range(depth):
        cols = [inp["b_in"][l].reshape(-1, 128).T]
        cols.append(inp["w_conv"][l].T.reshape(16, 128, CW).transpose(1, 0, 2).reshape(128, 16 * CW))
        cols.append(inp["b_conv"][l].reshape(16, 128).T)
        cols.append(inp["conv_ln_g"][l].reshape(16, 128).T)
        cols.append(inp["conv_ln_b"][l].reshape(16, 128).T)
        cols.append(inp["b_ple_gate"][l].reshape(32, 128).T)
        cols.append(inp["ln_g"][l].reshape(32, 128).T)
        cols.append(inp["ln_b"][l].reshape(32, 128).T)
        out.append(np.concatenate(cols, axis=1))
    return np.ascontiguousarray(np.stack(out).astype(np.float32))


def make_in_maps(inp, cfg, depth, seq_len=None):
    TOK = cfg.TOK
    vecs = pack_vecs(inp, depth)
    maps = []
    for c in range(NCORES):
        b, r = c // GROUP, c % GROUP
        sl = slice(r * TOK, (r + 1) * TOK)
        cst, pm, hs = host_constants(cfg, r)
        m = {
            "xT": np.ascontiguousarray(inp["x"][b, sl, :].T),
            "pT": np.ascontiguousarray(inp["p"][:depth, b, sl, :].transpose(0, 2, 1)),
            "w_in": inp["w_in"][:depth], "w_o_attn": inp["w_o_attn"][:depth],
            "w_o_conv": inp["w_o_conv"][:depth], "w_out": inp["w_out"][:depth],
            "w_ple_up": inp["w_ple_up"][:depth], "w_ple_gate": inp["w_ple_gate"][:depth],
            "vecs": vecs, "cst_bf": cst, "pm": pm, "halo_w": hs,
        }
        maps.append(m)
    return maps


_CACHE = {}


def kernel(**inputs):
    inp = {k: np.asarray(v) for k, v in inputs.items()}
    B, Tfull, _ = inp["x"].shape
    tok = Tfull // GROUP
    if tok not in _CACHE:
        _CACHE[tok] = build_program(tok)
    nc, cfg = _CACHE[tok]
    maps = make_in_maps(inp, cfg, DEPTH)
    res = run_bass_kernel_spmd(nc, maps, core_ids=list(range(NCORES)))
    out = np.empty((B, Tfull, D), np.float32)
    for c in range(NCORES):
        b, r = c // GROUP, c % GROUP
        out[b, r * tok:(r + 1) * tok, :] = res.results[c]["outT"].T
    return out
```
